# Optimizing a Trainium2 kernel written in Bass

```python
import jax
import jax.numpy as jnp
from jax import lax
import numpy as np

D_MODEL = 2048
BATCH = 8
SEQ = 2048
DEPTH = 2

GRID_W = 64
CTX_LEN = 256
NORM_EPS = 1e-6
NEG_INF = -1e30
Q_BLOCK = 128

MLA_HEADS = D_MODEL // 256
MLA_Q_LORA = D_MODEL // 4
MLA_KV_LORA = D_MODEL // 8
MLA_NOPE = 128
MLA_ROPE = 64
MLA_V = 128
MLA_SCALE = (MLA_NOPE + MLA_ROPE) ** -0.5
ROPE_THETA = 10000.0
ROPE_AX_PAIRS = MLA_ROPE // 4

LRU_WIDTH = D_MODEL // 2
LRU_BLOCKS = 8
LRU_BLOCK_W = LRU_WIDTH // LRU_BLOCKS
LRU_CONV = 4
LRU_PAD = (2, 1)
LRU_C = 8.0

MIX_WIDTH = MLA_HEADS * MLA_V + LRU_WIDTH
IN_SPLITS = (MLA_Q_LORA, MLA_Q_LORA + MLA_KV_LORA + MLA_ROPE, MLA_Q_LORA + MLA_KV_LORA + MLA_ROPE + LRU_WIDTH)
IN_COLS = MLA_Q_LORA + MLA_KV_LORA + MLA_ROPE + 2 * LRU_WIDTH

NA_HEADS = 16
NA_HEAD_DIM = D_MODEL // NA_HEADS
NA_WIDTH = NA_HEADS * NA_HEAD_DIM
NA_SCALE = NA_HEAD_DIM ** -0.5
NA_ROWS = 8
NA_COLS = 16

FFN_HIDDEN = (D_MODEL * 11) // 4
FFN_CONV = 3
FFN_PAD = (1, 1)

kernel_name = 'hybrid_mla_rglru_natten_dit'


def rms_norm(x, g):
    xf = x.astype(jnp.float32)
    y = xf * lax.rsqrt(jnp.mean(xf * xf, axis=-1, keepdims=True) + NORM_EPS)
    return (y * g.astype(jnp.float32)).astype(x.dtype)


def modulate(h, shift, scale):
    return h * (1.0 + scale) + shift


def rope_tables(row, col, dtype):
    inv = ROPE_THETA ** (-jnp.arange(ROPE_AX_PAIRS, dtype=jnp.float32) / ROPE_AX_PAIRS)
    ar = row.astype(jnp.float32)[:, None] * inv
    ac = col.astype(jnp.float32)[:, None] * inv
    ang = jnp.concatenate([ar, ar, ac, ac], axis=-1)
    return jnp.cos(ang).astype(dtype), jnp.sin(ang).astype(dtype)


def apply_rope(x, cos, sin):
    x4 = x.reshape(x.shape[:-1] + (2, 2, ROPE_AX_PAIRS))
    rot = jnp.stack([-x4[..., 1, :], x4[..., 0, :]], axis=-2).reshape(x.shape)
    return x * cos[:, None, :] + rot * sin[:, None, :]


def softmax_attention(q, k, v, scale):
    s = jnp.einsum('bqhd,bkhd->bhqk', q, k).astype(jnp.float32) * scale
    p = jax.nn.softmax(s, axis=-1).astype(v.dtype)
    return jnp.einsum('bhqk,bkhd->bqhd', p, v)


def blockwise_attention(q, k, v, scale):
    B, T, H, dk = q.shape
    qb = jnp.moveaxis(q.reshape(B, T // Q_BLOCK, Q_BLOCK, H, dk), 1, 0)
    o = lax.map(lambda qi: softmax_attention(qi, k, v, scale), qb)
    return jnp.moveaxis(o, 0, 1).reshape(B, T, H, v.shape[-1])


def dwconv(x, w, b, pad):
    T = x.shape[1]
    xp = jnp.pad(x, ((0, 0), pad, (0, 0)))
    out = b
    for tap in range(w.shape[0]):
        out = out + xp[:, tap:tap + T] * w[tap]
    return out


def mla_queries(cq, q_norm, w_uq, cos, sin):
    B, T, _ = cq.shape
    q = (rms_norm(cq, q_norm) @ w_uq).reshape(B, T, MLA_HEADS, MLA_NOPE + MLA_ROPE)
    if cos is not None:
        q = jnp.concatenate([q[..., :MLA_NOPE], apply_rope(q[..., MLA_NOPE:], cos, sin)], axis=-1)
    return q


def mla_keys_values(ckv, kv_norm, w_ukv, cos, sin):
    B, T, _ = ckv.shape
    c_lat = rms_norm(ckv[..., :MLA_KV_LORA], kv_norm)
    k_r = ckv[..., MLA_KV_LORA:][:, :, None, :]
    kv = (c_lat @ w_ukv).reshape(B, T, MLA_HEADS, MLA_NOPE + MLA_V)
    if cos is not None:
        k_r = apply_rope(k_r, cos, sin)
    k = jnp.concatenate([kv[..., :MLA_NOPE], jnp.broadcast_to(k_r, (B, T, MLA_HEADS, MLA_ROPE))], axis=-1)
    return k, kv[..., MLA_NOPE:]


def rglru_coeffs(xc, ga_w, ga_b, gx_w, gx_b, lam):
    B, T, W = xc.shape
    xf = xc.astype(jnp.float32)
    xb = xf.reshape(B, T, LRU_BLOCKS, LRU_BLOCK_W)
    r = jax.nn.sigmoid(jnp.einsum('btnd,nde->btne', xb, ga_w.astype(jnp.float32)).reshape(B, T, W) + ga_b.astype(jnp.float32))
    i = jax.nn.sigmoid(jnp.einsum('btnd,nde->btne', xb, gx_w.astype(jnp.float32)).reshape(B, T, W) + gx_b.astype(jnp.float32))
    log_a = -LRU_C * r * jax.nn.softplus(-lam.astype(jnp.float32))
    a = jnp.exp(log_a)
    b = jnp.sqrt(-jnp.expm1(2.0 * log_a)) * (i * xf)
    return a, b


def linear_scan(a, b, h0):
    def combine(left, right):
        a1, b1 = left
        a2, b2 = right
        return a1 * a2, a2 * b1 + b2
    a_cum, b_cum = lax.associative_scan(combine, (a, b), axis=1)
    return a_cum * h0[:, None, :] + b_cum


def rglru_bidir(xl, xc, ga_w, ga_b, gx_w, gx_b, lam, with_ctx):
    B, _, W = xl.shape
    zeros = jnp.zeros((B, W), jnp.float32)
    a, b = rglru_coeffs(xc, ga_w[0], ga_b[0], gx_w[0], gx_b[0], lam[0])
    hc_f = linear_scan(a, b, zeros)
    a, b = rglru_coeffs(xl, ga_w[0], ga_b[0], gx_w[0], gx_b[0], lam[0])
    hl_f = linear_scan(a, b, hc_f[:, -1])
    a, b = rglru_coeffs(jnp.flip(xc, 1), ga_w[1], ga_b[1], gx_w[1], gx_b[1], lam[1])
    hc_b = linear_scan(a, b, zeros)
    a, b = rglru_coeffs(jnp.flip(xl, 1), ga_w[1], ga_b[1], gx_w[1], gx_b[1], lam[1])
    hl_b = linear_scan(a, b, hc_b[:, -1])
    h_lat = (hl_f + jnp.flip(hl_b, 1)).astype(xl.dtype)
    h_ctx = (hc_f + jnp.flip(hc_b, 1)).astype(xc.dtype) if with_ctx else None
    return h_lat, h_ctx


def mla_rglru_mixer(h, hc, w_in, q_norm, w_uq, kv_norm, w_ukv, conv_w, conv_b, ga_w, ga_b, gx_w, gx_b, lam, w_out, cos, sin, with_ctx):
    B, T, _ = h.shape
    Tc = hc.shape[1]
    q_l, kv_l, rx_l, ry_l = jnp.split(h @ w_in, IN_SPLITS, axis=-1)
    q_c, kv_c, rx_c, ry_c = jnp.split(hc @ w_in, IN_SPLITS, axis=-1)
    q = mla_queries(q_l, q_norm, w_uq, cos, sin)
    k_l, v_l = mla_keys_values(kv_l, kv_norm, w_ukv, cos, sin)
    k_c, v_c = mla_keys_values(kv_c, kv_norm, w_ukv, None, None)
    att_l = blockwise_attention(q, jnp.concatenate([k_c, k_l], 1), jnp.concatenate([v_c, v_l], 1), MLA_SCALE)
    x_l = dwconv(rx_l, conv_w, conv_b, LRU_PAD)
    x_c = dwconv(rx_c, conv_w, conv_b, LRU_PAD)
    rec_l, rec_c = rglru_bidir(x_l, x_c, ga_w, ga_b, gx_w, gx_b, lam, with_ctx)
    out_l = jnp.concatenate([att_l.reshape(B, T, -1), rec_l * jax.nn.gelu(ry_l)], axis=-1) @ w_out
    if not with_ctx:
        return out_l, None
    att_c = softmax_attention(mla_queries(q_c, q_norm, w_uq, None, None), k_c, v_c, MLA_SCALE)
    out_c = jnp.concatenate([att_c.reshape(B, Tc, -1), rec_c * jax.nn.gelu(ry_c)], axis=-1) @ w_out
    return out_l, out_c


def na_mixer(h, hc, w_qkv, rel_bias, w_out, with_ctx):
    B, T, _ = h.shape
    Tc = hc.shape[1]
    rows = T // GRID_W
    kh = min(NA_ROWS, rows)
    n_band = kh * GRID_W
    qkv = (h @ w_qkv).reshape(B, T, 3, NA_HEADS, NA_HEAD_DIM)
    q, k, v = qkv[:, :, 0], qkv[:, :, 1], qkv[:, :, 2]
    kv_c = (hc @ w_qkv[:, NA_WIDTH:]).reshape(B, Tc, 2, NA_HEADS, NA_HEAD_DIM)
    k_c, v_c = kv_c[:, :, 0], kv_c[:, :, 1]
    kg = k.reshape(B, rows, GRID_W, NA_HEADS, NA_HEAD_DIM)
    vg = v.reshape(B, rows, GRID_W, NA_HEADS, NA_HEAD_DIM)
    qg = jnp.moveaxis(q.reshape(B, rows, GRID_W, NA_HEADS, NA_HEAD_DIM), 1, 0)
    col = jnp.arange(GRID_W)
    cs = jnp.clip(col - NA_COLS // 2, 0, GRID_W - NA_COLS)
    col_ok = (col[None, :] >= cs[:, None]) & (col[None, :] < cs[:, None] + NA_COLS)
    band_mask = jnp.broadcast_to(col_ok[:, None, :], (GRID_W, kh, GRID_W)).reshape(GRID_W, n_band)
    dc_idx = jnp.clip(col[None, :] - col[:, None] + NA_COLS - 1, 0, 2 * NA_COLS - 2)
    bias_tab = rel_bias.astype(jnp.float32)

    def row_block(args):
        r, q_row = args
        rs = jnp.clip(r - kh // 2, 0, rows - kh)
        k_band = lax.dynamic_slice_in_dim(kg, rs, kh, axis=1).reshape(B, n_band, NA_HEADS, NA_HEAD_DIM)
        v_band = lax.dynamic_slice_in_dim(vg, rs, kh, axis=1).reshape(B, n_band, NA_HEADS, NA_HEAD_DIM)
        dr_idx = rs + jnp.arange(kh) - r + NA_ROWS - 1
        bias = bias_tab[:, dr_idx[:, None, None], dc_idx[None, :, :]]
        bias = jnp.transpose(bias, (0, 2, 1, 3)).reshape(NA_HEADS, GRID_W, n_band)
        s_band = jnp.einsum('bqhd,bkhd->bhqk', q_row, k_band).astype(jnp.float32) * NA_SCALE + bias
        s_band = jnp.where(band_mask, s_band, NEG_INF)
        s_ctx = jnp.einsum('bqhd,bkhd->bhqk', q_row, k_c).astype(jnp.float32) * NA_SCALE
        p = jax.nn.softmax(jnp.concatenate([s_band, s_ctx], axis=-1), axis=-1).astype(v.dtype)
        return (jnp.einsum('bhqk,bkhd->bqhd', p[..., :n_band], v_band)
                + jnp.einsum('bhqk,bkhd->bqhd', p[..., n_band:], v_c))

    o = lax.map(row_block, (jnp.arange(rows), qg))
    o = jnp.moveaxis(o, 0, 1).reshape(B, T, NA_WIDTH)
    out_l = o @ w_out
    if not with_ctx:
        return out_l, None
    q_c = (hc @ w_qkv[:, :NA_WIDTH]).reshape(B, Tc, NA_HEADS, NA_HEAD_DIM)
    o_c = softmax_attention(q_c, k_c, v_c, NA_SCALE).reshape(B, Tc, NA_WIDTH)
    return out_l, o_c @ w_out


def conv_ffn(h, w_up, conv_w, conv_b, w_down):
    g, u = jnp.split(h @ w_up, 2, axis=-1)
    g = dwconv(g, conv_w, conv_b, FFN_PAD)
    return (jax.nn.gelu(g) * u) @ w_down


def setup_inputs(seed: int = 0) -> dict:
    key = jax.random.key(seed)
    ks = iter(jax.random.split(key, 40))
    ne = (DEPTH + 1) // 2
    no = DEPTH // 2

    def nrm(shape, scale):
        return jax.random.normal(next(ks), shape, jnp.float32) * scale

    def gain(shape):
        return 1.0 + nrm(shape, 0.02)

    u = jax.random.uniform(next(ks), (ne, 2, LRU_WIDTH), jnp.float32, minval=0.9, maxval=0.999)
    a0 = u ** (1.0 / LRU_C)
    lam = jnp.log(a0) - jnp.log1p(-a0)
    return {
        'x': nrm((BATCH, SEQ, D_MODEL), 1.0),
        'c': nrm((BATCH, D_MODEL), 1.0),
        'ctx': nrm((BATCH, CTX_LEN, D_MODEL), 1.0),
        'c_ctx': nrm((D_MODEL,), 1.0),
        'mod_w': nrm((DEPTH, D_MODEL, 6 * D_MODEL), 0.5 * D_MODEL ** -0.5),
        'mod_b': nrm((DEPTH, 6 * D_MODEL), 0.02),
        'norm_mix': gain((DEPTH, D_MODEL)),
        'norm_ffn': gain((DEPTH, D_MODEL)),
        'mla_w_in': nrm((ne, D_MODEL, IN_COLS), D_MODEL ** -0.5),
        'mla_q_norm': gain((ne, MLA_Q_LORA)),
        'mla_w_uq': nrm((ne, MLA_Q_LORA, MLA_HEADS * (MLA_NOPE + MLA_ROPE)), MLA_Q_LORA ** -0.5),
        'mla_kv_norm': gain((ne, MLA_KV_LORA)),
        'mla_w_ukv': nrm((ne, MLA_KV_LORA, MLA_HEADS * (MLA_NOPE + MLA_V)), MLA_KV_LORA ** -0.5),
        'lru_conv_w': nrm((ne, LRU_CONV, LRU_WIDTH), LRU_CONV ** -0.5),
        'lru_conv_b': nrm((ne, LRU_WIDTH), 0.02),
        'lru_gate_a_w': nrm((ne, 2, LRU_BLOCKS, LRU_BLOCK_W, LRU_BLOCK_W), LRU_BLOCK_W ** -0.5),
        'lru_gate_a_b': nrm((ne, 2, LRU_WIDTH), 0.02),
        'lru_gate_x_w': nrm((ne, 2, LRU_BLOCKS, LRU_BLOCK_W, LRU_BLOCK_W), LRU_BLOCK_W ** -0.5),
        'lru_gate_x_b': nrm((ne, 2, LRU_WIDTH), 0.02),
        'lru_lambda': lam,
        'mix_w_out': nrm((ne, MIX_WIDTH, D_MODEL), MIX_WIDTH ** -0.5),
        'na_w_qkv': nrm((no, D_MODEL, 3 * NA_WIDTH), D_MODEL ** -0.5),
        'na_rel_bias': nrm((no, NA_HEADS, 2 * NA_ROWS - 1, 2 * NA_COLS - 1), 0.1),
        'na_w_out': nrm((no, NA_WIDTH, D_MODEL), NA_WIDTH ** -0.5),
        'ffn_w_up': nrm((DEPTH, D_MODEL, 2 * FFN_HIDDEN), D_MODEL ** -0.5),
        'ffn_conv_w': nrm((DEPTH, FFN_CONV, FFN_HIDDEN), FFN_CONV ** -0.5),
        'ffn_conv_b': nrm((DEPTH, FFN_HIDDEN), 0.02),
        'ffn_w_down': nrm((DEPTH, FFN_HIDDEN, D_MODEL), FFN_HIDDEN ** -0.5),
        'final_norm': gain((D_MODEL,)),
    }


def reference(x, c, ctx, c_ctx, mod_w, mod_b, norm_mix, norm_ffn,
              mla_w_in, mla_q_norm, mla_w_uq, mla_kv_norm, mla_w_ukv,
              lru_conv_w, lru_conv_b, lru_gate_a_w, lru_gate_a_b, lru_gate_x_w, lru_gate_x_b, lru_lambda,
              mix_w_out, na_w_qkv, na_rel_bias, na_w_out,
              ffn_w_up, ffn_conv_w, ffn_conv_b, ffn_w_down, final_norm):
    T = x.shape[1]
    pos = jnp.arange(T)
    cos, sin = rope_tables(pos // GRID_W, pos % GRID_W, x.dtype)
    s_lat = jax.nn.silu(c)
    s_ctx = jax.nn.silu(c_ctx)
    h_ctx = ctx
    for i in range(DEPTH):
        with_ctx = i < DEPTH - 1
        sh1, sc1, g1, sh2, sc2, g2 = jnp.split((s_lat @ mod_w[i] + mod_b[i])[:, None, :], 6, axis=-1)
        csh1, csc1, cg1, csh2, csc2, cg2 = jnp.split(s_ctx @ mod_w[i] + mod_b[i], 6, axis=-1)
        hx = modulate(rms_norm(x, norm_mix[i]), sh1, sc1)
        hc = modulate(rms_norm(h_ctx, norm_mix[i]), csh1, csc1)
        j = i // 2
        if i % 2 == 0:
            out_x, out_c = mla_rglru_mixer(hx, hc, mla_w_in[j], mla_q_norm[j], mla_w_uq[j], mla_kv_norm[j], mla_w_ukv[j],
                                           lru_conv_w[j], lru_conv_b[j], lru_gate_a_w[j], lru_gate_a_b[j],
                                           lru_gate_x_w[j], lru_gate_x_b[j], lru_lambda[j], mix_w_out[j],
                                           cos, sin, with_ctx)
        else:
            out_x, out_c = na_mixer(hx, hc, na_w_qkv[j], na_rel_bias[j], na_w_out[j], with_ctx)
        x = x + g1 * out_x
        x = x + g2 * conv_ffn(modulate(rms_norm(x, norm_ffn[i]), sh2, sc2),
                              ffn_w_up[i], ffn_conv_w[i], ffn_conv_b[i], ffn_w_down[i])
        if with_ctx:
            h_ctx = h_ctx + cg1 * out_c
            h_ctx = h_ctx + cg2 * conv_ffn(modulate(rms_norm(h_ctx, norm_ffn[i]), csh2, csc2),
                                           ffn_w_up[i], ffn_conv_w[i], ffn_conv_b[i], ffn_w_down[i])
    return rms_norm(x, final_norm)
```

```python
import numpy as np
from contextlib import ExitStack
import concourse.bass as bass
import concourse.mybir as mybir
from concourse.bass_utils import run_bass_kernel_spmd

F32 = mybir.dt.float32
BF16 = mybir.dt.bfloat16
AF = mybir.ActivationFunctionType
ALU = mybir.AluOpType
AX = mybir.AxisListType

ENGS = ['sp', 'pe', 'act', 'dve', 'pool']


class V:
    __slots__ = ('ap', 'keys')

    def __init__(self, ap, keys):
        self.ap = ap
        self.keys = tuple(keys)


class Tile:
    def __init__(self, tensor, name, keys=None):
        self.t = tensor
        self.name = name
        self.keys = (name,) if keys is None else tuple(keys)

    def __getitem__(self, idx):
        return V(self.t[idx], self.keys)

    def sub(self, *subs):
        return Tile(self.t, self.name, [(self.name, s) for s in subs])


class Instr:
    __slots__ = ('eng', 'fn', 'deps', 'is_dma', 'dsem', 'signal', 'val')


class DSem:
    __slots__ = ('h', 'count')


class Prog:
    def __init__(self, nc, n_dma_sems=40):
        self.nc = nc
        self.stack = ExitStack()
        self.csem = {e: self.stack.enter_context(nc.semaphore("c_" + e)) for e in ['pe', 'act', 'dve', 'pool']}
        self.ccount = {e: 0 for e in self.csem}
        self.dpool = [self.stack.enter_context(nc.semaphore("d%d" % i)) for i in range(n_dma_sems)]
        self.dsems = {}
        self.ins = {e: [] for e in ENGS}
        self.lastw = {}
        self.readers = {}
        self.uid = 0
        self.n_instr = 0
        with nc.Block() as b:
            @b.sync
            def _(e):
                for s in list(self.csem.values()) + self.dpool:
                    e.sem_clear(s)

    def close(self):
        self.stack.close()

    def uname(self, base):
        self.uid += 1
        return "%s_%d" % (base, self.uid)

    def sbuf(self, st, name, shape, dtype):
        n = self.uname(name)
        t = st.enter_context(self.nc.sbuf_tensor(n, list(shape), dtype))
        return Tile(t, n)

    def psum(self, st, name, shape, dtype=F32):
        n = self.uname(name)
        t = st.enter_context(self.nc.psum_tensor(n, list(shape), dtype))
        return Tile(t, n)

    def dsem(self, name):
        d = self.dsems.get(name)
        if d is None:
            d = DSem()
            d.h = self.dpool[len(self.dsems)]
            d.count = 0
            self.dsems[name] = d
        return d

    def add(self, eng, fn, reads=(), writes=(), dsem=None):
        I = Instr()
        I.eng = eng
        I.fn = fn
        I.is_dma = dsem is not None
        I.signal = False
        I.val = 0
        I.dsem = None
        if I.is_dma:
            d = self.dsem(dsem)
            d.count += 16
            I.dsem = d
            I.val = d.count
        deps = set()
        for k in reads:
            w = self.lastw.get(k)
            if w is not None:
                deps.add(w)
        for k in writes:
            w = self.lastw.get(k)
            if w is not None:
                deps.add(w)
            for r in self.readers.get(k, ()):
                deps.add(r)
        I.deps = [d for d in deps if d.is_dma or not (eng == 'pe' and d.eng == 'pe' and not I.is_dma)]
        for k in reads:
            self.readers.setdefault(k, []).append(I)
        for k in writes:
            self.lastw[k] = I
            self.readers[k] = []
        self.ins[eng].append(I)
        self.n_instr += 1
        return I

    def _emit_engine(self, e, eng):
        waited = {}
        dma_final = {}
        for I in self.ins[e]:
            toks = {}
            for d in I.deps:
                s = d.dsem.h if d.is_dma else self.csem[d.eng]
                key = id(s)
                if key not in toks or toks[key][1] < d.val:
                    toks[key] = (s, d.val)
            for key, (s, v) in toks.items():
                if waited.get(key, 0) < v:
                    eng.wait_ge(s, v)
                    waited[key] = v
            r = I.fn(eng)
            if I.is_dma:
                r.then_inc(I.dsem.h, 16)
                dma_final[id(I.dsem.h)] = (I.dsem.h, I.val)
            elif I.signal:
                r.then_inc(self.csem[e], 1)
        for key, (s, v) in dma_final.items():
            if waited.get(key, 0) < v:
                eng.wait_ge(s, v)

    def run_phase(self):
        nc = self.nc
        for e in ENGS:
            for I in self.ins[e]:
                for d in I.deps:
                    if not d.is_dma:
                        d.signal = True
        for e in ENGS:
            if e == 'sp':
                continue
            c = self.ccount[e]
            for I in self.ins[e]:
                if I.is_dma:
                    continue
                if I.signal:
                    c += 1
                    I.val = c
            self.ccount[e] = c
        with nc.Block() as b:
            decs = {'sp': b.sync, 'pe': b.tensor, 'act': b.scalar, 'dve': b.vector, 'pool': b.gpsimd}
            for e in ENGS:
                if not self.ins[e]:
                    continue
                decs[e](lambda eng, e=e: self._emit_engine(e, eng))
        self.ins = {e: [] for e in ENGS}
        self.lastw = {}
        self.readers = {}

    @staticmethod
    def _rk(*vs):
        ks = []
        for v in vs:
            if isinstance(v, V):
                ks.extend(v.keys)
        return ks

    @staticmethod
    def _a(v):
        return v.ap if isinstance(v, V) else v

    def matmul(self, out, lhsT, rhs, start=True, stop=True):
        o, l, r = out.ap, lhsT.ap, rhs.ap
        return self.add('pe', lambda e: e.matmul(o, l, r, start=start, stop=stop),
                        reads=self._rk(lhsT, rhs), writes=self._rk(out))

    def transpose(self, out, in_, ident):
        o, i, d = out.ap, in_.ap, ident.ap
        return self.add('pe', lambda e: e.transpose(o, i, d), reads=self._rk(in_, ident), writes=self._rk(out))

    def act(self, out, in_, func, bias=None, scale=None, accum_out=None, eng='act'):
        kw = {}
        if bias is not None:
            kw['bias'] = self._a(bias)
        if scale is not None:
            kw['scale'] = self._a(scale)
        if accum_out is not None:
            kw['accum_out'] = accum_out.ap
        o, i = out.ap, in_.ap
        return self.add(eng, lambda e: e.activation(o, i, func, **kw),
                        reads=self._rk(in_, bias, scale), writes=self._rk(out, accum_out))

    def tt(self, eng, out, in0, in1, op):
        o, a, b = out.ap, in0.ap, in1.ap
        return self.add(eng, lambda e: e.tensor_tensor(o, a, b, op), reads=self._rk(in0, in1), writes=self._rk(out))

    def ts(self, eng, out, in0, s1, s2, op0, op1=None, accum_out=None):
        o, a = out.ap, in0.ap
        x1, x2 = self._a(s1), self._a(s2)
        kw = {}
        if op1 is not None:
            kw['op1'] = op1
        if accum_out is not None:
            kw['accum_out'] = accum_out.ap
        return self.add(eng, lambda e: e.tensor_scalar(o, a, x1, x2, op0, **kw),
                        reads=self._rk(in0, s1, s2), writes=self._rk(out, accum_out))

    def stt(self, out, in0, scalar, in1, op0, op1, eng='dve'):
        o, a, b = out.ap, in0.ap, in1.ap
        s = self._a(scalar)
        return self.add(eng, lambda e: e.scalar_tensor_tensor(o, a, s, b, op0, op1),
                        reads=self._rk(in0, scalar, in1), writes=self._rk(out))

    def copy(self, eng, out, in_):
        o, i = out.ap, in_.ap
        if eng == 'act':
            return self.add(eng, lambda e: e.copy(o, i), reads=self._rk(in_), writes=self._rk(out))
        return self.add(eng, lambda e: e.tensor_copy(o, i), reads=self._rk(in_), writes=self._rk(out))

    def memset(self, eng, out, c):
        o = out.ap
        return self.add(eng, lambda e: e.memset(o, c), writes=self._rk(out))

    def recip(self, out, in_, eng='dve'):
        o, i = out.ap, in_.ap
        return self.add(eng, lambda e: e.reciprocal(o, i), reads=self._rk(in_), writes=self._rk(out))

    def scan(self, out, d0, d1, initial, op0=None, op1=None):
        o, a, b = out.ap, d0.ap, d1.ap
        ini = self._a(initial)
        op0 = op0 or ALU.mult
        op1 = op1 or ALU.add
        return self.add('dve', lambda e: e.tensor_tensor_scan(o, a, b, ini, op0, op1),
                        reads=self._rk(d0, d1, initial), writes=self._rk(out))

    def reduce(self, out, in_, op, axis=None, eng='dve'):
        o, i = out.ap, in_.ap
        axis = axis or AX.X
        return self.add(eng, lambda e: e.tensor_reduce(o, i, axis, op), reads=self._rk(in_), writes=self._rk(out))

    def dma(self, q, out, in_, sem, **kw):
        o, i = out.ap, in_.ap
        return self.add(q, lambda e: e.dma_start(o, i, **kw), reads=self._rk(in_), writes=self._rk(out),
                        dsem=("sw_" if q == 'pool' else "hw_") + sem)


def dram_view(ap, key):
    return V(ap, (key,))


T = 2048
TC = 256
NT = 2304
D = 2048
KC = 16
HID = 5632
NHC = 44
EPS = 1e-6
TB_ALL = [(0, 256), (256, 512), (768, 512), (1280, 512), (1792, 512)]
TB_LAT = TB_ALL[1:]
MLA_SCALE = 192.0 ** -0.5
NA_SCALE = 128.0 ** -0.5
MASKVAL = -30000.0


def bcast_row(ap_row, n):
    return bass.AP(ap_row.tensor, ap_row.offset, [[0, 128], [1, n]])


def na_patterns():
    combos = {}
    pairs = {}
    for r0 in range(0, 32, 2):
        lst = []
        for R in range(0, 32, 2):
            pat = []
            for krl in range(2):
                for qrl in range(2):
                    r = r0 + qrl
                    rs = min(max(r - 4, 0), 24)
                    kr = R + krl
                    pat.append(1 if rs <= kr <= rs + 7 else 0)
            if not any(pat):
                continue
            key = (R - r0, tuple(pat))
            if key not in combos:
                combos[key] = len(combos)
            lst.append((R, combos[key]))
        pairs[r0] = lst
    return combos, pairs


def build(debug=False, stop_after=99):
    nc = bass.Bass("TRN2", target_bir_lowering=False)
    combos, na_pairs = na_patterns()
    NCMB = len(combos) + 1

    def din(name, shape):
        return nc.dram_tensor(name, list(shape), F32, kind="ExternalInput").ap()

    def dscr(name, shape, dt):
        return nc.dram_tensor(name, list(shape), dt, kind="ExternalOutput" if debug else "Internal").ap()

    x_in = din("x", [T, D])
    ctx_in = din("ctx", [TC, D])
    cc_in = din("cc", [128, 32])
    ident_in = din("ident", [128, 128])
    mod_w = din("mod_w", [2, D, 6 * D])
    mod_b = din("mod_b", [2, 6 * D])
    norm_mix = din("norm_mix", [2, D])
    norm_ffn = din("norm_ffn", [2, D])
    final_norm = din("final_norm", [1, D])
    w_in = din("w_in", [D, 3072])
    w_uq = din("w_uq", [512, 2048])
    qn_in = din("qn", [128, 4])
    kvn_in = din("kvn", [128, 2])
    w_ukv_k = din("w_ukv_k", [256, 1024])
    w_ukv_v = din("w_ukv_v", [256, 1024])
    cosT_in = din("cosT", [128, T])
    ssT_in = din("ssT", [128, T])
    lcw_in = din("lcw", [128, 8 * 4])
    lcb_in = din("lcb", [128, 8])
    ga_w = din("ga_w", [2, 8, 128, 128])
    gx_w = din("gx_w", [2, 8, 128, 128])
    gab_in = din("gab", [128, 16])
    gxb_in = din("gxb", [128, 16])
    lam_in = din("lam", [128, 16])
    w_out0 = din("w_out0", [D, D])
    ffn_w_up = din("ffn_w_up", [2, D, 2 * HID])
    fcw_in = din("fcw", [2, 128, NHC * 3])
    fcb_in = din("fcb", [2, 128, NHC])
    ffn_w_down = din("ffn_w_down", [2, HID, D])
    na_w_qkv = din("na_w_qkv", [D, 3 * D])
    na_bias = din("na_bias", [16, 128, NCMB * 128])
    na_w_out = din("na_w_out", [D, D])
    out = nc.dram_tensor("out", [T, D], F32, kind="ExternalOutput").ap()

    modv = dscr("modv", [2, 2, 6 * D], F32)
    xres = dscr("xres", [NT, D], F32)
    cT = dscr("cT", [8, 128, NT], F32)
    rxT = dscr("rxT", [8, 128, NT], F32)
    ryT = dscr("ryT", [8, 128, NT], F32)
    qT = dscr("qT", [16, 128, NT], BF16)
    kT = dscr("kT", [17, 128, NT], BF16)
    vtok = dscr("vtok", [NT, D], BF16)
    mixT = dscr("mixT", [16, 128, NT], BF16)
    mT = dscr("mT", [18, 128, NHC, 128], BF16)

    p = Prog(nc, n_dma_sems=90)
    gst = ExitStack()
    ident = p.sbuf(gst, "ident", [128, 128], BF16)
    ones16 = p.sbuf(gst, "ones16", [128, 128], BF16)
    ones32 = p.sbuf(gst, "ones32", [128, 128], F32)
    epsc = p.sbuf(gst, "epsc", [128, 1], F32)
    p.dma('pool', ident[:, :], V(ident_in, ('c',)), 's_c0')
    p.memset('dve', ones16[:, :], 1.0)
    p.memset('dve', ones32[:, :], 1.0)
    p.memset('dve', epsc[:, :], EPS)

    def dv(ap, *key):
        return V(ap, (key,))

    def xsrc(tt, stage):
        if stage == 0:
            if tt < 2:
                return dv(ctx_in[tt * 128:(tt + 1) * 128, :], 'xin')
            return dv(x_in[(tt - 2) * 128:(tt - 1) * 128, :], 'xin')
        return dv(xres[tt * 128:(tt + 1) * 128, :], 'xres', tt)

    def modrow(layer, stream, j):
        return modv[layer, stream:stream + 1, j * D:(j + 1) * D]

    sT = p.sbuf(gst, "sT", [128, 32], BF16)

    def gen_mod(st, layer, wcols, u0=0, u1=None):
        mw = [p.sbuf(st, "mw", [128, 16, wcols], BF16) for i in range(3)]
        mb = [p.sbuf(st, "mb_", [2, wcols], F32) for i in range(2)]
        mo = [p.sbuf(st, "mo", [2, wcols], F32) for i in range(2)]
        ps = [p.psum(st, "mps", [128, 512]) for i in range(2)]
        wsrc = mod_w[layer].rearrange("(kc p) n -> p kc n", p=128)
        nu = 6 * D // wcols if u1 is None else u1

        def wl(u):
            p.dma('pool', mw[u % 3][:, :, :], dv(wsrc[:, :, u * wcols:(u + 1) * wcols], 'cin'), 's_h%d' % (u % 3))

        wl(u0)
        wl(u0 + 1)
        for u in range(u0, nu):
            if u + 2 < nu:
                wl(u + 2)
            p.dma('sp', mb[u % 2][:, :], dv(bass.AP(mod_b.tensor, layer * 6 * D + u * wcols, [[0, 2], [1, wcols]]), 'cin'), 's_i%d' % (u % 2))
            pp = ps[u % 2]
            for kc in range(16):
                p.matmul(pp[0:2, 0:wcols], sT[:, 2 * kc:2 * kc + 2], mw[u % 3][:, kc, :], start=(kc == 0), stop=(kc == 15))
            p.tt('dve', mo[u % 2][:, :], pp[0:2, 0:wcols], mb[u % 2][:, :], ALU.add)
            p.dma('sp', dv(modv[layer][:, u * wcols:(u + 1) * wcols], 'modv', layer, u), mo[u % 2][:, :], 's_j%d' % (u % 2))
            yield

    def phase_mod():
        with ExitStack() as st:
            cct = p.sbuf(st, "cc", [128, 32], F32)
            p.dma('sp', cct[:, :], dv(cc_in, 'cin'), 's_a0')
            p.act(sT[:, :], cct[:, :], AF.Silu)
            for _ in gen_mod(st, 0, 512, 0, 8):
                pass
            p.run_phase()

    def phase_norm(hT, tiles, stage, layer, which, run=True, pre=None):
        norm_w = norm_mix if which == 0 else norm_ffn
        jsh, jsc = (0, 1) if which == 0 else (3, 4)
        with ExitStack() as st:
            if pre is not None:
                pre()
            NB = 4
            xt = [p.sbuf(st, "xt", [128, D], F32) for i in range(NB)]
            junk = p.sbuf(st, "junk", [128, D], BF16)
            t1 = [p.sbuf(st, "t1", [128, D], F32) for i in range(2)]
            hb = [p.sbuf(st, "hb", [128, D], BF16) for i in range(3)]
            stat = p.sbuf(st, "stat", [128, 4 * NB], F32)
            pst = [p.psum(st, "pst", [128, 8, 128], BF16) for i in range(4)]
            streams = sorted(set(1 if tt < 2 else 0 for tt in tiles))
            gsb = {}
            shb = {}
            gb = xt[0]
            p.dma('sp', gb[:, :], dv(bcast_row(norm_w[layer:layer + 1, :], D), 'cin'), 's_x0')
            for s in streams:
                gsb[s] = p.sbuf(st, "gsb", [128, D], F32)
                shb[s] = p.sbuf(st, "shb", [128, D], F32)
                p.dma('sp', gsb[s][:, :], dv(bcast_row(modrow(layer, s, jsc), D), 'modv', layer), 's_a%d' % s)
                p.dma('sp', shb[s][:, :], dv(bcast_row(modrow(layer, s, jsh), D), 'modv', layer), 's_b%d' % s)
                p.stt(gsb[s][:, :], gsb[s][:, :], 1.0, gb[:, :], ALU.add, ALU.mult)
            N_ = len(tiles)
            H2 = D // 2

            def stage0(n):
                tt = tiles[n]
                x = xt[n % NB]
                p.dma('sp', x[:, :], xsrc(tt, stage), 's_x%d' % (n % NB))
                ss = stat.sub(n % NB)
                c0 = (n % NB) * 4
                p.act(junk[:, :], x[:, :], AF.Square, accum_out=ss[:, c0:c0 + 1])
                p.ts('dve', ss[:, c0 + 1:c0 + 2], ss[:, c0:c0 + 1], 1.0 / D, EPS, ALU.mult, ALU.add)
                p.act(ss[:, c0 + 2:c0 + 3], ss[:, c0 + 1:c0 + 2], AF.Sqrt)
                p.recip(ss[:, c0 + 3:c0 + 4], ss[:, c0 + 2:c0 + 3])

            def stage1(n):
                tt = tiles[n]
                s = 1 if tt < 2 else 0
                x = xt[n % NB]
                ss = stat.sub(n % NB)
                c0 = (n % NB) * 4
                t = t1[n % 2]
                h = hb[n % 3]
                Q = D // 4
                for q in range(4):
                    cs = slice(q * Q, (q + 1) * Q)
                    p.stt(t.sub(q)[:, cs], x[:, cs], ss[:, c0 + 3:c0 + 4], gsb[s][:, cs], ALU.mult, ALU.mult)
                    p.tt('pool' if q < 3 else 'dve', h.sub(q)[:, cs], t.sub(q)[:, cs], shb[s][:, cs], ALU.add)

            def stage2(n):
                tt = tiles[n]
                h = hb[n % 3]
                for half in range(2):
                    pp = pst[(2 * n + half) % 4]
                    for j in range(8):
                        kc = half * 8 + j
                        p.transpose(pp[:, j, :], h.sub(kc // 4)[:, kc * 128:(kc + 1) * 128], ident[:, :])
                    p.copy('act', hT.sub(tt)[:, half * 8:(half + 1) * 8, tt * 128:(tt + 1) * 128], pp[:, :, :])

            for step in range(N_ + 2):
                if step < N_:
                    stage0(step)
                if 0 <= step - 1 < N_:
                    stage1(step - 1)
                if 0 <= step - 2 < N_:
                    stage2(step - 2)
            if run:
                p.run_phase()

    def hview(hT, kc, t0, n):
        keys = [(hT.name, tt) for tt in range(t0 // 128, (t0 + n + 127) // 128)]
        return V(hT.t[:, kc, t0:t0 + n], keys)

    def linear_fm(st, src, kcn, W, ncols, tokblocks, epilogue, skip=None, gw=256, nbuf=3, tag="w", ps=None, wb=None, preloaded=0):
        if wb is None:
            wb = [p.sbuf(st, "wl", [128, kcn, gw], BF16) for i in range(nbuf)]
        if ps is None:
            ps = [p.psum(st, "lps", [128, 512]) for i in range(4)]
        wsrc = W.rearrange("(kc p) n -> p kc n", p=128)
        cnt = 0
        ng = ncols // gw

        def wload(gi):
            p.dma('pool', wb[gi % nbuf][:, :, :], dv(wsrc[:, :, gi * gw:(gi + 1) * gw], 'cin'), 's_%s%d' % (tag, gi % nbuf))

        for gi in range(preloaded, min(nbuf - 1, ng)):
            wload(gi)
        for gi in range(ng):
            w = wb[gi % nbuf]
            if gi + nbuf - 1 < ng:
                wload(gi + nbuf - 1)
            for cj in range(gw // 128):
                chunk = gi * (gw // 128) + cj
                for bi, (t0, n) in enumerate(tokblocks):
                    if skip is not None and skip(chunk, bi):
                        continue
                    pp = ps[cnt % 4]
                    cnt += 1
                    for kc in range(kcn):
                        p.matmul(pp[:, 0:n], w[:, kc, cj * 128:(cj + 1) * 128], src(kc, t0, n),
                                 start=(kc == 0), stop=(kc == kcn - 1))
                    epilogue(chunk, bi, t0, n, pp)

    def phase_l0_proj():
        st_h = ExitStack()
        hT = p.sbuf(st_h, "hT", [128, KC, NT], BF16)

        st_w = ExitStack()
        wbp = [p.sbuf(st_w, "wl", [128, KC, 256], BF16) for i in range(3)]
        wsrc_p = w_in.rearrange("(kc p) n -> p kc n", p=128)

        def _pre():
            for gi in range(2):
                p.dma('pool', wbp[gi][:, :, :], dv(wsrc_p[:, :, gi * 256:(gi + 1) * 256], 'cin'), 's_w%d' % gi)
        phase_norm(hT, list(range(18)), 0, 0, 0, pre=_pre)
        with ExitStack() as st:
            stg = [p.sbuf(st, "stg", [128, 512], F32) for i in range(4)]
            cnt = [0]

            def epi(chunk, bi, t0, n, pp):
                k = cnt[0] % 4
                cnt[0] += 1
                s = stg[k]
                p.copy('act' if k % 2 == 0 else 'dve', s[:, 0:n], pp[:, 0:n])
                if chunk < 8:
                    dst = cT[chunk]
                else:
                    dst = (rxT if chunk < 16 else ryT)[(chunk - 8) % 8]
                p.dma('sp', dv(dst[:, t0:t0 + n], 'rT', chunk, bi), s[:, 0:n], 's_g%d' % k)

            linear_fm(st, lambda kc, t0, n: hview(hT, kc, t0, n), KC, w_in, 3072, TB_ALL, epi, gw=256, wb=wbp, preloaded=2)
            p.run_phase()
        st_w.close()
        st_h.close()
        if stop_after <= 1:
            return
        with ExitStack() as st:
            cqT = p.sbuf(st, "cqT", [128, 4, NT], F32)
            ckvT = p.sbuf(st, "ckvT", [128, 2, NT], F32)
            krT = p.sbuf(st, "krT", [128, NT], F32)
            krsT = p.sbuf(st, "krsT", [128, NT], F32)
            allb = list(range(5))
            for c in range(4):
                p.dma('sp', cqT.sub(*allb)[:, c, :], dv(cT[c], 'cT_in'), 's_q%d' % (c % 2))
            for c in range(2):
                p.dma('sp', ckvT.sub(*allb)[:, c, :], dv(cT[4 + c], 'cT_in'), 's_r%d' % c)
            p.dma('sp', krT.sub(*allb)[:, :], dv(cT[6], 'cT_in'), 's_k0')
            p.dma('sp', krsT.sub(*allb)[:, :], dv(cT[7], 'cT_in'), 's_k1')
            cqn = p.sbuf(st, "cqn", [128, 4, NT], BF16)
            cln = p.sbuf(st, "cln", [128, 2, NT], BF16)
            qn = p.sbuf(st, "qn", [128, 4], F32)
            kvn = p.sbuf(st, "kvn", [128, 2], F32)
            cosT = p.sbuf(st, "cosT", [128, T], F32)
            ssT = p.sbuf(st, "ssT", [128, T], F32)
            p.dma('sp', qn[:, :], dv(qn_in, 'cin'), 's_a0')
            p.dma('sp', kvn[:, :], dv(kvn_in, 'cin'), 's_a1')
            p.dma('sp', cosT[:, :], dv(cosT_in, 'cin'), 's_a2')
            p.dma('sp', ssT[:, :], dv(ssT_in, 'cin'), 's_b0')
            sq = [p.sbuf(st, "sq", [128, 512], F32) for i in range(2)]
            rsb = [p.sbuf(st, "rsb", [128, 512], F32) for i in range(2)]
            psn = [p.psum(st, "psn", [128, 512]) for i in range(2)]
            k = 0
            for (srcT, nch, dstT, gvec) in ((cqT, 4, cqn, qn), (ckvT, 2, cln, kvn)):
                for bi, (t0, n) in enumerate(TB_ALL):
                    pp = psn[k % 2]
                    r = rsb[k % 2]
                    for c in range(nch):
                        s = sq[(k * 4 + c) % 2]
                        p.act(s[:, 0:n], srcT.sub(bi)[:, c, t0:t0 + n], AF.Square)
                        p.matmul(pp[:, 0:n], ones32[:, :], s[:, 0:n], start=(c == 0), stop=(c == nch - 1))
                    p.ts('dve', r[:, 0:n], pp[:, 0:n], 1.0 / (128 * nch), EPS, ALU.mult, ALU.add)
                    p.act(r[:, 0:n], r[:, 0:n], AF.Sqrt)
                    p.recip(r[:, 0:n], r[:, 0:n])
                    for c in range(nch):
                        p.stt(dstT.sub(bi)[:, c, t0:t0 + n], srcT.sub(bi)[:, c, t0:t0 + n], gvec[:, c:c + 1], r[:, 0:n],
                              ALU.mult, ALU.mult)
                    k += 1
            KR = p.sbuf(st, "KR", [128, NT], BF16)
            ta = p.sbuf(st, "ta", [128, T], F32)
            tb = p.sbuf(st, "tb", [128, T], F32)
            allb = list(range(5))
            p.copy('act', KR[:, 0:TC], krT.sub(*allb)[:, 0:TC])
            p.tt('dve', ta[:, :], krT.sub(*allb)[:, TC:NT], cosT[:, :], ALU.mult)
            p.tt('pool', tb[:, :], krsT.sub(*allb)[:, TC:NT], ssT[:, :], ALU.mult)
            p.tt('dve', KR[:, TC:NT], ta[:, :], tb[:, :], ALU.add)
            p.dma('sp', dv(kT[8], 'kT', 8), KR[:, :], 's_b1')
            qstg = [p.sbuf(st, "qstg", [128, 512], BF16) for i in range(4)]
            fstg = [p.sbuf(st, "fstg", [128, 512], F32) for i in range(2)]
            tmpA = p.sbuf(st, "tmpA", [128, T], F32)
            qc = [0]

            def epi_q(chunk, bi, t0, n, pp):
                k = qc[0] % 4
                qc[0] += 1
                s = qstg[k]
                if chunk < 8:
                    p.copy('act', s[:, 0:n], pp[:, 0:n])
                    p.dma('sp', dv(qT[chunk][:, t0:t0 + n], 'qT', chunk, bi), s[:, 0:n], 's_g%d' % k)
                    return
                j = (chunk - 8) // 2
                if (chunk - 8) % 2 == 0:
                    if bi == 0:
                        p.copy('act', s[:, 0:n], pp[:, 0:n])
                        p.dma('sp', dv(qT[8 + j][:, t0:t0 + n], 'qT', 8 + j, bi), s[:, 0:n], 's_g%d' % k)
                    else:
                        p.tt('dve', tmpA.sub(bi)[:, t0 - TC:t0 - TC + n], pp[:, 0:n], cosT[:, t0 - TC:t0 - TC + n], ALU.mult)
                else:
                    f = fstg[k % 2]
                    p.tt('dve', f[:, 0:n], pp[:, 0:n], ssT[:, t0 - TC:t0 - TC + n], ALU.mult)
                    p.tt('pool', s[:, 0:n], f[:, 0:n], tmpA.sub(bi)[:, t0 - TC:t0 - TC + n], ALU.add)
                    p.dma('sp', dv(qT[8 + j][:, t0:t0 + n], 'qT', 8 + j, bi), s[:, 0:n], 's_g%d' % k)

            def skip_q(chunk, bi):
                return chunk >= 8 and (chunk - 8) % 2 == 1 and bi == 0

            def src_q(kc, t0, n):
                return V(cqn.t[:, kc, t0:t0 + n], [(cqn.name, bi) for bi, (a, b) in enumerate(TB_ALL) if a == t0])

            lps = [p.psum(st, "lps", [128, 512]) for i in range(4)]
            linear_fm(st, src_q, 4, w_uq, 2048, TB_ALL, epi_q, skip=skip_q, gw=512, tag="u", ps=lps)

            def epi_k(chunk, bi, t0, n, pp):
                k = qc[0] % 4
                qc[0] += 1
                s = qstg[k]
                p.copy('act' if k % 2 == 0 else 'dve', s[:, 0:n], pp[:, 0:n])
                p.dma('sp', dv(kT[chunk][:, t0:t0 + n], 'kT', chunk, bi), s[:, 0:n], 's_g%d' % k)

            def src_k(kc, t0, n):
                return V(cln.t[:, kc, t0:t0 + n], [(cln.name, bi) for bi, (a, b) in enumerate(TB_ALL) if a == t0])

            linear_fm(st, src_k, 2, w_ukv_k, 1024, TB_ALL, epi_k, gw=512, tag="v", ps=lps)
            wv = p.sbuf(st, "wv", [128, 2, 1024], BF16)
            p.dma('pool', wv[:, :, :], dv(w_ukv_v.rearrange("(kc p) n -> p kc n", p=128), 'cin'), 's_c1')
            vst = [p.sbuf(st, "vst", [128, 1024], BF16) for i in range(2)]
            psv = [p.psum(st, "psv", [128, 512]) for i in range(2)]
            for tt in range(18):
                bi = 0 if tt < 2 else 1 + (tt - 2) // 4
                vs = vst[tt % 2]
                for cb in range(2):
                    pp = psv[cb]
                    for kc in range(2):
                        p.matmul(pp[:, :], cln.sub(bi)[:, kc, tt * 128:(tt + 1) * 128], wv[:, kc, cb * 512:(cb + 1) * 512],
                                 start=(kc == 0), stop=(kc == 1))
                    p.copy('act' if cb == 0 else 'dve', vs[:, cb * 512:(cb + 1) * 512], pp[:, :])
                p.dma('sp', dv(vtok[tt * 128:(tt + 1) * 128, 0:1024], 'vtok', tt), vs[:, :], 's_x%d' % (tt % 2))
            p.run_phase()

    def gen_mla(st):
        if True:
            KR = p.sbuf(st, "KR", [128, NT], BF16)
            p.dma('sp', KR[:, :], dv(kT[8], 'kT8'), 's_a0')
            qh = [p.sbuf(st, "qh", [128, NT], BF16) for i in range(2)]
            qr = [p.sbuf(st, "qr", [128, NT], BF16) for i in range(2)]
            kh = [p.sbuf(st, "kh", [128, NT], BF16) for i in range(2)]
            vh = [p.sbuf(st, "vh", [128, 18, 128], BF16) for i in range(2)]
            oh = [p.sbuf(st, "oh", [128, NT], BF16) for i in range(2)]
            pT = [p.sbuf(st, "pT", [128, 512], BF16) for i in range(4)]
            rc = [p.sbuf(st, "rc", [128, 512], F32) for i in range(2)]
            psS = [p.psum(st, "psS", [128, 512]) for i in range(4)]
            psO = [p.psum(st, "psO", [128, 512]) for i in range(2)]
            psZ = [p.psum(st, "psZ", [128, 512]) for i in range(2)]
            def hload(h):
                b = h % 2
                p.dma('sp', qh[b][:, :], dv(qT[h], 'qTh'), 's_q%d' % b)
                p.dma('sp', qr[b][:, :], dv(qT[8 + h // 2], 'qTr'), 's_r%d' % b)
                p.dma('sp', kh[b][:, :], dv(kT[h], 'kTh'), 's_k%d' % b)
                p.dma('sp', vh[b][:, :, :], dv(vtok[:, h * 128:(h + 1) * 128].rearrange("(t p) c -> p t c", p=128), 'vh'),
                      's_v%d' % b)

            steps = []
            for h in range(8):
                for qi, (q0, qn_) in enumerate(TB_ALL):
                    kts = [0, 1] if qi == 0 else list(range(18))
                    for ki, kt in enumerate(kts):
                        steps.append((h, qi, q0, qn_, ki, kt, len(kts)))

            def qk(si):
                h, qi, q0, qn_, ki, kt, nk = steps[si]
                b = h % 2
                pb = (h % 2) * 64
                pS = psS[si % 4]
                p.matmul(pS[:, 0:qn_], kh[b][:, kt * 128:(kt + 1) * 128], qh[b][:, q0:q0 + qn_], start=True, stop=False)
                p.matmul(pS[:, 0:qn_], KR[pb:pb + 64, kt * 128:(kt + 1) * 128], qr[b][pb:pb + 64, q0:q0 + qn_],
                         start=False, stop=True)

            hload(0)
            nq = 0
            qk(0)
            qk(1)
            for si in range(len(steps)):
                h, qi, q0, qn_, ki, kt, nk = steps[si]
                b = h % 2
                if qi == 0 and ki == 0 and h + 1 < 8:
                    hload(h + 1)
                if si + 2 < len(steps):
                    qk(si + 2)
                pS = psS[si % 4]
                pt = pT[si % 4]
                po = psO[nq % 2]
                pz = psZ[nq % 2]
                p.act(pt[:, 0:qn_], pS[:, 0:qn_], AF.Exp, scale=MLA_SCALE)
                p.matmul(po[:, 0:qn_], vh[b][:, kt, :], pt[:, 0:qn_], start=(ki == 0), stop=(ki == nk - 1))
                p.matmul(pz[:, 0:qn_], ones16[:, :], pt[:, 0:qn_], start=(ki == 0), stop=(ki == nk - 1))
                if ki == nk - 1:
                    r = rc[nq % 2]
                    nq += 1
                    p.recip(r[:, 0:qn_], pz[:, 0:qn_])
                    p.tt('dve', oh[b].sub(qi)[:, q0:q0 + qn_], po[:, 0:qn_], r[:, 0:qn_], ALU.mult)
                    if qi == len(TB_ALL) - 1:
                        p.dma('sp', dv(mixT[h], 'mixT', h), oh[b].sub(0, 1, 2, 3, 4)[:, :], 's_o%d' % b)
                yield

    def gen_lru(st):
        W = NT + 6
        CT0 = 2
        LT0 = 261
        if True:
            gw = p.sbuf(st, "gw", [128, 32, 128], F32)
            for g, wsrc in enumerate((ga_w, gx_w)):
                for d in range(2):
                    p.dma('sp', gw[:, (g * 2 + d) * 8:(g * 2 + d) * 8 + 8, :], dv(wsrc[d].rearrange("n k e -> k n e"), 'cin'),
                          's_e%d' % (g * 2 + d))
            lcw = p.sbuf(st, "lcw", [128, 32], F32)
            lcb = p.sbuf(st, "lcb", [128, 8], F32)
            gab = p.sbuf(st, "gab", [128, 16], F32)
            gxb = p.sbuf(st, "gxb", [128, 16], F32)
            lam = p.sbuf(st, "lam", [128, 16], F32)
            p.dma('sp', lcw[:, :], dv(lcw_in, 'cin'), 's_b0')
            p.dma('sp', lcb[:, :], dv(lcb_in, 'cin'), 's_b1')
            p.dma('sp', gab[:, :], dv(gab_in, 'cin'), 's_b2')
            p.dma('sp', gxb[:, :], dv(gxb_in, 'cin'), 's_b3')
            p.dma('sp', lam[:, :], dv(lam_in, 'cin'), 's_b4')
            z = p.sbuf(st, "z", [128, 16], F32)
            y = p.sbuf(st, "y", [128, 16], F32)
            y2 = p.sbuf(st, "y2", [128, 16], F32)
            acc = p.sbuf(st, "acc", [128, 16], F32)
            c1 = p.sbuf(st, "c1", [128, 16], F32)
            c2 = p.sbuf(st, "c2", [128, 16], F32)
            p.ts('dve', z[:, :], lam[:, :], -1.0, None, ALU.mult)
            p.tt('dve', z[:, :], z[:, :], lam[:, :], ALU.max)
            p.act(z[:, :], z[:, :], AF.Exp, scale=-1.0)
            p.ts('dve', y[:, :], z[:, :], 2.0, None, ALU.add)
            p.recip(y[:, :], y[:, :])
            p.tt('dve', y[:, :], y[:, :], z[:, :], ALU.mult)
            p.tt('dve', y2[:, :], y[:, :], y[:, :], ALU.mult)
            p.ts('dve', acc[:, :], y2[:, :], 1.0 / 13, 1.0 / 11, ALU.mult, ALU.add)
            for cf in (1.0 / 9, 1.0 / 7, 1.0 / 5, 1.0 / 3, 1.0):
                p.tt('dve', acc[:, :], acc[:, :], y2[:, :], ALU.mult)
                p.ts('dve', acc[:, :], acc[:, :], cf, None, ALU.add)
            p.tt('dve', acc[:, :], acc[:, :], y[:, :], ALU.mult)
            p.ts('dve', z[:, :], lam[:, :], -1.0, 0.0, ALU.mult, ALU.max)
            p.stt(acc[:, :], acc[:, :], 2.0, z[:, :], ALU.mult, ALU.add)
            p.ts('dve', c1[:, :], acc[:, :], -8.0, None, ALU.mult)
            p.ts('dve', c2[:, :], acc[:, :], -16.0, None, ALU.mult)

            xp = [p.sbuf(st, "xp", [128, W], F32) for i in range(2)]
            ry = [p.sbuf(st, "ry", [128, NT], F32) for i in range(2)]
            xc = [p.sbuf(st, "xc", [128, W], F32) for i in range(2)]
            rr = [p.sbuf(st, "rr", [128, W], F32) for i in range(2)]
            ii = [p.sbuf(st, "ii", [128, W], F32) for i in range(2)]
            bb = [p.sbuf(st, "bb", [128, W], F32) for i in range(2)]
            hf = [p.sbuf(st, "hf", [128, W], F32) for i in range(2)]
            hbk = [p.sbuf(st, "hbk", [128, W], F32) for i in range(2)]
            lo = [p.sbuf(st, "lo", [128, NT], BF16) for i in range(2)]
            NPG = 6
            psg = [p.psum(st, "psg", [128, 512]) for i in range(NPG)]
            for i in range(2):
                p.memset('pool', xp[i][:, :], 0.0)
            blks = []
            c = CT0
            while c < W - 1:
                n = min(512, W - 1 - c)
                blks.append((c, n))
                c += n
            ng = [0]
            L = W - 3
            S = slice(2, W - 1)

            def S0(n):
                x = xp[n % 2]
                p.dma('sp', x[:, CT0:CT0 + TC], dv(rxT[n][:, 0:TC], 'rxl'), 's_x%d' % (n % 2))
                p.dma('sp', x[:, LT0:LT0 + T], dv(rxT[n][:, TC:NT], 'rxl'), 's_y%d' % (n % 2))
                p.dma('sp', ry[n % 2][:, :], dv(ryT[n], 'ryl'), 's_z%d' % (n % 2))
                xcn = xc[n % 2]
                p.act(xcn[:, 2:2 + L], x[:, 0:L], AF.Identity, bias=lcb[:, n:n + 1], scale=lcw[:, n * 4:n * 4 + 1])
                for tap in range(1, 4):
                    p.stt(xcn[:, 2:2 + L], x[:, tap:tap + L], lcw[:, n * 4 + tap:n * 4 + tap + 1], xcn[:, 2:2 + L], ALU.mult, ALU.add)

            def S1(n, d):
                u = 2 * n + d
                xcn = xc[n % 2]
                r_, i_, b_ = rr[u % 2], ii[u % 2], bb[u % 2]
                for (b0, bn) in blks:
                    pr = psg[ng[0] % NPG]
                    pi = psg[(ng[0] + 1) % NPG]
                    ng[0] += 2
                    p.matmul(pr[:, 0:bn], gw[:, (0 * 2 + d) * 8 + n, :], xcn[:, b0:b0 + bn])
                    p.matmul(pi[:, 0:bn], gw[:, (1 * 2 + d) * 8 + n, :], xcn[:, b0:b0 + bn])
                    p.act(r_[:, b0:b0 + bn], pr[:, 0:bn], AF.Sigmoid, bias=gab[:, d * 8 + n:d * 8 + n + 1])
                    p.act(i_[:, b0:b0 + bn], pi[:, 0:bn], AF.Sigmoid, bias=gxb[:, d * 8 + n:d * 8 + n + 1])
                p.act(b_[:, S], r_[:, S], AF.Exp, scale=c2[:, d * 8 + n:d * 8 + n + 1])
                p.act(r_[:, S], r_[:, S], AF.Exp, scale=c1[:, d * 8 + n:d * 8 + n + 1])
                p.act(b_[:, S], b_[:, S], AF.Sqrt, bias=1.0, scale=-1.0)
                p.tt('pool', i_[:, S], i_[:, S], xcn[:, S], ALU.mult)
                p.tt('dve', b_[:, S], b_[:, S], i_[:, S], ALU.mult)
                if d == 0:
                    h_ = hf[n % 2]
                    p.scan(h_[:, CT0:CT0 + TC], r_[:, CT0:CT0 + TC], b_[:, CT0:CT0 + TC], 0.0)
                    p.scan(h_[:, LT0:LT0 + T], r_[:, LT0:LT0 + T], b_[:, LT0:LT0 + T], h_[:, CT0 + TC - 1:CT0 + TC])
                else:
                    h_ = hbk[n % 2]
                    p.scan(h_[:, CT0 + TC - 1:CT0 - 1:-1], r_[:, CT0 + TC - 1:CT0 - 1:-1], b_[:, CT0 + TC - 1:CT0 - 1:-1], 0.0)
                    p.scan(h_[:, LT0 + T - 1:LT0 - 1:-1], r_[:, LT0 + T - 1:LT0 - 1:-1], b_[:, LT0 + T - 1:LT0 - 1:-1],
                           h_[:, CT0:CT0 + 1])

            def S2(n):
                h_, hb_ = hf[n % 2], hbk[n % 2]
                yv = ry[n % 2]
                p.tt('pool', h_[:, CT0:CT0 + TC], h_[:, CT0:CT0 + TC], hb_[:, CT0:CT0 + TC], ALU.add)
                p.tt('pool', h_[:, LT0:LT0 + T], h_[:, LT0:LT0 + T], hb_[:, LT0:LT0 + T], ALU.add)
                p.act(yv[:, :], yv[:, :], AF.Gelu_apprx_tanh)
                o = lo[n % 2]
                p.tt('dve', o[:, 0:TC], h_[:, CT0:CT0 + TC], yv[:, 0:TC], ALU.mult)
                p.tt('dve', o[:, TC:NT], h_[:, LT0:LT0 + T], yv[:, TC:NT], ALU.mult)
                p.dma('sp', dv(mixT[8 + n], 'mixT', 8 + n), o[:, :], 's_p%d' % (n % 2))

            gmod = gen_mod(st, 0, 256, 16, 48)

            def modstep(k):
                for _ in range(k):
                    try:
                        next(gmod)
                    except StopIteration:
                        pass

            S0(0)
            for n in range(8):
                S1(n, 0)
                modstep(2)
                if n + 1 < 8:
                    S0(n + 1)
                S1(n, 1)
                modstep(2)
                S2(n)
                yield
            modstep(64)

    def phase_mla():
        with ExitStack() as st:
            for _ in gen_mla(st):
                pass
            p.run_phase()

    def phase_lru():
        with ExitStack() as st:
            for _ in gen_lru(st):
                pass
            p.run_phase()

    def phase_mla_lru():
        with ExitStack() as st:
            ga = gen_mla(st)
            gb_ = gen_lru(st)
            alive_a = alive_b = True
            while alive_a or alive_b:
                for _ in range(5):
                    if alive_a:
                        try:
                            next(ga)
                        except StopIteration:
                            alive_a = False
                for _ in range(2):
                    if alive_b:
                        try:
                            next(gb_)
                        except StopIteration:
                            alive_b = False
            p.run_phase()

    def phase_outproj(Wout, layer, tiles, stage):
        with ExitStack() as st:
            wo = p.sbuf(st, "wo", [128, KC, D], BF16)
            wsrc = Wout.rearrange("(kc p) n -> p kc n", p=128)
            for q in range(4):
                p.dma('pool', wo.sub(q)[:, :, q * 512:(q + 1) * 512], dv(wsrc[:, :, q * 512:(q + 1) * 512], 'cin'), 's_w%d' % q)
            streams = sorted(set(1 if tt < 2 else 0 for tt in tiles))
            gb = {}
            for s in streams:
                gb[s] = p.sbuf(st, "gb", [128, D], F32)
                p.dma('sp', gb[s][:, :], dv(bcast_row(modrow(layer, s, 2), D), 'modv', layer), 's_a%d' % s)
            mt = [p.sbuf(st, "mt", [128, KC, 128], BF16) for i in range(3)]
            xt = [p.sbuf(st, "xt", [128, D], F32) for i in range(2)]
            tm = [p.sbuf(st, "tm", [128, 512], F32) for i in range(2)]
            ps = [p.psum(st, "ops", [128, 512]) for i in range(4)]
            k = 0
            for n, tt in enumerate(tiles):
                s = 1 if tt < 2 else 0
                m = mt[n % 3]
                p.dma('sp', m[:, :, :], dv(mixT[:, :, tt * 128:(tt + 1) * 128].rearrange("c p t -> p c t"), 'mixT_in'),
                      's_m%d' % (n % 3))
                x = xt[n % 2]
                p.dma('sp', x[:, :], xsrc(tt, stage), 's_x%d' % (n % 2))
                for cb in range(4):
                    pp = ps[k % 4]
                    t = tm[k % 2]
                    k += 1
                    for kc in range(KC):
                        p.matmul(pp[:, :], m[:, kc, :], wo.sub(cb)[:, kc, cb * 512:(cb + 1) * 512], start=(kc == 0), stop=(kc == KC - 1))
                    p.tt('dve', t[:, :], pp[:, :], gb[s][:, cb * 512:(cb + 1) * 512], ALU.mult)
                    p.tt('pool', x[:, cb * 512:(cb + 1) * 512], t[:, :], x[:, cb * 512:(cb + 1) * 512], ALU.add)
                p.dma('act', dv(xres[tt * 128:(tt + 1) * 128, :], 'xres', tt), x[:, :], 's_y%d' % (n % 2))
            p.run_phase()

    def phase_ffn(layer, tiles):
        lat_only = tiles[0] >= 2
        tokblocks = TB_LAT if lat_only else TB_ALL
        st_h = ExitStack()
        hT = p.sbuf(st_h, "hT", [128, KC, NT], BF16)
        st_w = ExitStack()
        wg = [p.sbuf(st_w, "wg", [128, KC, 256], BF16) for i in range(2)]
        wu = [p.sbuf(st_w, "wu", [128, KC, 256], BF16) for i in range(2)]
        wsrc = ffn_w_up[layer].rearrange("(kc p) n -> p kc n", p=128)

        def wload(g_):
            p.dma('pool', wg[g_ % 2][:, :, :], dv(wsrc[:, :, g_ * 256:(g_ + 1) * 256], 'cin'), 's_w%d' % (g_ % 2))
            p.dma('pool', wu[g_ % 2][:, :, :], dv(wsrc[:, :, HID + g_ * 256:HID + (g_ + 1) * 256], 'cin'), 's_u%d' % (g_ % 2))

        phase_norm(hT, tiles, 1, layer, 1, pre=lambda: wload(0))
        W = NT + 4
        CT0 = 1
        LT0 = 259

        def col(t0):
            return CT0 + t0 if t0 < TC else LT0 + (t0 - TC)

        with ExitStack() as st:
            fcw = p.sbuf(st, "fcw", [128, NHC * 3], F32)
            fcb = p.sbuf(st, "fcb", [128, NHC], F32)
            p.dma('sp', fcw[:, :], dv(fcw_in[layer], 'cin'), 's_c0')
            p.dma('sp', fcb[:, :], dv(fcb_in[layer], 'cin'), 's_c1')
            gbuf = [p.sbuf(st, "gbuf", [128, W], F32) for i in range(2)]
            ubuf = [p.sbuf(st, "ubuf", [128, W], F32) for i in range(2)]
            accb = [p.sbuf(st, "accb", [128, W], F32) for i in range(2)]
            mb = [p.sbuf(st, "mb", [128, W], BF16) for i in range(2)]
            ps = [p.psum(st, "fps", [128, 512]) for i in range(6)]
            NPS = 6
            for i in range(2):
                p.memset('pool', gbuf[i].sub(*range(len(tokblocks)))[:, :], 0.0)
                p.memset('pool', ubuf[i].sub(*range(len(tokblocks)))[:, :], 0.0)
            gmod = gen_mod(st, 1, 256) if layer == 0 else None
            k = 0
            c_lo = LT0 if lat_only else CT0
            Lc = (W - 1) - c_lo
            for c in range(NHC):
                gi = c // 2

                if c % 2 == 0 and gi + 1 < NHC // 2:
                    wload(gi + 1)
                cj = c % 2
                g = gbuf[c % 2]
                u = ubuf[c % 2]
                a = accb[c % 2]
                m = mb[c % 2]
                for bi, (t0, n) in enumerate(tokblocks):
                    pg = ps[k % 6]
                    pu = ps[(k + 1) % 6]
                    k += 2
                    for kc in range(KC):
                        p.matmul(pg[:, 0:n], wg[gi % 2][:, kc, cj * 128:(cj + 1) * 128], hview(hT, kc, t0, n), start=(kc == 0), stop=(kc == KC - 1))
                    p.copy('act', g.sub(bi)[:, col(t0):col(t0) + n], pg[:, 0:n])
                    for kc in range(KC):
                        p.matmul(pu[:, 0:n], wu[gi % 2][:, kc, cj * 128:(cj + 1) * 128], hview(hT, kc, t0, n), start=(kc == 0), stop=(kc == KC - 1))
                    p.copy('dve', u.sub(bi)[:, col(t0):col(t0) + n], pu[:, 0:n])
                allb = list(range(len(tokblocks)))
                gA = g.sub(*allb)
                uA = u.sub(*allb)
                p.act(a[:, c_lo:c_lo + Lc], gA[:, c_lo - 1:c_lo - 1 + Lc], AF.Identity, bias=fcb[:, c:c + 1], scale=fcw[:, 3 * c:3 * c + 1])
                p.stt(a[:, c_lo:c_lo + Lc], gA[:, c_lo:c_lo + Lc], fcw[:, 3 * c + 1:3 * c + 2], a[:, c_lo:c_lo + Lc], ALU.mult, ALU.add)
                p.stt(a[:, c_lo:c_lo + Lc], gA[:, c_lo + 1:c_lo + 1 + Lc], fcw[:, 3 * c + 2:3 * c + 3], a[:, c_lo:c_lo + Lc], ALU.mult, ALU.add)
                p.act(a[:, c_lo:c_lo + Lc], a[:, c_lo:c_lo + Lc], AF.Gelu_apprx_tanh)
                p.tt('dve', m[:, c_lo:c_lo + Lc], a[:, c_lo:c_lo + Lc], uA[:, c_lo:c_lo + Lc], ALU.mult)
                if not lat_only:
                    p.dma('sp', dv(mT[0:2, :, c, :].rearrange("a p t -> p a t"), 'mT', c, 0),
                          V(m.t[:, CT0:CT0 + TC].rearrange("p (a t) -> p a t", t=128), m.keys), 's_m%d' % (c % 2))
                p.dma('sp', dv(mT[2:18, :, c, :].rearrange("a p t -> p a t"), 'mT', c, 1),
                      V(m.t[:, LT0:LT0 + T].rearrange("p (a t) -> p a t", t=128), m.keys), 's_n%d' % (c % 2))
                if gmod is not None:
                    for _ in range(2 if c < 4 else 1):
                        try:
                            next(gmod)
                        except StopIteration:
                            gmod = None
                            break
            p.run_phase()
        st_w.close()
        st_h.close()
        with ExitStack() as st:
            wd = [p.sbuf(st, "wd", [128, NHC, 512], BF16) for i in range(2)]
            streams = sorted(set(1 if tt < 2 else 0 for tt in tiles))
            gb = {}
            for s in streams:
                gb[s] = p.sbuf(st, "gb", [128, D], F32)
                p.dma('sp', gb[s][:, :], dv(bcast_row(modrow(layer, s, 5), D), 'modv', layer), 's_a%d' % s)
            mt = [p.sbuf(st, "mt", [128, NHC, 128], BF16) for i in range(3)]
            xt = [p.sbuf(st, "xt", [128, 512], F32) for i in range(3)]
            tm = [p.sbuf(st, "tm", [128, 512], F32) for i in range(2)]
            ps = [p.psum(st, "dps", [128, 512]) for i in range(4)]
            wsrc = ffn_w_down[layer].rearrange("(kc p) n -> p kc n", p=128)
            k = 0
            def wdload(cb_):
                for hf_ in range(4):
                    p.dma('pool', wd[cb_ % 2].sub(hf_)[:, hf_ * 11:(hf_ + 1) * 11, :],
                          dv(wsrc[:, hf_ * 11:(hf_ + 1) * 11, cb_ * 512:(cb_ + 1) * 512], 'cin'), 's_w%d' % ((cb_ % 2) * 4 + hf_))

            wdload(0)
            for cb in range(4):
                w = wd[cb % 2]
                if cb + 1 < 4:
                    wdload(cb + 1)
                for tt in tiles:
                    s = 1 if tt < 2 else 0
                    m = mt[k % 3]
                    x = xt[k % 3]
                    t = tm[k % 2]
                    pp = ps[k % 4]
                    p.dma('sp', m[:, :, :], dv(mT[tt], 'mT_in'), 's_m%d' % (k % 3))
                    p.dma('sp', x[:, :], dv(xres[tt * 128:(tt + 1) * 128, cb * 512:(cb + 1) * 512], 'xres', tt, cb), 's_x%d' % (k % 3))
                    for c in range(NHC):
                        p.matmul(pp[:, :], m[:, c, :], w.sub(c // 11)[:, c, :], start=(c == 0), stop=(c == NHC - 1))
                    p.tt('dve', t[:, :], pp[:, :], gb[s][:, cb * 512:(cb + 1) * 512], ALU.mult)
                    p.tt('dve', x[:, :], t[:, :], x[:, :], ALU.add)
                    p.dma('act', dv(xres[tt * 128:(tt + 1) * 128, cb * 512:(cb + 1) * 512], 'xres', tt, cb), x[:, :], 's_y%d' % (k % 3))
                    k += 1
            p.run_phase()

    def phase_l1_proj():
        st_h = ExitStack()
        hT = p.sbuf(st_h, "hT", [128, KC, NT], BF16)

        st_w = ExitStack()
        wbp = [p.sbuf(st_w, "wl", [128, KC, 256], BF16) for i in range(3)]
        wsrc_p = na_w_qkv[:, 0:2 * D].rearrange("(kc p) n -> p kc n", p=128)

        def _pre():
            for gi in range(2):
                p.dma('pool', wbp[gi][:, :, :], dv(wsrc_p[:, :, gi * 256:(gi + 1) * 256], 'cin'), 's_w%d' % gi)
        phase_norm(hT, list(range(18)), 1, 1, 0, pre=_pre)
        with ExitStack() as st:
            qstg = [p.sbuf(st, "qstg", [128, 512], BF16) for i in range(4)]
            qc = [0]

            def epi(chunk, bi, t0, n, pp):
                k = qc[0] % 4
                qc[0] += 1
                s = qstg[k]
                p.copy('act' if k % 2 == 0 else 'dve', s[:, 0:n], pp[:, 0:n])
                if chunk < 16:
                    p.dma('sp', dv(qT[chunk][:, t0:t0 + n], 'qT', chunk, bi), s[:, 0:n], 's_g%d' % k)
                else:
                    p.dma('sp', dv(kT[chunk - 16][:, t0:t0 + n], 'kT', chunk, bi), s[:, 0:n], 's_g%d' % k)

            def skip(chunk, bi):
                return chunk < 16 and bi == 0

            linear_fm(st, lambda kc, t0, n: hview(hT, kc, t0, n), KC, na_w_qkv[:, 0:2 * D], 2 * D, TB_ALL, epi, skip=skip, gw=256, wb=wbp, preloaded=2)
            wv = p.sbuf(st, "wv", [128, KC, D], BF16)
            wsrc = na_w_qkv[:, 2 * D:3 * D].rearrange("(kc p) n -> p kc n", p=128)
            for q in range(4):
                p.dma('pool', wv.sub(q)[:, :, q * 512:(q + 1) * 512], dv(wsrc[:, :, q * 512:(q + 1) * 512], 'cin'), 's_v%d' % q)
            vst = [p.sbuf(st, "vst", [128, D], BF16) for i in range(2)]
            psv = [p.psum(st, "psv", [128, 512]) for i in range(4)]
            for tt in range(18):
                vs = vst[tt % 2]
                for cb in range(4):
                    pp = psv[cb]
                    for kc in range(KC):
                        p.matmul(pp[:, :], hT.sub(tt)[:, kc, tt * 128:(tt + 1) * 128], wv.sub(cb)[:, kc, cb * 512:(cb + 1) * 512],
                                 start=(kc == 0), stop=(kc == KC - 1))
                    p.copy('act' if cb % 2 == 0 else 'dve', vs[:, cb * 512:(cb + 1) * 512], pp[:, :])
                p.dma('sp', dv(vtok[tt * 128:(tt + 1) * 128, :], 'vtok', tt), vs[:, :], 's_x%d' % (tt % 2))
            p.run_phase()
        st_w.close()
        st_h.close()

    def phase_na_attn():
        CM_MASK = NCMB - 1
        with ExitStack() as st:
            qh = [p.sbuf(st, "qh", [128, NT], BF16) for i in range(2)]
            kh = [p.sbuf(st, "kh", [128, NT], BF16) for i in range(2)]
            vh = [p.sbuf(st, "vh", [128, 18, 128], BF16) for i in range(2)]
            bf = [p.sbuf(st, "bf", [128, NCMB * 128], F32) for i in range(2)]
            bq = [p.sbuf(st, "bq", [128, NCMB * 128], BF16) for i in range(2)]
            oh = [p.sbuf(st, "oh", [128, NT], BF16) for i in range(2)]
            pT = [p.sbuf(st, "pT", [128, 4, 256], BF16) for i in range(4)]
            rc = [p.sbuf(st, "rc", [128, 256], F32) for i in range(2)]
            psS = [p.psum(st, "psS", [128, 2, 256]) for i in range(4)]
            psO = [p.psum(st, "psO", [128, 512]) for i in range(2)]
            psZ = [p.psum(st, "psZ", [128, 512]) for i in range(2)]

            def hload(h):
                b = h % 2
                p.dma('sp', qh[b][:, TC:NT], dv(qT[h][:, TC:NT], 'qTh'), 's_q%d' % b)
                p.dma('sp', kh[b][:, :], dv(kT[h], 'kTh'), 's_k%d' % b)
                p.dma('sp', vh[b][:, :, :], dv(vtok[:, h * 128:(h + 1) * 128].rearrange("(t p) c -> p t c", p=128), 'vh'),
                      's_v%d' % b)
                p.dma('sp', bf[b][:, :], dv(na_bias[h], 'cin'), 's_b%d' % b)
                p.act(bq[b][:, :], bf[b][:, :], AF.Copy, scale=1.0 / NA_SCALE)

            steps = []
            for h in range(16):
                for qp in range(8):
                    dA = dict(na_pairs[4 * qp])
                    dB = dict(na_pairs[4 * qp + 2])
                    items = [(0, None, None), (1, None, None)]
                    for R in sorted(set(dA) | set(dB)):
                        items.append((2 + R // 2, dA.get(R, CM_MASK), dB.get(R, CM_MASK)))
                    assert len(items) <= 8
                    for half in range(2):
                        steps.append((h, qp, half, items[half * 4:half * 4 + 4], len(items)))
            NS = len(steps)

            def qk(si):
                h, qp, half, its, ntot = steps[si]
                b = h % 2
                q0 = TC + qp * 256
                for j, (kt, cmA, cmB) in enumerate(its):
                    pS = psS[(si % 2) * 2 + j // 2]
                    jj = j % 2
                    p.matmul(pS[:, jj, :], kh[b][:, kt * 128:(kt + 1) * 128], qh[b][:, q0:q0 + 256], start=True, stop=(cmA is None))
                    if cmA is not None:
                        p.matmul(pS[:, jj, 0:128], bq[b][:, cmA * 128:(cmA + 1) * 128], ident[:, :], start=False, stop=False)
                        p.matmul(pS[:, jj, 128:256], bq[b][:, cmB * 128:(cmB + 1) * 128], ident[:, :], start=False, stop=True)

            def front(si):
                h, qp, half, its, ntot = steps[si]
                b = h % 2
                pt = pT[si % 4]
                po = psO[(si // 2) % 2]
                n_ = len(its)
                for bk in range((n_ + 1) // 2):
                    m_ = min(2, n_ - bk * 2)
                    pS = psS[(si % 2) * 2 + bk]
                    p.act(pt.sub(bk)[:, bk * 2:bk * 2 + m_, :], pS[:, 0:m_, :], AF.Exp, scale=NA_SCALE)
                for j, (kt, cmA, cmB) in enumerate(its):
                    gi = half * 4 + j
                    p.matmul(po[:, 0:256], vh[b][:, kt, :], pt.sub(j // 2)[:, j, :], start=(gi == 0), stop=(gi == ntot - 1))

            def back(si):
                h, qp, half, its, ntot = steps[si]
                b = h % 2
                q0 = TC + qp * 256
                pt = pT[si % 4]
                po = psO[(si // 2) % 2]
                pz = psZ[(si // 2) % 2]
                for j in range(len(its)):
                    gi = half * 4 + j
                    p.matmul(pz[:, 0:256], ones16[:, :], pt.sub(j // 2)[:, j, :], start=(gi == 0), stop=(gi == ntot - 1))
                if half == 1:
                    r = rc[(si // 2) % 2]
                    p.recip(r[:, :], pz[:, 0:256])
                    p.tt('dve', oh[b].sub(qp)[:, q0:q0 + 256], po[:, 0:256], r[:, :], ALU.mult)
                    if qp == 7:
                        p.dma('sp', dv(mixT[h][:, TC:NT], 'mixT', h), oh[b].sub(*range(8))[:, TC:NT], 's_o%d' % b)

            hload(0)
            qk(0)
            for si in range(NS + 1):
                if si < NS:
                    h, qp, half, its, ntot = steps[si]
                    if qp == 0 and half == 0 and h + 1 < 16:
                        hload(h + 1)
                    if si + 1 < NS:
                        qk(si + 1)
                if si >= 1:
                    back(si - 1)
                if si < NS:
                    front(si)
            p.run_phase()

    def phase_final():
        with ExitStack() as st:
            gb = p.sbuf(st, "gb", [128, D], F32)
            p.dma('sp', gb[:, :], dv(bcast_row(final_norm[0:1, :], D), 'cin'), 's_a0')
            NB = 4
            xt = [p.sbuf(st, "xt", [128, D], F32) for i in range(NB)]
            ot = [p.sbuf(st, "ot", [128, D], F32) for i in range(3)]
            junk = p.sbuf(st, "junk", [128, D], BF16)
            stat = p.sbuf(st, "stat", [128, 4 * NB], F32)
            H2 = D // 2

            def stage0(n):
                x = xt[n % NB]
                p.dma('sp', x[:, :], xsrc(n + 2, 1), 's_x%d' % (n % NB))
                ss = stat.sub(n % NB)
                c0 = (n % NB) * 4
                p.act(junk[:, :], x[:, :], AF.Square, accum_out=ss[:, c0:c0 + 1])
                p.ts('dve', ss[:, c0 + 1:c0 + 2], ss[:, c0:c0 + 1], 1.0 / D, EPS, ALU.mult, ALU.add)
                p.act(ss[:, c0 + 2:c0 + 3], ss[:, c0 + 1:c0 + 2], AF.Sqrt)
                p.recip(ss[:, c0 + 3:c0 + 4], ss[:, c0 + 2:c0 + 3])

            def stage1(n):
                x = xt[n % NB]
                o = ot[n % 3]
                ss = stat.sub(n % NB)
                c0 = (n % NB) * 4
                p.stt(o[:, :], x[:, :], ss[:, c0 + 3:c0 + 4], gb[:, :], ALU.mult, ALU.mult)
                p.dma('act', dv(out[n * 128:(n + 1) * 128, :], 'out', n), o[:, :], 's_y%d' % (n % 3))

            for step in range(17):
                if step < 16:
                    stage0(step)
                if 0 <= step - 1 < 16:
                    stage1(step - 1)
            p.run_phase()

    phases = [
        phase_mod,
        phase_l0_proj,
        None,
        phase_mla,
        phase_lru,
        lambda: phase_outproj(w_out0, 0, list(range(18)), 0),
        lambda: phase_ffn(0, list(range(18))),
        phase_l1_proj,
        phase_na_attn,
        lambda: phase_outproj(na_w_out, 1, list(range(2, 18)), 1),
        lambda: phase_ffn(1, list(range(2, 18))),
        phase_final,
    ]
    for i, ph in enumerate(phases):
        if i > stop_after:
            break
        if ph is not None:
            ph()
    gst.close()
    p.close()
    return nc, p


def _fo_part(v, nchunks):
    return np.ascontiguousarray(np.asarray(v, np.float32).reshape(nchunks, 128).T)


def prep_shared(inp):
    f32 = np.float32
    sh = {}
    sh["ident"] = np.eye(128, dtype=f32)
    sh["mod_w"] = np.ascontiguousarray(inp["mod_w"], f32)
    sh["mod_b"] = np.ascontiguousarray(inp["mod_b"], f32)
    sh["norm_mix"] = np.ascontiguousarray(inp["norm_mix"], f32)
    sh["norm_ffn"] = np.ascontiguousarray(inp["norm_ffn"], f32)
    sh["final_norm"] = np.ascontiguousarray(inp["final_norm"], f32).reshape(1, D)
    f = np.arange(64)
    a_, j_, p_ = f // 32, (f // 16) % 2, f % 16
    partner = a_ * 32 + (1 - j_) * 16 + p_
    w_in = np.asarray(inp["mla_w_in"][0], f32)
    kr = w_in[:, 768:832]
    krs = kr[:, partner]
    sh["w_in"] = np.ascontiguousarray(np.concatenate(
        [w_in[:, 0:768], kr, kr, krs, krs, w_in[:, 832:1856], w_in[:, 1856:2880]], axis=1))
    w_uq = np.asarray(inp["mla_w_uq"][0], f32)
    cols = []
    for h in range(8):
        cols.append(w_uq[:, h * 192:h * 192 + 128])
    for j in range(4):
        r0 = w_uq[:, (2 * j) * 192 + 128:(2 * j) * 192 + 192]
        r1 = w_uq[:, (2 * j + 1) * 192 + 128:(2 * j + 1) * 192 + 192]
        cols += [r0, r1, r0[:, partner], r1[:, partner]]
    sh["w_uq"] = np.ascontiguousarray(np.concatenate(cols, axis=1))
    sh["qn"] = _fo_part(inp["mla_q_norm"][0], 4)
    sh["kvn"] = _fo_part(inp["mla_kv_norm"][0], 2)
    w_ukv = np.asarray(inp["mla_w_ukv"][0], f32).reshape(256, 8, 256)
    sh["w_ukv_k"] = np.ascontiguousarray(w_ukv[:, :, 0:128].reshape(256, 1024))
    sh["w_ukv_v"] = np.ascontiguousarray(w_ukv[:, :, 128:256].reshape(256, 1024))
    pos = np.arange(T)
    inv = (10000.0 ** (-np.arange(16, dtype=np.float32) / 16)).astype(f32)
    ar = (pos // 64).astype(f32)[:, None] * inv
    ac = (pos % 64).astype(f32)[:, None] * inv
    ang = np.concatenate([ar, ar, ac, ac], axis=-1)
    cos = np.cos(ang).astype(f32)
    sin = np.sin(ang).astype(f32)
    sgn = np.where(j_ == 0, -1.0, 1.0).astype(f32)
    ss = sin * sgn[None, :]
    sh["cosT"] = np.ascontiguousarray(np.concatenate([cos.T, cos.T], axis=0))
    sh["ssT"] = np.ascontiguousarray(np.concatenate([ss.T, ss.T], axis=0))
    lcw = np.asarray(inp["lru_conv_w"][0], f32)
    sh["lcw"] = np.ascontiguousarray(lcw.reshape(4, 8, 128).transpose(2, 1, 0).reshape(128, 32))
    sh["lcb"] = _fo_part(inp["lru_conv_b"][0], 8)
    sh["ga_w"] = np.ascontiguousarray(inp["lru_gate_a_w"][0], f32)
    sh["gx_w"] = np.ascontiguousarray(inp["lru_gate_x_w"][0], f32)
    sh["gab"] = _fo_part(np.asarray(inp["lru_gate_a_b"][0]).reshape(-1), 16)
    sh["gxb"] = _fo_part(np.asarray(inp["lru_gate_x_b"][0]).reshape(-1), 16)
    sh["lam"] = _fo_part(np.asarray(inp["lru_lambda"][0]).reshape(-1), 16)
    sh["w_out0"] = np.ascontiguousarray(inp["mix_w_out"][0], f32)
    sh["ffn_w_up"] = np.ascontiguousarray(inp["ffn_w_up"], f32)
    fcw = np.asarray(inp["ffn_conv_w"], f32)
    sh["fcw"] = np.ascontiguousarray(fcw.reshape(2, 3, NHC, 128).transpose(0, 3, 2, 1).reshape(2, 128, NHC * 3))
    fcb = np.asarray(inp["ffn_conv_b"], f32)
    sh["fcb"] = np.ascontiguousarray(fcb.reshape(2, NHC, 128).transpose(0, 2, 1))
    sh["ffn_w_down"] = np.ascontiguousarray(inp["ffn_w_down"], f32)
    sh["na_w_qkv"] = np.ascontiguousarray(inp["na_w_qkv"][0], f32)
    sh["na_w_out"] = np.ascontiguousarray(inp["na_w_out"][0], f32)
    combos, _ = na_patterns()
    rb = np.asarray(inp["na_rel_bias"][0], f32)
    colq = np.arange(64)
    cs = np.clip(colq - 8, 0, 48)
    col_ok = (colq[None, :] >= cs[:, None]) & (colq[None, :] < cs[:, None] + 16)
    dc_idx = np.clip(colq[None, :] - colq[:, None] + 15, 0, 30)
    tab = np.full((16, 128, len(combos) + 1, 128), MASKVAL, f32)
    for (delta, pat), cm in combos.items():
        for krl in range(2):
            for qrl in range(2):
                if not pat[krl * 2 + qrl]:
                    continue
                dr = delta + krl - qrl + 7
                g = rb[:, dr, :][:, dc_idx]
                blk = np.where(col_ok[None], g, f32(MASKVAL))
                tab[:, qrl * 64:(qrl + 1) * 64, cm, krl * 64:(krl + 1) * 64] = blk
    sh["na_bias"] = np.ascontiguousarray(tab.reshape(16, 128, (len(combos) + 1) * 128))
    return sh


def prep_core(inp, b):
    m = {}
    m["x"] = np.ascontiguousarray(inp["x"][b], np.float32)
    m["ctx"] = np.ascontiguousarray(inp["ctx"][b], np.float32)
    cc = np.stack([np.asarray(inp["c"][b], np.float32).reshape(16, 128).T,
                   np.asarray(inp["c_ctx"], np.float32).reshape(16, 128).T], axis=-1)
    m["cc"] = np.ascontiguousarray(cc.reshape(128, 32))
    return m


_NC_CACHE = {}


def kernel(**inputs):
    if "nc" not in _NC_CACHE:
        _NC_CACHE["nc"] = build()[0]
    nc = _NC_CACHE["nc"]
    sh = prep_shared(inputs)
    in_maps = []
    for b in range(8):
        m = dict(sh)
        m.update(prep_core(inputs, b))
        in_maps.append(m)
    res = run_bass_kernel_spmd(nc, in_maps, core_ids=list(range(8)))
    return np.stack([np.asarray(r["out"], np.float32) for r in res.results], axis=0)
```

```python
import numpy as np
from contextlib import ExitStack
import concourse.bass as bass
import concourse.mybir as mybir
from concourse.bass_utils import run_bass_kernel_spmd

F32 = mybir.dt.float32
BF16 = mybir.dt.bfloat16
AF = mybir.ActivationFunctionType
ALU = mybir.AluOpType
AX = mybir.AxisListType

ENGS = ['sp', 'pe', 'act', 'dve', 'pool']


class V:
    __slots__ = ('ap', 'keys')

    def __init__(self, ap, keys):
        self.ap = ap
        self.keys = tuple(keys)


class Tile:
    def __init__(self, tensor, name, keys=None):
        self.t = tensor
        self.name = name
        self.keys = (name,) if keys is None else tuple(keys)

    def __getitem__(self, idx):
        return V(self.t[idx], self.keys)

    def sub(self, *subs):
        return Tile(self.t, self.name, [(self.name, s) for s in subs])


class Instr:
    __slots__ = ('eng', 'fn', 'deps', 'is_dma', 'dsem', 'signal', 'val')


class DSem:
    __slots__ = ('h', 'count')


class Prog:
    def __init__(self, nc, n_dma_sems=40):
        self.nc = nc
        self.stack = ExitStack()
        self.csem = {e: self.stack.enter_context(nc.semaphore("c_" + e)) for e in ['pe', 'act', 'dve', 'pool']}
        self.ccount = {e: 0 for e in self.csem}
        self.dpool = [self.stack.enter_context(nc.semaphore("d%d" % i)) for i in range(n_dma_sems)]
        self.dsems = {}
        self.ins = {e: [] for e in ENGS}
        self.lastw = {}
        self.readers = {}
        self.uid = 0
        self.n_instr = 0
        with nc.Block() as b:
            @b.sync
            def _(e):
                for s in list(self.csem.values()) + self.dpool:
                    e.sem_clear(s)

    def close(self):
        self.stack.close()

    def uname(self, base):
        self.uid += 1
        return "%s_%d" % (base, self.uid)

    def sbuf(self, st, name, shape, dtype):
        n = self.uname(name)
        t = st.enter_context(self.nc.sbuf_tensor(n, list(shape), dtype))
        return Tile(t, n)

    def psum(self, st, name, shape, dtype=F32):
        n = self.uname(name)
        t = st.enter_context(self.nc.psum_tensor(n, list(shape), dtype))
        return Tile(t, n)

    def dsem(self, name):
        d = self.dsems.get(name)
        if d is None:
            d = DSem()
            d.h = self.dpool[len(self.dsems)]
            d.count = 0
            self.dsems[name] = d
        return d

    def add(self, eng, fn, reads=(), writes=(), dsem=None):
        I = Instr()
        I.eng = eng
        I.fn = fn
        I.is_dma = dsem is not None
        I.signal = False
        I.val = 0
        I.dsem = None
        if I.is_dma:
            d = self.dsem(dsem)
            d.count += 16
            I.dsem = d
            I.val = d.count
        deps = set()
        for k in reads:
            w = self.lastw.get(k)
            if w is not None:
                deps.add(w)
        for k in writes:
            w = self.lastw.get(k)
            if w is not None:
                deps.add(w)
            for r in self.readers.get(k, ()):
                deps.add(r)
        I.deps = [d for d in deps if d.is_dma or not (eng == 'pe' and d.eng == 'pe' and not I.is_dma)]
        for k in reads:
            self.readers.setdefault(k, []).append(I)
        for k in writes:
            self.lastw[k] = I
            self.readers[k] = []
        self.ins[eng].append(I)
        self.n_instr += 1
        return I

    def _emit_engine(self, e, eng):
        waited = {}
        dma_final = {}
        for I in self.ins[e]:
            toks = {}
            for d in I.deps:
                s = d.dsem.h if d.is_dma else self.csem[d.eng]
                key = id(s)
                if key not in toks or toks[key][1] < d.val:
                    toks[key] = (s, d.val)
            for key, (s, v) in toks.items():
                if waited.get(key, 0) < v:
                    eng.wait_ge(s, v)
                    waited[key] = v
            r = I.fn(eng)
            if I.is_dma:
                r.then_inc(I.dsem.h, 16)
                dma_final[id(I.dsem.h)] = (I.dsem.h, I.val)
            elif I.signal:
                r.then_inc(self.csem[e], 1)
        for key, (s, v) in dma_final.items():
            if waited.get(key, 0) < v:
                eng.wait_ge(s, v)

    def run_phase(self):
        nc = self.nc
        for e in ENGS:
            for I in self.ins[e]:
                for d in I.deps:
                    if not d.is_dma:
                        d.signal = True
        for e in ENGS:
            if e == 'sp':
                continue
            c = self.ccount[e]
            for I in self.ins[e]:
                if I.is_dma:
                    continue
                if I.signal:
                    c += 1
                    I.val = c
            self.ccount[e] = c
        with nc.Block() as b:
            decs = {'sp': b.sync, 'pe': b.tensor, 'act': b.scalar, 'dve': b.vector, 'pool': b.gpsimd}
            for e in ENGS:
                if not self.ins[e]:
                    continue
                decs[e](lambda eng, e=e: self._emit_engine(e, eng))
        self.ins = {e: [] for e in ENGS}
        self.lastw = {}
        self.readers = {}

    @staticmethod
    def _rk(*vs):
        ks = []
        for v in vs:
            if isinstance(v, V):
                ks.extend(v.keys)
        return ks

    @staticmethod
    def _a(v):
        return v.ap if isinstance(v, V) else v

    def matmul(self, out, lhsT, rhs, start=True, stop=True):
        o, l, r = out.ap, lhsT.ap, rhs.ap
        return self.add('pe', lambda e: e.matmul(o, l, r, start=start, stop=stop),
                        reads=self._rk(lhsT, rhs), writes=self._rk(out))

    def transpose(self, out, in_, ident):
        o, i, d = out.ap, in_.ap, ident.ap
        return self.add('pe', lambda e: e.transpose(o, i, d), reads=self._rk(in_, ident), writes=self._rk(out))

    def act(self, out, in_, func, bias=None, scale=None, accum_out=None, eng='act'):
        kw = {}
        if bias is not None:
            kw['bias'] = self._a(bias)
        if scale is not None:
            kw['scale'] = self._a(scale)
        if accum_out is not None:
            kw['accum_out'] = accum_out.ap
        o, i = out.ap, in_.ap
        return self.add(eng, lambda e: e.activation(o, i, func, **kw),
                        reads=self._rk(in_, bias, scale), writes=self._rk(out, accum_out))

    def tt(self, eng, out, in0, in1, op):
        o, a, b = out.ap, in0.ap, in1.ap
        return self.add(eng, lambda e: e.tensor_tensor(o, a, b, op), reads=self._rk(in0, in1), writes=self._rk(out))

    def ts(self, eng, out, in0, s1, s2, op0, op1=None, accum_out=None):
        o, a = out.ap, in0.ap
        x1, x2 = self._a(s1), self._a(s2)
        kw = {}
        if op1 is not None:
            kw['op1'] = op1
        if accum_out is not None:
            kw['accum_out'] = accum_out.ap
        return self.add(eng, lambda e: e.tensor_scalar(o, a, x1, x2, op0, **kw),
                        reads=self._rk(in0, s1, s2), writes=self._rk(out, accum_out))

    def stt(self, out, in0, scalar, in1, op0, op1, eng='dve'):
        o, a, b = out.ap, in0.ap, in1.ap
        s = self._a(scalar)
        return self.add(eng, lambda e: e.scalar_tensor_tensor(o, a, s, b, op0, op1),
                        reads=self._rk(in0, scalar, in1), writes=self._rk(out))

    def copy(self, eng, out, in_):
        o, i = out.ap, in_.ap
        if eng == 'act':
            return self.add(eng, lambda e: e.copy(o, i), reads=self._rk(in_), writes=self._rk(out))
        return self.add(eng, lambda e: e.tensor_copy(o, i), reads=self._rk(in_), writes=self._rk(out))

    def memset(self, eng, out, c):
        o = out.ap
        return self.add(eng, lambda e: e.memset(o, c), writes=self._rk(out))

    def recip(self, out, in_, eng='dve'):
        o, i = out.ap, in_.ap
        return self.add(eng, lambda e: e.reciprocal(o, i), reads=self._rk(in_), writes=self._rk(out))

    def scan(self, out, d0, d1, initial, op0=None, op1=None):
        o, a, b = out.ap, d0.ap, d1.ap
        ini = self._a(initial)
        op0 = op0 or ALU.mult
        op1 = op1 or ALU.add
        return self.add('dve', lambda e: e.tensor_tensor_scan(o, a, b, ini, op0, op1),
                        reads=self._rk(d0, d1, initial), writes=self._rk(out))

    def reduce(self, out, in_, op, axis=None, eng='dve'):
        o, i = out.ap, in_.ap
        axis = axis or AX.X
        return self.add(eng, lambda e: e.tensor_reduce(o, i, axis, op), reads=self._rk(in_), writes=self._rk(out))

    def dma(self, q, out, in_, sem, **kw):
        o, i = out.ap, in_.ap
        return self.add(q, lambda e: e.dma_start(o, i, **kw), reads=self._rk(in_), writes=self._rk(out),
                        dsem=("sw_" if q == 'pool' else "hw_") + sem)


def dram_view(ap, key):
    return V(ap, (key,))


T = 2048
TC = 256
NT = 2304
D = 2048
KC = 16
HID = 5632
NHC = 44
EPS = 1e-6
TB_ALL = [(0, 256), (256, 512), (768, 512), (1280, 512), (1792, 512)]
TB_LAT = TB_ALL[1:]
MLA_SCALE = 192.0 ** -0.5
NA_SCALE = 128.0 ** -0.5
MASKVAL = -30000.0


def bcast_row(ap_row, n):
    return bass.AP(ap_row.tensor, ap_row.offset, [[0, 128], [1, n]])


def na_patterns():
    combos = {}
    pairs = {}
    for r0 in range(0, 32, 2):
        lst = []
        for R in range(0, 32, 2):
            pat = []
            for krl in range(2):
                for qrl in range(2):
                    r = r0 + qrl
                    rs = min(max(r - 4, 0), 24)
                    kr = R + krl
                    pat.append(1 if rs <= kr <= rs + 7 else 0)
            if not any(pat):
                continue
            key = (R - r0, tuple(pat))
            if key not in combos:
                combos[key] = len(combos)
            lst.append((R, combos[key]))
        pairs[r0] = lst
    return combos, pairs


def build(debug=False, stop_after=99):
    nc = bass.Bass("TRN2", target_bir_lowering=False)
    combos, na_pairs = na_patterns()
    NCMB = len(combos) + 1

    def din(name, shape):
        return nc.dram_tensor(name, list(shape), F32, kind="ExternalInput").ap()

    def dscr(name, shape, dt):
        return nc.dram_tensor(name, list(shape), dt, kind="ExternalOutput" if debug else "Internal").ap()

    x_in = din("x", [T, D])
    ctx_in = din("ctx", [TC, D])
    cc_in = din("cc", [128, 32])
    ident_in = din("ident", [128, 128])
    mod_w = din("mod_w", [2, D, 6 * D])
    mod_b = din("mod_b", [2, 6 * D])
    norm_mix = din("norm_mix", [2, D])
    norm_ffn = din("norm_ffn", [2, D])
    final_norm = din("final_norm", [1, D])
    w_in = din("w_in", [D, 3072])
    w_uq = din("w_uq", [512, 2048])
    qn_in = din("qn", [128, 4])
    kvn_in = din("kvn", [128, 2])
    w_ukv_k = din("w_ukv_k", [256, 1024])
    w_ukv_v = din("w_ukv_v", [256, 1024])
    cosT_in = din("cosT", [128, T])
    ssT_in = din("ssT", [128, T])
    lcw_in = din("lcw", [128, 8 * 4])
    lcb_in = din("lcb", [128, 8])
    ga_w = din("ga_w", [2, 8, 128, 128])
    gx_w = din("gx_w", [2, 8, 128, 128])
    gab_in = din("gab", [128, 16])
    gxb_in = din("gxb", [128, 16])
    lam_in = din("lam", [128, 16])
    w_out0 = din("w_out0", [D, D])
    ffn_w_up = din("ffn_w_up", [2, D, 2 * HID])
    fcw_in = din("fcw", [2, 128, NHC * 3])
    fcb_in = din("fcb", [2, 128, NHC])
    ffn_w_down = din("ffn_w_down", [2, HID, D])
    na_w_qkv = din("na_w_qkv", [D, 3 * D])
    na_bias = din("na_bias", [16, 128, NCMB * 128])
    na_w_out = din("na_w_out", [D, D])
    out = nc.dram_tensor("out", [T, D], F32, kind="ExternalOutput").ap()

    modv = dscr("modv", [2, 2, 6 * D], F32)
    xres = dscr("xres", [NT, D], F32)
    cT = dscr("cT", [8, 128, NT], F32)
    rxT = dscr("rxT", [8, 128, NT], F32)
    ryT = dscr("ryT", [8, 128, NT], F32)
    qT = dscr("qT", [16, 128, NT], BF16)
    kT = dscr("kT", [17, 128, NT], BF16)
    vtok = dscr("vtok", [NT, D], BF16)
    mixT = dscr("mixT", [16, 128, NT], BF16)
    mT = dscr("mT", [18, 128, NHC, 128], BF16)

    p = Prog(nc, n_dma_sems=90)
    gst = ExitStack()
    ident = p.sbuf(gst, "ident", [128, 128], BF16)
    ones16 = p.sbuf(gst, "ones16", [128, 128], BF16)
    ones32 = p.sbuf(gst, "ones32", [128, 128], F32)
    epsc = p.sbuf(gst, "epsc", [128, 1], F32)
    p.dma('pool', ident[:, :], V(ident_in, ('c',)), 's_c0')
    p.memset('dve', ones16[:, :], 1.0)
    p.memset('dve', ones32[:, :], 1.0)
    p.memset('dve', epsc[:, :], EPS)

    def dv(ap, *key):
        return V(ap, (key,))

    def xsrc(tt, stage):
        if stage == 0:
            if tt < 2:
                return dv(ctx_in[tt * 128:(tt + 1) * 128, :], 'xin')
            return dv(x_in[(tt - 2) * 128:(tt - 1) * 128, :], 'xin')
        return dv(xres[tt * 128:(tt + 1) * 128, :], 'xres', tt)

    def modrow(layer, stream, j):
        return modv[layer, stream:stream + 1, j * D:(j + 1) * D]

    sT = p.sbuf(gst, "sT", [128, 32], BF16)

    def gen_mod(st, layer, wcols, u0=0, u1=None):
        mw = [p.sbuf(st, "mw", [128, 16, wcols], BF16) for i in range(3)]
        mb = [p.sbuf(st, "mb_", [2, wcols], F32) for i in range(2)]
        mo = [p.sbuf(st, "mo", [2, wcols], F32) for i in range(2)]
        ps = [p.psum(st, "mps", [128, 512]) for i in range(2)]
        wsrc = mod_w[layer].rearrange("(kc p) n -> p kc n", p=128)
        nu = 6 * D // wcols if u1 is None else u1

        def wl(u):
            p.dma('pool', mw[u % 3][:, :, :], dv(wsrc[:, :, u * wcols:(u + 1) * wcols], 'cin'), 's_h%d' % (u % 3))

        wl(u0)
        wl(u0 + 1)
        for u in range(u0, nu):
            if u + 2 < nu:
                wl(u + 2)
            p.dma('sp', mb[u % 2][:, :], dv(bass.AP(mod_b.tensor, layer * 6 * D + u * wcols, [[0, 2], [1, wcols]]), 'cin'), 's_i%d' % (u % 2))
            pp = ps[u % 2]
            for kc in range(16):
                p.matmul(pp[0:2, 0:wcols], sT[:, 2 * kc:2 * kc + 2], mw[u % 3][:, kc, :], start=(kc == 0), stop=(kc == 15))
            p.tt('dve', mo[u % 2][:, :], pp[0:2, 0:wcols], mb[u % 2][:, :], ALU.add)
            p.dma('sp', dv(modv[layer][:, u * wcols:(u + 1) * wcols], 'modv', layer, u), mo[u % 2][:, :], 's_j%d' % (u % 2))
            yield

    def phase_mod():
        with ExitStack() as st:
            cct = p.sbuf(st, "cc", [128, 32], F32)
            p.dma('sp', cct[:, :], dv(cc_in, 'cin'), 's_a0')
            p.act(sT[:, :], cct[:, :], AF.Silu)
            for _ in gen_mod(st, 0, 512, 0, 8):
                pass
            p.run_phase()

    def phase_norm(hT, tiles, stage, layer, which, run=True, pre=None):
        norm_w = norm_mix if which == 0 else norm_ffn
        jsh, jsc = (0, 1) if which == 0 else (3, 4)
        with ExitStack() as st:
            if pre is not None:
                pre()
            NB = 4
            xt = [p.sbuf(st, "xt", [128, D], F32) for i in range(NB)]
            junk = p.sbuf(st, "junk", [128, D], BF16)
            t1 = [p.sbuf(st, "t1", [128, D], F32) for i in range(2)]
            hb = [p.sbuf(st, "hb", [128, D], BF16) for i in range(3)]
            stat = p.sbuf(st, "stat", [128, 4 * NB], F32)
            pst = [p.psum(st, "pst", [128, 8, 128], BF16) for i in range(4)]
            streams = sorted(set(1 if tt < 2 else 0 for tt in tiles))
            gsb = {}
            shb = {}
            gb = xt[0]
            p.dma('sp', gb[:, :], dv(bcast_row(norm_w[layer:layer + 1, :], D), 'cin'), 's_x0')
            for s in streams:
                gsb[s] = p.sbuf(st, "gsb", [128, D], F32)
                shb[s] = p.sbuf(st, "shb", [128, D], F32)
                p.dma('sp', gsb[s][:, :], dv(bcast_row(modrow(layer, s, jsc), D), 'modv', layer), 's_a%d' % s)
                p.dma('sp', shb[s][:, :], dv(bcast_row(modrow(layer, s, jsh), D), 'modv', layer), 's_b%d' % s)
                p.stt(gsb[s][:, :], gsb[s][:, :], 1.0, gb[:, :], ALU.add, ALU.mult)
            N_ = len(tiles)
            H2 = D // 2

            def stage0(n):
                tt = tiles[n]
                x = xt[n % NB]
                p.dma('sp', x[:, :], xsrc(tt, stage), 's_x%d' % (n % NB))
                ss = stat.sub(n % NB)
                c0 = (n % NB) * 4
                p.act(junk[:, :], x[:, :], AF.Square, accum_out=ss[:, c0:c0 + 1])
                p.ts('dve', ss[:, c0 + 1:c0 + 2], ss[:, c0:c0 + 1], 1.0 / D, EPS, ALU.mult, ALU.add)
                p.act(ss[:, c0 + 2:c0 + 3], ss[:, c0 + 1:c0 + 2], AF.Sqrt)
                p.recip(ss[:, c0 + 3:c0 + 4], ss[:, c0 + 2:c0 + 3])

            def stage1(n):
                tt = tiles[n]
                s = 1 if tt < 2 else 0
                x = xt[n % NB]
                ss = stat.sub(n % NB)
                c0 = (n % NB) * 4
                t = t1[n % 2]
                h = hb[n % 3]
                Q = D // 4
                for q in range(4):
                    cs = slice(q * Q, (q + 1) * Q)
                    p.stt(t.sub(q)[:, cs], x[:, cs], ss[:, c0 + 3:c0 + 4], gsb[s][:, cs], ALU.mult, ALU.mult)
                    p.tt('pool' if q < 3 else 'dve', h.sub(q)[:, cs], t.sub(q)[:, cs], shb[s][:, cs], ALU.add)

            def stage2(n):
                tt = tiles[n]
                h = hb[n % 3]
                for half in range(2):
                    pp = pst[(2 * n + half) % 4]
                    for j in range(8):
                        kc = half * 8 + j
                        p.transpose(pp[:, j, :], h.sub(kc // 4)[:, kc * 128:(kc + 1) * 128], ident[:, :])
                    p.copy('act', hT.sub(tt)[:, half * 8:(half + 1) * 8, tt * 128:(tt + 1) * 128], pp[:, :, :])

            for step in range(N_ + 2):
                if step < N_:
                    stage0(step)
                if 0 <= step - 1 < N_:
                    stage1(step - 1)
                if 0 <= step - 2 < N_:
                    stage2(step - 2)
            if run:
                p.run_phase()

    def hview(hT, kc, t0, n):
        keys = [(hT.name, tt) for tt in range(t0 // 128, (t0 + n + 127) // 128)]
        return V(hT.t[:, kc, t0:t0 + n], keys)

    def linear_fm(st, src, kcn, W, ncols, tokblocks, epilogue, skip=None, gw=256, nbuf=3, tag="w", ps=None, wb=None, preloaded=0):
        if wb is None:
            wb = [p.sbuf(st, "wl", [128, kcn, gw], BF16) for i in range(nbuf)]
        if ps is None:
            ps = [p.psum(st, "lps", [128, 512]) for i in range(4)]
        wsrc = W.rearrange("(kc p) n -> p kc n", p=128)
        cnt = 0
        ng = ncols // gw

        def wload(gi):
            p.dma('pool', wb[gi % nbuf][:, :, :], dv(wsrc[:, :, gi * gw:(gi + 1) * gw], 'cin'), 's_%s%d' % (tag, gi % nbuf))

        for gi in range(preloaded, min(nbuf - 1, ng)):
            wload(gi)
        for gi in range(ng):
            w = wb[gi % nbuf]
            if gi + nbuf - 1 < ng:
                wload(gi + nbuf - 1)
            for cj in range(gw // 128):
                chunk = gi * (gw // 128) + cj
                for bi, (t0, n) in enumerate(tokblocks):
                    if skip is not None and skip(chunk, bi):
                        continue
                    pp = ps[cnt % 4]
                    cnt += 1
                    for kc in range(kcn):
                        p.matmul(pp[:, 0:n], w[:, kc, cj * 128:(cj + 1) * 128], src(kc, t0, n),
                                 start=(kc == 0), stop=(kc == kcn - 1))
                    epilogue(chunk, bi, t0, n, pp)

    def phase_l0_proj():
        st_h = ExitStack()
        hT = p.sbuf(st_h, "hT", [128, KC, NT], BF16)

        st_w = ExitStack()
        wbp = [p.sbuf(st_w, "wl", [128, KC, 256], BF16) for i in range(3)]
        wsrc_p = w_in.rearrange("(kc p) n -> p kc n", p=128)

        def _pre():
            for gi in range(2):
                p.dma('pool', wbp[gi][:, :, :], dv(wsrc_p[:, :, gi * 256:(gi + 1) * 256], 'cin'), 's_w%d' % gi)
        phase_norm(hT, list(range(18)), 0, 0, 0, pre=_pre)
        with ExitStack() as st:
            stg = [p.sbuf(st, "stg", [128, 512], F32) for i in range(4)]
            cnt = [0]

            def epi(chunk, bi, t0, n, pp):
                k = cnt[0] % 4
                cnt[0] += 1
                s = stg[k]
                p.copy('act' if k % 2 == 0 else 'dve', s[:, 0:n], pp[:, 0:n])
                if chunk < 8:
                    dst = cT[chunk]
                else:
                    dst = (rxT if chunk < 16 else ryT)[(chunk - 8) % 8]
                p.dma('sp', dv(dst[:, t0:t0 + n], 'rT', chunk, bi), s[:, 0:n], 's_g%d' % k)

            linear_fm(st, lambda kc, t0, n: hview(hT, kc, t0, n), KC, w_in, 3072, TB_ALL, epi, gw=256, wb=wbp, preloaded=2)
            p.run_phase()
        st_w.close()
        st_h.close()
        if stop_after <= 1:
            return
        with ExitStack() as st:
            cqT = p.sbuf(st, "cqT", [128, 4, NT], F32)
            ckvT = p.sbuf(st, "ckvT", [128, 2, NT], F32)
            krT = p.sbuf(st, "krT", [128, NT], F32)
            krsT = p.sbuf(st, "krsT", [128, NT], F32)
            allb = list(range(5))
            for c in range(4):
                p.dma('sp', cqT.sub(*allb)[:, c, :], dv(cT[c], 'cT_in'), 's_q%d' % (c % 2))
            for c in range(2):
                p.dma('sp', ckvT.sub(*allb)[:, c, :], dv(cT[4 + c], 'cT_in'), 's_r%d' % c)
            p.dma('sp', krT.sub(*allb)[:, :], dv(cT[6], 'cT_in'), 's_k0')
            p.dma('sp', krsT.sub(*allb)[:, :], dv(cT[7], 'cT_in'), 's_k1')
            cqn = p.sbuf(st, "cqn", [128, 4, NT], BF16)
            cln = p.sbuf(st, "cln", [128, 2, NT], BF16)
            qn = p.sbuf(st, "qn", [128, 4], F32)
            kvn = p.sbuf(st, "kvn", [128, 2], F32)
            cosT = p.sbuf(st, "cosT", [128, T], F32)
            ssT = p.sbuf(st, "ssT", [128, T], F32)
            p.dma('sp', qn[:, :], dv(qn_in, 'cin'), 's_a0')
            p.dma('sp', kvn[:, :], dv(kvn_in, 'cin'), 's_a1')
            p.dma('sp', cosT[:, :], dv(cosT_in, 'cin'), 's_a2')
            p.dma('sp', ssT[:, :], dv(ssT_in, 'cin'), 's_b0')
            sq = [p.sbuf(st, "sq", [128, 512], F32) for i in range(2)]
            rsb = [p.sbuf(st, "rsb", [128, 512], F32) for i in range(2)]
            psn = [p.psum(st, "psn", [128, 512]) for i in range(2)]
            k = 0
            for (srcT, nch, dstT, gvec) in ((cqT, 4, cqn, qn), (ckvT, 2, cln, kvn)):
                for bi, (t0, n) in enumerate(TB_ALL):
                    pp = psn[k % 2]
                    r = rsb[k % 2]
                    for c in range(nch):
                        s = sq[(k * 4 + c) % 2]
                        p.act(s[:, 0:n], srcT.sub(bi)[:, c, t0:t0 + n], AF.Square)
                        p.matmul(pp[:, 0:n], ones32[:, :], s[:, 0:n], start=(c == 0), stop=(c == nch - 1))
                    p.ts('dve', r[:, 0:n], pp[:, 0:n], 1.0 / (128 * nch), EPS, ALU.mult, ALU.add)
                    p.act(r[:, 0:n], r[:, 0:n], AF.Sqrt)
                    p.recip(r[:, 0:n], r[:, 0:n])
                    for c in range(nch):
                        p.stt(dstT.sub(bi)[:, c, t0:t0 + n], srcT.sub(bi)[:, c, t0:t0 + n], gvec[:, c:c + 1], r[:, 0:n],
                              ALU.mult, ALU.mult)
                    k += 1
            KR = p.sbuf(st, "KR", [128, NT], BF16)
            ta = p.sbuf(st, "ta", [128, T], F32)
            tb = p.sbuf(st, "tb", [128, T], F32)
            allb = list(range(5))
            p.copy('act', KR[:, 0:TC], krT.sub(*allb)[:, 0:TC])
            p.tt('dve', ta[:, :], krT.sub(*allb)[:, TC:NT], cosT[:, :], ALU.mult)
            p.tt('pool', tb[:, :], krsT.sub(*allb)[:, TC:NT], ssT[:, :], ALU.mult)
            p.tt('dve', KR[:, TC:NT], ta[:, :], tb[:, :], ALU.add)
            p.dma('sp', dv(kT[8], 'kT', 8), KR[:, :], 's_b1')
            qstg = [p.sbuf(st, "qstg", [128, 512], BF16) for i in range(4)]
            fstg = [p.sbuf(st, "fstg", [128, 512], F32) for i in range(2)]
            tmpA = p.sbuf(st, "tmpA", [128, T], F32)
            qc = [0]

            def epi_q(chunk, bi, t0, n, pp):
                k = qc[0] % 4
                qc[0] += 1
                s = qstg[k]
                if chunk < 8:
                    p.copy('act', s[:, 0:n], pp[:, 0:n])
                    p.dma('sp', dv(qT[chunk][:, t0:t0 + n], 'qT', chunk, bi), s[:, 0:n], 's_g%d' % k)
                    return
                j = (chunk - 8) // 2
                if (chunk - 8) % 2 == 0:
                    if bi == 0:
                        p.copy('act', s[:, 0:n], pp[:, 0:n])
                        p.dma('sp', dv(qT[8 + j][:, t0:t0 + n], 'qT', 8 + j, bi), s[:, 0:n], 's_g%d' % k)
                    else:
                        p.tt('dve', tmpA.sub(bi)[:, t0 - TC:t0 - TC + n], pp[:, 0:n], cosT[:, t0 - TC:t0 - TC + n], ALU.mult)
                else:
                    f = fstg[k % 2]
                    p.tt('dve', f[:, 0:n], pp[:, 0:n], ssT[:, t0 - TC:t0 - TC + n], ALU.mult)
                    p.tt('pool', s[:, 0:n], f[:, 0:n], tmpA.sub(bi)[:, t0 - TC:t0 - TC + n], ALU.add)
                    p.dma('sp', dv(qT[8 + j][:, t0:t0 + n], 'qT', 8 + j, bi), s[:, 0:n], 's_g%d' % k)

            def skip_q(chunk, bi):
                return chunk >= 8 and (chunk - 8) % 2 == 1 and bi == 0

            def src_q(kc, t0, n):
                return V(cqn.t[:, kc, t0:t0 + n], [(cqn.name, bi) for bi, (a, b) in enumerate(TB_ALL) if a == t0])

            lps = [p.psum(st, "lps", [128, 512]) for i in range(4)]
            linear_fm(st, src_q, 4, w_uq, 2048, TB_ALL, epi_q, skip=skip_q, gw=512, tag="u", ps=lps)

            def epi_k(chunk, bi, t0, n, pp):
                k = qc[0] % 4
                qc[0] += 1
                s = qstg[k]
                p.copy('act' if k % 2 == 0 else 'dve', s[:, 0:n], pp[:, 0:n])
                p.dma('sp', dv(kT[chunk][:, t0:t0 + n], 'kT', chunk, bi), s[:, 0:n], 's_g%d' % k)

            def src_k(kc, t0, n):
                return V(cln.t[:, kc, t0:t0 + n], [(cln.name, bi) for bi, (a, b) in enumerate(TB_ALL) if a == t0])

            linear_fm(st, src_k, 2, w_ukv_k, 1024, TB_ALL, epi_k, gw=512, tag="v", ps=lps)
            wv = p.sbuf(st, "wv", [128, 2, 1024], BF16)
            p.dma('pool', wv[:, :, :], dv(w_ukv_v.rearrange("(kc p) n -> p kc n", p=128), 'cin'), 's_c1')
            vst = [p.sbuf(st, "vst", [128, 1024], BF16) for i in range(2)]
            psv = [p.psum(st, "psv", [128, 512]) for i in range(2)]
            for tt in range(18):
                bi = 0 if tt < 2 else 1 + (tt - 2) // 4
                vs = vst[tt % 2]
                for cb in range(2):
                    pp = psv[cb]
                    for kc in range(2):
                        p.matmul(pp[:, :], cln.sub(bi)[:, kc, tt * 128:(tt + 1) * 128], wv[:, kc, cb * 512:(cb + 1) * 512],
                                 start=(kc == 0), stop=(kc == 1))
                    p.copy('act' if cb == 0 else 'dve', vs[:, cb * 512:(cb + 1) * 512], pp[:, :])
                p.dma('sp', dv(vtok[tt * 128:(tt + 1) * 128, 0:1024], 'vtok', tt), vs[:, :], 's_x%d' % (tt % 2))
            p.run_phase()

    def gen_mla(st):
        if True:
            KR = p.sbuf(st, "KR", [128, NT], BF16)
            p.dma('sp', KR[:, :], dv(kT[8], 'kT8'), 's_a0')
            qh = [p.sbuf(st, "qh", [128, NT], BF16) for i in range(2)]
            qr = [p.sbuf(st, "qr", [128, NT], BF16) for i in range(2)]
            kh = [p.sbuf(st, "kh", [128, NT], BF16) for i in range(2)]
            vh = [p.sbuf(st, "vh", [128, 18, 128], BF16) for i in range(2)]
            oh = [p.sbuf(st, "oh", [128, NT], BF16) for i in range(2)]
            pT = [p.sbuf(st, "pT", [128, 512], BF16) for i in range(4)]
            rc = [p.sbuf(st, "rc", [128, 512], F32) for i in range(2)]
            psS = [p.psum(st, "psS", [128, 512]) for i in range(4)]
            psO = [p.psum(st, "psO", [128, 512]) for i in range(2)]
            psZ = [p.psum(st, "psZ", [128, 512]) for i in range(2)]
            def hload(h):
                b = h % 2
                p.dma('sp', qh[b][:, :], dv(qT[h], 'qTh'), 's_q%d' % b)
                p.dma('sp', qr[b][:, :], dv(qT[8 + h // 2], 'qTr'), 's_r%d' % b)
                p.dma('sp', kh[b][:, :], dv(kT[h], 'kTh'), 's_k%d' % b)
                p.dma('sp', vh[b][:, :, :], dv(vtok[:, h * 128:(h + 1) * 128].rearrange("(t p) c -> p t c", p=128), 'vh'),
                      's_v%d' % b)

            steps = []
            for h in range(8):
                for qi, (q0, qn_) in enumerate(TB_ALL):
                    kts = [0, 1] if qi == 0 else list(range(18))
                    for ki, kt in enumerate(kts):
                        steps.append((h, qi, q0, qn_, ki, kt, len(kts)))

            def qk(si):
                h, qi, q0, qn_, ki, kt, nk = steps[si]
                b = h % 2
                pb = (h % 2) * 64
                pS = psS[si % 4]
                p.matmul(pS[:, 0:qn_], kh[b][:, kt * 128:(kt + 1) * 128], qh[b][:, q0:q0 + qn_], start=True, stop=False)
                p.matmul(pS[:, 0:qn_], KR[pb:pb + 64, kt * 128:(kt + 1) * 128], qr[b][pb:pb + 64, q0:q0 + qn_],
                         start=False, stop=True)

            hload(0)
            nq = 0
            qk(0)
            qk(1)
            for si in range(len(steps)):
                h, qi, q0, qn_, ki, kt, nk = steps[si]
                b = h % 2
                if qi == 0 and ki == 0 and h + 1 < 8:
                    hload(h + 1)
                if si + 2 < len(steps):
                    qk(si + 2)
                pS = psS[si % 4]
                pt = pT[si % 4]
                po = psO[nq % 2]
                pz = psZ[nq % 2]
                p.act(pt[:, 0:qn_], pS[:, 0:qn_], AF.Exp, scale=MLA_SCALE)
                p.matmul(po[:, 0:qn_], vh[b][:, kt, :], pt[:, 0:qn_], start=(ki == 0), stop=(ki == nk - 1))
                p.matmul(pz[:, 0:qn_], ones16[:, :], pt[:, 0:qn_], start=(ki == 0), stop=(ki == nk - 1))
                if ki == nk - 1:
                    r = rc[nq % 2]
                    nq += 1
                    p.recip(r[:, 0:qn_], pz[:, 0:qn_])
                    p.tt('dve', oh[b].sub(qi)[:, q0:q0 + qn_], po[:, 0:qn_], r[:, 0:qn_], ALU.mult)
                    if qi == len(TB_ALL) - 1:
                        p.dma('sp', dv(mixT[h], 'mixT', h), oh[b].sub(0, 1, 2, 3, 4)[:, :], 's_o%d' % b)
                yield

    def gen_lru(st):
        W = NT + 6
        CT0 = 2
        LT0 = 261
        if True:
            gw = p.sbuf(st, "gw", [128, 32, 128], F32)
            for g, wsrc in enumerate((ga_w, gx_w)):
                for d in range(2):
                    p.dma('sp', gw[:, (g * 2 + d) * 8:(g * 2 + d) * 8 + 8, :], dv(wsrc[d].rearrange("n k e -> k n e"), 'cin'),
                          's_e%d' % (g * 2 + d))
            lcw = p.sbuf(st, "lcw", [128, 32], F32)
            lcb = p.sbuf(st, "lcb", [128, 8], F32)
            gab = p.sbuf(st, "gab", [128, 16], F32)
            gxb = p.sbuf(st, "gxb", [128, 16], F32)
            lam = p.sbuf(st, "lam", [128, 16], F32)
            p.dma('sp', lcw[:, :], dv(lcw_in, 'cin'), 's_b0')
            p.dma('sp', lcb[:, :], dv(lcb_in, 'cin'), 's_b1')
            p.dma('sp', gab[:, :], dv(gab_in, 'cin'), 's_b2')
            p.dma('sp', gxb[:, :], dv(gxb_in, 'cin'), 's_b3')
            p.dma('sp', lam[:, :], dv(lam_in, 'cin'), 's_b4')
            z = p.sbuf(st, "z", [128, 16], F32)
            y = p.sbuf(st, "y", [128, 16], F32)
            y2 = p.sbuf(st, "y2", [128, 16], F32)
            acc = p.sbuf(st, "acc", [128, 16], F32)
            c1 = p.sbuf(st, "c1", [128, 16], F32)
            c2 = p.sbuf(st, "c2", [128, 16], F32)
            p.ts('dve', z[:, :], lam[:, :], -1.0, None, ALU.mult)
            p.tt('dve', z[:, :], z[:, :], lam[:, :], ALU.max)
            p.act(z[:, :], z[:, :], AF.Exp, scale=-1.0)
            p.ts('dve', y[:, :], z[:, :], 2.0, None, ALU.add)
            p.recip(y[:, :], y[:, :])
            p.tt('dve', y[:, :], y[:, :], z[:, :], ALU.mult)
            p.tt('dve', y2[:, :], y[:, :], y[:, :], ALU.mult)
            p.ts('dve', acc[:, :], y2[:, :], 1.0 / 13, 1.0 / 11, ALU.mult, ALU.add)
            for cf in (1.0 / 9, 1.0 / 7, 1.0 / 5, 1.0 / 3, 1.0):
                p.tt('dve', acc[:, :], acc[:, :], y2[:, :], ALU.mult)
                p.ts('dve', acc[:, :], acc[:, :], cf, None, ALU.add)
            p.tt('dve', acc[:, :], acc[:, :], y[:, :], ALU.mult)
            p.ts('dve', z[:, :], lam[:, :], -1.0, 0.0, ALU.mult, ALU.max)
            p.stt(acc[:, :], acc[:, :], 2.0, z[:, :], ALU.mult, ALU.add)
            p.ts('dve', c1[:, :], acc[:, :], -8.0, None, ALU.mult)
            p.ts('dve', c2[:, :], acc[:, :], -16.0, None, ALU.mult)

            xp = [p.sbuf(st, "xp", [128, W], F32) for i in range(2)]
            ry = [p.sbuf(st, "ry", [128, NT], F32) for i in range(2)]
            xc = [p.sbuf(st, "xc", [128, W], F32) for i in range(2)]
            rr = [p.sbuf(st, "rr", [128, W], F32) for i in range(2)]
            ii = [p.sbuf(st, "ii", [128, W], F32) for i in range(2)]
            bb = [p.sbuf(st, "bb", [128, W], F32) for i in range(2)]
            hf = [p.sbuf(st, "hf", [128, W], F32) for i in range(2)]
            hbk = [p.sbuf(st, "hbk", [128, W], F32) for i in range(2)]
            lo = [p.sbuf(st, "lo", [128, NT], BF16) for i in range(2)]
            NPG = 6
            psg = [p.psum(st, "psg", [128, 512]) for i in range(NPG)]
            for i in range(2):
                p.memset('pool', xp[i][:, :], 0.0)
            blks = []
            c = CT0
            while c < W - 1:
                n = min(512, W - 1 - c)
                blks.append((c, n))
                c += n
            ng = [0]
            L = W - 3
            S = slice(2, W - 1)

            def S0(n):
                x = xp[n % 2]
                p.dma('sp', x[:, CT0:CT0 + TC], dv(rxT[n][:, 0:TC], 'rxl'), 's_x%d' % (n % 2))
                p.dma('sp', x[:, LT0:LT0 + T], dv(rxT[n][:, TC:NT], 'rxl'), 's_y%d' % (n % 2))
                p.dma('sp', ry[n % 2][:, :], dv(ryT[n], 'ryl'), 's_z%d' % (n % 2))
                xcn = xc[n % 2]
                p.act(xcn[:, 2:2 + L], x[:, 0:L], AF.Identity, bias=lcb[:, n:n + 1], scale=lcw[:, n * 4:n * 4 + 1])
                for tap in range(1, 4):
                    p.stt(xcn[:, 2:2 + L], x[:, tap:tap + L], lcw[:, n * 4 + tap:n * 4 + tap + 1], xcn[:, 2:2 + L], ALU.mult, ALU.add)

            def S1(n, d):
                u = 2 * n + d
                xcn = xc[n % 2]
                r_, i_, b_ = rr[u % 2], ii[u % 2], bb[u % 2]
                for (b0, bn) in blks:
                    pr = psg[ng[0] % NPG]
                    pi = psg[(ng[0] + 1) % NPG]
                    ng[0] += 2
                    p.matmul(pr[:, 0:bn], gw[:, (0 * 2 + d) * 8 + n, :], xcn[:, b0:b0 + bn])
                    p.matmul(pi[:, 0:bn], gw[:, (1 * 2 + d) * 8 + n, :], xcn[:, b0:b0 + bn])
                    p.act(r_[:, b0:b0 + bn], pr[:, 0:bn], AF.Sigmoid, bias=gab[:, d * 8 + n:d * 8 + n + 1])
                    p.act(i_[:, b0:b0 + bn], pi[:, 0:bn], AF.Sigmoid, bias=gxb[:, d * 8 + n:d * 8 + n + 1])
                p.act(b_[:, S], r_[:, S], AF.Exp, scale=c2[:, d * 8 + n:d * 8 + n + 1])
                p.act(r_[:, S], r_[:, S], AF.Exp, scale=c1[:, d * 8 + n:d * 8 + n + 1])
                p.act(b_[:, S], b_[:, S], AF.Sqrt, bias=1.0, scale=-1.0)
                p.tt('pool', i_[:, S], i_[:, S], xcn[:, S], ALU.mult)
                p.tt('dve', b_[:, S], b_[:, S], i_[:, S], ALU.mult)
                if d == 0:
                    h_ = hf[n % 2]
                    p.scan(h_[:, CT0:CT0 + TC], r_[:, CT0:CT0 + TC], b_[:, CT0:CT0 + TC], 0.0)
                    p.scan(h_[:, LT0:LT0 + T], r_[:, LT0:LT0 + T], b_[:, LT0:LT0 + T], h_[:, CT0 + TC - 1:CT0 + TC])
                else:
                    h_ = hbk[n % 2]
                    p.scan(h_[:, CT0 + TC - 1:CT0 - 1:-1], r_[:, CT0 + TC - 1:CT0 - 1:-1], b_[:, CT0 + TC - 1:CT0 - 1:-1], 0.0)
                    p.scan(h_[:, LT0 + T - 1:LT0 - 1:-1], r_[:, LT0 + T - 1:LT0 - 1:-1], b_[:, LT0 + T - 1:LT0 - 1:-1],
                           h_[:, CT0:CT0 + 1])

            def S2(n):
                h_, hb_ = hf[n % 2], hbk[n % 2]
                yv = ry[n % 2]
                p.tt('pool', h_[:, CT0:CT0 + TC], h_[:, CT0:CT0 + TC], hb_[:, CT0:CT0 + TC], ALU.add)
                p.tt('pool', h_[:, LT0:LT0 + T], h_[:, LT0:LT0 + T], hb_[:, LT0:LT0 + T], ALU.add)
                p.act(yv[:, :], yv[:, :], AF.Gelu_apprx_tanh)
                o = lo[n % 2]
                p.tt('dve', o[:, 0:TC], h_[:, CT0:CT0 + TC], yv[:, 0:TC], ALU.mult)
                p.tt('dve', o[:, TC:NT], h_[:, LT0:LT0 + T], yv[:, TC:NT], ALU.mult)
                p.dma('sp', dv(mixT[8 + n], 'mixT', 8 + n), o[:, :], 's_p%d' % (n % 2))

            gmod = gen_mod(st, 0, 256, 16, 48)

            def modstep(k):
                for _ in range(k):
                    try:
                        next(gmod)
                    except StopIteration:
                        pass

            S0(0)
            for n in range(8):
                S1(n, 0)
                modstep(2)
                if n + 1 < 8:
                    S0(n + 1)
                S1(n, 1)
                modstep(2)
                S2(n)
                yield
            modstep(64)

    def phase_mla():
        with ExitStack() as st:
            for _ in gen_mla(st):
                pass
            p.run_phase()

    def phase_lru():
        with ExitStack() as st:
            for _ in gen_lru(st):
                pass
            p.run_phase()

    def phase_mla_lru():
        with ExitStack() as st:
            ga = gen_mla(st)
            gb_ = gen_lru(st)
            alive_a = alive_b = True
            while alive_a or alive_b:
                for _ in range(5):
                    if alive_a:
                        try:
                            next(ga)
                        except StopIteration:
                            alive_a = False
                for _ in range(2):
                    if alive_b:
                        try:
                            next(gb_)
                        except StopIteration:
                            alive_b = False
            p.run_phase()

    def phase_outproj(Wout, layer, tiles, stage):
        with ExitStack() as st:
            wo = p.sbuf(st, "wo", [128, KC, D], BF16)
            wsrc = Wout.rearrange("(kc p) n -> p kc n", p=128)
            for q in range(4):
                p.dma('pool', wo.sub(q)[:, :, q * 512:(q + 1) * 512], dv(wsrc[:, :, q * 512:(q + 1) * 512], 'cin'), 's_w%d' % q)
            streams = sorted(set(1 if tt < 2 else 0 for tt in tiles))
            gb = {}
            for s in streams:
                gb[s] = p.sbuf(st, "gb", [128, D], F32)
                p.dma('sp', gb[s][:, :], dv(bcast_row(modrow(layer, s, 2), D), 'modv', layer), 's_a%d' % s)
            mt = [p.sbuf(st, "mt", [128, KC, 128], BF16) for i in range(3)]
            xt = [p.sbuf(st, "xt", [128, D], F32) for i in range(2)]
            tm = [p.sbuf(st, "tm", [128, 512], F32) for i in range(2)]
            ps = [p.psum(st, "ops", [128, 512]) for i in range(4)]
            k = 0
            for n, tt in enumerate(tiles):
                s = 1 if tt < 2 else 0
                m = mt[n % 3]
                p.dma('sp', m[:, :, :], dv(mixT[:, :, tt * 128:(tt + 1) * 128].rearrange("c p t -> p c t"), 'mixT_in'),
                      's_m%d' % (n % 3))
                x = xt[n % 2]
                p.dma('sp', x[:, :], xsrc(tt, stage), 's_x%d' % (n % 2))
                for cb in range(4):
                    pp = ps[k % 4]
                    t = tm[k % 2]
                    k += 1
                    for kc in range(KC):
                        p.matmul(pp[:, :], m[:, kc, :], wo.sub(cb)[:, kc, cb * 512:(cb + 1) * 512], start=(kc == 0), stop=(kc == KC - 1))
                    p.tt('dve', t[:, :], pp[:, :], gb[s][:, cb * 512:(cb + 1) * 512], ALU.mult)
                    p.tt('pool', x[:, cb * 512:(cb + 1) * 512], t[:, :], x[:, cb * 512:(cb + 1) * 512], ALU.add)
                p.dma('act', dv(xres[tt * 128:(tt + 1) * 128, :], 'xres', tt), x[:, :], 's_y%d' % (n % 2))
            p.run_phase()

    def phase_ffn(layer, tiles):
        lat_only = tiles[0] >= 2
        tokblocks = TB_LAT if lat_only else TB_ALL
        st_h = ExitStack()
        hT = p.sbuf(st_h, "hT", [128, KC, NT], BF16)
        st_w = ExitStack()
        wg = [p.sbuf(st_w, "wg", [128, KC, 256], BF16) for i in range(2)]
        wu = [p.sbuf(st_w, "wu", [128, KC, 256], BF16) for i in range(2)]
        wsrc = ffn_w_up[layer].rearrange("(kc p) n -> p kc n", p=128)

        def wload(g_):
            p.dma('pool', wg[g_ % 2][:, :, :], dv(wsrc[:, :, g_ * 256:(g_ + 1) * 256], 'cin'), 's_w%d' % (g_ % 2))
            p.dma('pool', wu[g_ % 2][:, :, :], dv(wsrc[:, :, HID + g_ * 256:HID + (g_ + 1) * 256], 'cin'), 's_u%d' % (g_ % 2))

        phase_norm(hT, tiles, 1, layer, 1, pre=lambda: wload(0))
        W = NT + 4
        CT0 = 1
        LT0 = 259

        def col(t0):
            return CT0 + t0 if t0 < TC else LT0 + (t0 - TC)

        with ExitStack() as st:
            fcw = p.sbuf(st, "fcw", [128, NHC * 3], F32)
            fcb = p.sbuf(st, "fcb", [128, NHC], F32)
            p.dma('sp', fcw[:, :], dv(fcw_in[layer], 'cin'), 's_c0')
            p.dma('sp', fcb[:, :], dv(fcb_in[layer], 'cin'), 's_c1')
            gbuf = [p.sbuf(st, "gbuf", [128, W], F32) for i in range(2)]
            ubuf = [p.sbuf(st, "ubuf", [128, W], F32) for i in range(2)]
            accb = [p.sbuf(st, "accb", [128, W], F32) for i in range(2)]
            mb = [p.sbuf(st, "mb", [128, W], BF16) for i in range(2)]
            ps = [p.psum(st, "fps", [128, 512]) for i in range(6)]
            NPS = 6
            for i in range(2):
                p.memset('pool', gbuf[i].sub(*range(len(tokblocks)))[:, :], 0.0)
                p.memset('pool', ubuf[i].sub(*range(len(tokblocks)))[:, :], 0.0)
            gmod = gen_mod(st, 1, 256) if layer == 0 else None
            k = 0
            c_lo = LT0 if lat_only else CT0
            Lc = (W - 1) - c_lo
            for c in range(NHC):
                gi = c // 2

                if c % 2 == 0 and gi + 1 < NHC // 2:
                    wload(gi + 1)
                cj = c % 2
                g = gbuf[c % 2]
                u = ubuf[c % 2]
                a = accb[c % 2]
                m = mb[c % 2]
                for bi, (t0, n) in enumerate(tokblocks):
                    pg = ps[k % 6]
                    pu = ps[(k + 1) % 6]
                    k += 2
                    for kc in range(KC):
                        p.matmul(pg[:, 0:n], wg[gi % 2][:, kc, cj * 128:(cj + 1) * 128], hview(hT, kc, t0, n), start=(kc == 0), stop=(kc == KC - 1))
                    p.copy('act', g.sub(bi)[:, col(t0):col(t0) + n], pg[:, 0:n])
                    for kc in range(KC):
                        p.matmul(pu[:, 0:n], wu[gi % 2][:, kc, cj * 128:(cj + 1) * 128], hview(hT, kc, t0, n), start=(kc == 0), stop=(kc == KC - 1))
                    p.copy('dve', u.sub(bi)[:, col(t0):col(t0) + n], pu[:, 0:n])
                allb = list(range(len(tokblocks)))
                gA = g.sub(*allb)
                uA = u.sub(*allb)
                p.act(a[:, c_lo:c_lo + Lc], gA[:, c_lo - 1:c_lo - 1 + Lc], AF.Identity, bias=fcb[:, c:c + 1], scale=fcw[:, 3 * c:3 * c + 1])
                p.stt(a[:, c_lo:c_lo + Lc], gA[:, c_lo:c_lo + Lc], fcw[:, 3 * c + 1:3 * c + 2], a[:, c_lo:c_lo + Lc], ALU.mult, ALU.add)
                p.stt(a[:, c_lo:c_lo + Lc], gA[:, c_lo + 1:c_lo + 1 + Lc], fcw[:, 3 * c + 2:3 * c + 3], a[:, c_lo:c_lo + Lc], ALU.mult, ALU.add)
                p.act(a[:, c_lo:c_lo + Lc], a[:, c_lo:c_lo + Lc], AF.Gelu_apprx_tanh)
                p.tt('dve', m[:, c_lo:c_lo + Lc], a[:, c_lo:c_lo + Lc], uA[:, c_lo:c_lo + Lc], ALU.mult)
                if not lat_only:
                    p.dma('sp', dv(mT[0:2, :, c, :].rearrange("a p t -> p a t"), 'mT', c, 0),
                          V(m.t[:, CT0:CT0 + TC].rearrange("p (a t) -> p a t", t=128), m.keys), 's_m%d' % (c % 2))
                p.dma('sp', dv(mT[2:18, :, c, :].rearrange("a p t -> p a t"), 'mT', c, 1),
                      V(m.t[:, LT0:LT0 + T].rearrange("p (a t) -> p a t", t=128), m.keys), 's_n%d' % (c % 2))
                if gmod is not None:
                    for _ in range(2 if c < 4 else 1):
                        try:
                            next(gmod)
                        except StopIteration:
                            gmod = None
                            break
            p.run_phase()
        st_w.close()
        st_h.close()
        with ExitStack() as st:
            wd = [p.sbuf(st, "wd", [128, NHC, 512], BF16) for i in range(2)]
            streams = sorted(set(1 if tt < 2 else 0 for tt in tiles))
            gb = {}
            for s in streams:
                gb[s] = p.sbuf(st, "gb", [128, D], F32)
                p.dma('sp', gb[s][:, :], dv(bcast_row(modrow(layer, s, 5), D), 'modv', layer), 's_a%d' % s)
            mt = [p.sbuf(st, "mt", [128, NHC, 128], BF16) for i in range(3)]
            xt = [p.sbuf(st, "xt", [128, 512], F32) for i in range(3)]
            tm = [p.sbuf(st, "tm", [128, 512], F32) for i in range(2)]
            ps = [p.psum(st, "dps", [128, 512]) for i in range(4)]
            wsrc = ffn_w_down[layer].rearrange("(kc p) n -> p kc n", p=128)
            k = 0
            def wdload(cb_):
                for hf_ in range(4):
                    p.dma('pool', wd[cb_ % 2].sub(hf_)[:, hf_ * 11:(hf_ + 1) * 11, :],
                          dv(wsrc[:, hf_ * 11:(hf_ + 1) * 11, cb_ * 512:(cb_ + 1) * 512], 'cin'), 's_w%d' % ((cb_ % 2) * 4 + hf_))

            wdload(0)
            for cb in range(4):
                w = wd[cb % 2]
                if cb + 1 < 4:
                    wdload(cb + 1)
                for tt in tiles:
                    s = 1 if tt < 2 else 0
                    m = mt[k % 3]
                    x = xt[k % 3]
                    t = tm[k % 2]
                    pp = ps[k % 4]
                    p.dma('sp', m[:, :, :], dv(mT[tt], 'mT_in'), 's_m%d' % (k % 3))
                    p.dma('sp', x[:, :], dv(xres[tt * 128:(tt + 1) * 128, cb * 512:(cb + 1) * 512], 'xres', tt, cb), 's_x%d' % (k % 3))
                    for c in range(NHC):
                        p.matmul(pp[:, :], m[:, c, :], w.sub(c // 11)[:, c, :], start=(c == 0), stop=(c == NHC - 1))
                    p.tt('dve', t[:, :], pp[:, :], gb[s][:, cb * 512:(cb + 1) * 512], ALU.mult)
                    p.tt('dve', x[:, :], t[:, :], x[:, :], ALU.add)
                    p.dma('act', dv(xres[tt * 128:(tt + 1) * 128, cb * 512:(cb + 1) * 512], 'xres', tt, cb), x[:, :], 's_y%d' % (k % 3))
                    k += 1
            p.run_phase()

    def phase_l1_proj():
        st_h = ExitStack()
        hT = p.sbuf(st_h, "hT", [128, KC, NT], BF16)

        st_w = ExitStack()
        wbp = [p.sbuf(st_w, "wl", [128, KC, 256], BF16) for i in range(3)]
        wsrc_p = na_w_qkv[:, 0:2 * D].rearrange("(kc p) n -> p kc n", p=128)

        def _pre():
            for gi in range(2):
                p.dma('pool', wbp[gi][:, :, :], dv(wsrc_p[:, :, gi * 256:(gi + 1) * 256], 'cin'), 's_w%d' % gi)
        phase_norm(hT, list(range(18)), 1, 1, 0, pre=_pre)
        with ExitStack() as st:
            qstg = [p.sbuf(st, "qstg", [128, 512], BF16) for i in range(4)]
            qc = [0]

            def epi(chunk, bi, t0, n, pp):
                k = qc[0] % 4
                qc[0] += 1
                s = qstg[k]
                p.copy('act' if k % 2 == 0 else 'dve', s[:, 0:n], pp[:, 0:n])
                if chunk < 16:
                    p.dma('sp', dv(qT[chunk][:, t0:t0 + n], 'qT', chunk, bi), s[:, 0:n], 's_g%d' % k)
                else:
                    p.dma('sp', dv(kT[chunk - 16][:, t0:t0 + n], 'kT', chunk, bi), s[:, 0:n], 's_g%d' % k)

            def skip(chunk, bi):
                return chunk < 16 and bi == 0

            linear_fm(st, lambda kc, t0, n: hview(hT, kc, t0, n), KC, na_w_qkv[:, 0:2 * D], 2 * D, TB_ALL, epi, skip=skip, gw=256, wb=wbp, preloaded=2)
            wv = p.sbuf(st, "wv", [128, KC, D], BF16)
            wsrc = na_w_qkv[:, 2 * D:3 * D].rearrange("(kc p) n -> p kc n", p=128)
            for q in range(4):
                p.dma('pool', wv.sub(q)[:, :, q * 512:(q + 1) * 512], dv(wsrc[:, :, q * 512:(q + 1) * 512], 'cin'), 's_v%d' % q)
            vst = [p.sbuf(st, "vst", [128, D], BF16) for i in range(2)]
            psv = [p.psum(st, "psv", [128, 512]) for i in range(4)]
            for tt in range(18):
                vs = vst[tt % 2]
                for cb in range(4):
                    pp = psv[cb]
                    for kc in range(KC):
                        p.matmul(pp[:, :], hT.sub(tt)[:, kc, tt * 128:(tt + 1) * 128], wv.sub(cb)[:, kc, cb * 512:(cb + 1) * 512],
                                 start=(kc == 0), stop=(kc == KC - 1))
                    p.copy('act' if cb % 2 == 0 else 'dve', vs[:, cb * 512:(cb + 1) * 512], pp[:, :])
                p.dma('sp', dv(vtok[tt * 128:(tt + 1) * 128, :], 'vtok', tt), vs[:, :], 's_x%d' % (tt % 2))
            p.run_phase()
        st_w.close()
        st_h.close()

    def phase_na_attn():
        CM_MASK = NCMB - 1
        with ExitStack() as st:
            qh = [p.sbuf(st, "qh", [128, NT], BF16) for i in range(2)]
            kh = [p.sbuf(st, "kh", [128, NT], BF16) for i in range(2)]
            vh = [p.sbuf(st, "vh", [128, 18, 128], BF16) for i in range(2)]
            bf = [p.sbuf(st, "bf", [128, NCMB * 128], F32) for i in range(2)]
            bq = [p.sbuf(st, "bq", [128, NCMB * 128], BF16) for i in range(2)]
            oh = [p.sbuf(st, "oh", [128, NT], BF16) for i in range(2)]
            pT = [p.sbuf(st, "pT", [128, 4, 256], BF16) for i in range(4)]
            rc = [p.sbuf(st, "rc", [128, 256], F32) for i in range(2)]
            psS = [p.psum(st, "psS", [128, 2, 256]) for i in range(4)]
            psO = [p.psum(st, "psO", [128, 512]) for i in range(2)]
            psZ = [p.psum(st, "psZ", [128, 512]) for i in range(2)]

            def hload(h):
                b = h % 2
                p.dma('sp', qh[b][:, TC:NT], dv(qT[h][:, TC:NT], 'qTh'), 's_q%d' % b)
                p.dma('sp', kh[b][:, :], dv(kT[h], 'kTh'), 's_k%d' % b)
                p.dma('sp', vh[b][:, :, :], dv(vtok[:, h * 128:(h + 1) * 128].rearrange("(t p) c -> p t c", p=128), 'vh'),
                      's_v%d' % b)
                p.dma('sp', bf[b][:, :], dv(na_bias[h], 'cin'), 's_b%d' % b)

            def bqprep(h):
                b = h % 2
                p.act(bq[b][:, :], bf[b][:, :], AF.Copy, scale=1.0 / NA_SCALE)

            steps = []
            for h in range(16):
                for qp in range(8):
                    dA = dict(na_pairs[4 * qp])
                    dB = dict(na_pairs[4 * qp + 2])
                    items = [(0, None, None), (1, None, None)]
                    for R in sorted(set(dA) | set(dB)):
                        items.append((2 + R // 2, dA.get(R, CM_MASK), dB.get(R, CM_MASK)))
                    assert len(items) <= 8
                    for half in range(2):
                        steps.append((h, qp, half, items[half * 4:half * 4 + 4], len(items)))
            NS = len(steps)

            def qk(si):
                h, qp, half, its, ntot = steps[si]
                b = h % 2
                q0 = TC + qp * 256
                for j, (kt, cmA, cmB) in enumerate(its):
                    pS = psS[(si % 2) * 2 + j // 2]
                    jj = j % 2
                    p.matmul(pS[:, jj, :], kh[b][:, kt * 128:(kt + 1) * 128], qh[b][:, q0:q0 + 256], start=True, stop=(cmA is None))
                    if cmA is not None:
                        p.matmul(pS[:, jj, 0:128], bq[b][:, cmA * 128:(cmA + 1) * 128], ident[:, :], start=False, stop=False)
                        p.matmul(pS[:, jj, 128:256], bq[b][:, cmB * 128:(cmB + 1) * 128], ident[:, :], start=False, stop=True)

            def front(si):
                h, qp, half, its, ntot = steps[si]
                b = h % 2
                pt = pT[si % 4]
                po = psO[(si // 2) % 2]
                n_ = len(its)
                for bk in range((n_ + 1) // 2):
                    m_ = min(2, n_ - bk * 2)
                    pS = psS[(si % 2) * 2 + bk]
                    p.act(pt.sub(bk)[:, bk * 2:bk * 2 + m_, :], pS[:, 0:m_, :], AF.Exp, scale=NA_SCALE)
                for j, (kt, cmA, cmB) in enumerate(its):
                    gi = half * 4 + j
                    p.matmul(po[:, 0:256], vh[b][:, kt, :], pt.sub(j // 2)[:, j, :], start=(gi == 0), stop=(gi == ntot - 1))

            def back(si):
                h, qp, half, its, ntot = steps[si]
                b = h % 2
                q0 = TC + qp * 256
                pt = pT[si % 4]
                po = psO[(si // 2) % 2]
                pz = psZ[(si // 2) % 2]
                for j in range(len(its)):
                    gi = half * 4 + j
                    p.matmul(pz[:, 0:256], ones16[:, :], pt.sub(j // 2)[:, j, :], start=(gi == 0), stop=(gi == ntot - 1))
                if half == 1:
                    r = rc[(si // 2) % 2]
                    p.recip(r[:, :], pz[:, 0:256])
                    p.tt('dve', oh[b].sub(qp)[:, q0:q0 + 256], po[:, 0:256], r[:, :], ALU.mult)
                    if qp == 7:
                        p.dma('sp', dv(mixT[h][:, TC:NT], 'mixT', h), oh[b].sub(*range(8))[:, TC:NT], 's_o%d' % b)

            hload(0)
            bqprep(0)
            qk(0)
            for si in range(NS + 1):
                if si < NS:
                    h, qp, half, its, ntot = steps[si]
                    if qp == 0 and half == 0 and h + 1 < 16:
                        hload(h + 1)
                    if qp == 4 and half == 0 and h + 1 < 16:
                        bqprep(h + 1)
                    if si + 1 < NS:
                        qk(si + 1)
                if si >= 1:
                    back(si - 1)
                if si < NS:
                    front(si)
            p.run_phase()

    def phase_final():
        with ExitStack() as st:
            gb = p.sbuf(st, "gb", [128, D], F32)
            p.dma('sp', gb[:, :], dv(bcast_row(final_norm[0:1, :], D), 'cin'), 's_a0')
            NB = 4
            xt = [p.sbuf(st, "xt", [128, D], F32) for i in range(NB)]
            ot = [p.sbuf(st, "ot", [128, D], F32) for i in range(3)]
            junk = p.sbuf(st, "junk", [128, D], BF16)
            stat = p.sbuf(st, "stat", [128, 4 * NB], F32)
            H2 = D // 2

            def stage0(n):
                x = xt[n % NB]
                p.dma('sp', x[:, :], xsrc(n + 2, 1), 's_x%d' % (n % NB))
                ss = stat.sub(n % NB)
                c0 = (n % NB) * 4
                p.act(junk[:, :], x[:, :], AF.Square, accum_out=ss[:, c0:c0 + 1])
                p.ts('dve', ss[:, c0 + 1:c0 + 2], ss[:, c0:c0 + 1], 1.0 / D, EPS, ALU.mult, ALU.add)
                p.act(ss[:, c0 + 2:c0 + 3], ss[:, c0 + 1:c0 + 2], AF.Sqrt)
                p.recip(ss[:, c0 + 3:c0 + 4], ss[:, c0 + 2:c0 + 3])

            def stage1(n):
                x = xt[n % NB]
                o = ot[n % 3]
                ss = stat.sub(n % NB)
                c0 = (n % NB) * 4
                p.stt(o[:, :], x[:, :], ss[:, c0 + 3:c0 + 4], gb[:, :], ALU.mult, ALU.mult)
                p.dma('sp', dv(out[n * 128:(n + 1) * 128, :], 'out', n), o[:, :], 's_y%d' % (n % 3))

            for step in range(17):
                if step < 16:
                    stage0(step)
                if 0 <= step - 1 < 16:
                    stage1(step - 1)
            p.run_phase()

    phases = [
        phase_mod,
        phase_l0_proj,
        None,
        phase_mla,
        phase_lru,
        lambda: phase_outproj(w_out0, 0, list(range(18)), 0),
        lambda: phase_ffn(0, list(range(18))),
        phase_l1_proj,
        phase_na_attn,
        lambda: phase_outproj(na_w_out, 1, list(range(2, 18)), 1),
        lambda: phase_ffn(1, list(range(2, 18))),
        phase_final,
    ]
    for i, ph in enumerate(phases):
        if i > stop_after:
            break
        if ph is not None:
            ph()
    gst.close()
    p.close()
    return nc, p


def _fo_part(v, nchunks):
    return np.ascontiguousarray(np.asarray(v, np.float32).reshape(nchunks, 128).T)


def prep_shared(inp):
    f32 = np.float32
    sh = {}
    sh["ident"] = np.eye(128, dtype=f32)
    sh["mod_w"] = np.ascontiguousarray(inp["mod_w"], f32)
    sh["mod_b"] = np.ascontiguousarray(inp["mod_b"], f32)
    sh["norm_mix"] = np.ascontiguousarray(inp["norm_mix"], f32)
    sh["norm_ffn"] = np.ascontiguousarray(inp["norm_ffn"], f32)
    sh["final_norm"] = np.ascontiguousarray(inp["final_norm"], f32).reshape(1, D)
    f = np.arange(64)
    a_, j_, p_ = f // 32, (f // 16) % 2, f % 16
    partner = a_ * 32 + (1 - j_) * 16 + p_
    w_in = np.asarray(inp["mla_w_in"][0], f32)
    kr = w_in[:, 768:832]
    krs = kr[:, partner]
    sh["w_in"] = np.ascontiguousarray(np.concatenate(
        [w_in[:, 0:768], kr, kr, krs, krs, w_in[:, 832:1856], w_in[:, 1856:2880]], axis=1))
    w_uq = np.asarray(inp["mla_w_uq"][0], f32)
    cols = []
    for h in range(8):
        cols.append(w_uq[:, h * 192:h * 192 + 128])
    for j in range(4):
        r0 = w_uq[:, (2 * j) * 192 + 128:(2 * j) * 192 + 192]
        r1 = w_uq[:, (2 * j + 1) * 192 + 128:(2 * j + 1) * 192 + 192]
        cols += [r0, r1, r0[:, partner], r1[:, partner]]
    sh["w_uq"] = np.ascontiguousarray(np.concatenate(cols, axis=1))
    sh["qn"] = _fo_part(inp["mla_q_norm"][0], 4)
    sh["kvn"] = _fo_part(inp["mla_kv_norm"][0], 2)
    w_ukv = np.asarray(inp["mla_w_ukv"][0], f32).reshape(256, 8, 256)
    sh["w_ukv_k"] = np.ascontiguousarray(w_ukv[:, :, 0:128].reshape(256, 1024))
    sh["w_ukv_v"] = np.ascontiguousarray(w_ukv[:, :, 128:256].reshape(256, 1024))
    pos = np.arange(T)
    inv = (10000.0 ** (-np.arange(16, dtype=np.float32) / 16)).astype(f32)
    ar = (pos // 64).astype(f32)[:, None] * inv
    ac = (pos % 64).astype(f32)[:, None] * inv
    ang = np.concatenate([ar, ar, ac, ac], axis=-1)
    cos = np.cos(ang).astype(f32)
    sin = np.sin(ang).astype(f32)
    sgn = np.where(j_ == 0, -1.0, 1.0).astype(f32)
    ss = sin * sgn[None, :]
    sh["cosT"] = np.ascontiguousarray(np.concatenate([cos.T, cos.T], axis=0))
    sh["ssT"] = np.ascontiguousarray(np.concatenate([ss.T, ss.T], axis=0))
    lcw = np.asarray(inp["lru_conv_w"][0], f32)
    sh["lcw"] = np.ascontiguousarray(lcw.reshape(4, 8, 128).transpose(2, 1, 0).reshape(128, 32))
    sh["lcb"] = _fo_part(inp["lru_conv_b"][0], 8)
    sh["ga_w"] = np.ascontiguousarray(inp["lru_gate_a_w"][0], f32)
    sh["gx_w"] = np.ascontiguousarray(inp["lru_gate_x_w"][0], f32)
    sh["gab"] = _fo_part(np.asarray(inp["lru_gate_a_b"][0]).reshape(-1), 16)
    sh["gxb"] = _fo_part(np.asarray(inp["lru_gate_x_b"][0]).reshape(-1), 16)
    sh["lam"] = _fo_part(np.asarray(inp["lru_lambda"][0]).reshape(-1), 16)
    sh["w_out0"] = np.ascontiguousarray(inp["mix_w_out"][0], f32)
    sh["ffn_w_up"] = np.ascontiguousarray(inp["ffn_w_up"], f32)
    fcw = np.asarray(inp["ffn_conv_w"], f32)
    sh["fcw"] = np.ascontiguousarray(fcw.reshape(2, 3, NHC, 128).transpose(0, 3, 2, 1).reshape(2, 128, NHC * 3))
    fcb = np.asarray(inp["ffn_conv_b"], f32)
    sh["fcb"] = np.ascontiguousarray(fcb.reshape(2, NHC, 128).transpose(0, 2, 1))
    sh["ffn_w_down"] = np.ascontiguousarray(inp["ffn_w_down"], f32)
    sh["na_w_qkv"] = np.ascontiguousarray(inp["na_w_qkv"][0], f32)
    sh["na_w_out"] = np.ascontiguousarray(inp["na_w_out"][0], f32)
    combos, _ = na_patterns()
    rb = np.asarray(inp["na_rel_bias"][0], f32)
    colq = np.arange(64)
    cs = np.clip(colq - 8, 0, 48)
    col_ok = (colq[None, :] >= cs[:, None]) & (colq[None, :] < cs[:, None] + 16)
    dc_idx = np.clip(colq[None, :] - colq[:, None] + 15, 0, 30)
    tab = np.full((16, 128, len(combos) + 1, 128), MASKVAL, f32)
    for (delta, pat), cm in combos.items():
        for krl in range(2):
            for qrl in range(2):
                if not pat[krl * 2 + qrl]:
                    continue
                dr = delta + krl - qrl + 7
                g = rb[:, dr, :][:, dc_idx]
                blk = np.where(col_ok[None], g, f32(MASKVAL))
                tab[:, qrl * 64:(qrl + 1) * 64, cm, krl * 64:(krl + 1) * 64] = blk
    sh["na_bias"] = np.ascontiguousarray(tab.reshape(16, 128, (len(combos) + 1) * 128))
    return sh


def prep_core(inp, b):
    m = {}
    m["x"] = np.ascontiguousarray(inp["x"][b], np.float32)
    m["ctx"] = np.ascontiguousarray(inp["ctx"][b], np.float32)
    cc = np.stack([np.asarray(inp["c"][b], np.float32).reshape(16, 128).T,
                   np.asarray(inp["c_ctx"], np.float32).reshape(16, 128).T], axis=-1)
    m["cc"] = np.ascontiguousarray(cc.reshape(128, 32))
    return m


_NC_CACHE = {}


def kernel(**inputs):
    if "nc" not in _NC_CACHE:
        _NC_CACHE["nc"] = build()[0]
    nc = _NC_CACHE["nc"]
    sh = prep_shared(inputs)
    in_maps = []
    for b in range(8):
        m = dict(sh)
        m.update(prep_core(inputs, b))
        in_maps.append(m)
    res = run_bass_kernel_spmd(nc, in_maps, core_ids=list(range(8)))
    return np.stack([np.asarray(r["out"], np.float32) for r in res.results], axis=0)
```

```python
import numpy as np
from contextlib import ExitStack
import concourse.bass as bass
import concourse.mybir as mybir
from concourse.bass_utils import run_bass_kernel_spmd

F32 = mybir.dt.float32
BF16 = mybir.dt.bfloat16
AF = mybir.ActivationFunctionType
ALU = mybir.AluOpType
AX = mybir.AxisListType

ENGS = ['sp', 'pe', 'act', 'dve', 'pool']


class V:
    __slots__ = ('ap', 'keys')

    def __init__(self, ap, keys):
        self.ap = ap
        self.keys = tuple(keys)


class Tile:
    def __init__(self, tensor, name, keys=None):
        self.t = tensor
        self.name = name
        self.keys = (name,) if keys is None else tuple(keys)

    def __getitem__(self, idx):
        return V(self.t[idx], self.keys)

    def sub(self, *subs):
        return Tile(self.t, self.name, [(self.name, s) for s in subs])


class Instr:
    __slots__ = ('eng', 'fn', 'deps', 'is_dma', 'dsem', 'signal', 'val')


class DSem:
    __slots__ = ('h', 'count')


class Prog:
    def __init__(self, nc, n_dma_sems=40):
        self.nc = nc
        self.stack = ExitStack()
        self.csem = {e: self.stack.enter_context(nc.semaphore("c_" + e)) for e in ['pe', 'act', 'dve', 'pool']}
        self.ccount = {e: 0 for e in self.csem}
        self.dpool = [self.stack.enter_context(nc.semaphore("d%d" % i)) for i in range(n_dma_sems)]
        self.dsems = {}
        self.ins = {e: [] for e in ENGS}
        self.lastw = {}
        self.readers = {}
        self.uid = 0
        self.n_instr = 0
        with nc.Block() as b:
            @b.sync
            def _(e):
                for s in list(self.csem.values()) + self.dpool:
                    e.sem_clear(s)

    def close(self):
        self.stack.close()

    def uname(self, base):
        self.uid += 1
        return "%s_%d" % (base, self.uid)

    def sbuf(self, st, name, shape, dtype):
        n = self.uname(name)
        t = st.enter_context(self.nc.sbuf_tensor(n, list(shape), dtype))
        return Tile(t, n)

    def psum(self, st, name, shape, dtype=F32):
        n = self.uname(name)
        t = st.enter_context(self.nc.psum_tensor(n, list(shape), dtype))
        return Tile(t, n)

    def dsem(self, name):
        d = self.dsems.get(name)
        if d is None:
            d = DSem()
            d.h = self.dpool[len(self.dsems)]
            d.count = 0
            self.dsems[name] = d
        return d

    def add(self, eng, fn, reads=(), writes=(), dsem=None):
        I = Instr()
        I.eng = eng
        I.fn = fn
        I.is_dma = dsem is not None
        I.signal = False
        I.val = 0
        I.dsem = None
        if I.is_dma:
            d = self.dsem(dsem)
            d.count += 16
            I.dsem = d
            I.val = d.count
        deps = set()
        for k in reads:
            w = self.lastw.get(k)
            if w is not None:
                deps.add(w)
        for k in writes:
            w = self.lastw.get(k)
            if w is not None:
                deps.add(w)
            for r in self.readers.get(k, ()):
                deps.add(r)
        I.deps = [d for d in deps if d.is_dma or not (eng == 'pe' and d.eng == 'pe' and not I.is_dma)]
        for k in reads:
            self.readers.setdefault(k, []).append(I)
        for k in writes:
            self.lastw[k] = I
            self.readers[k] = []
        self.ins[eng].append(I)
        self.n_instr += 1
        return I

    def _emit_engine(self, e, eng):
        waited = {}
        dma_final = {}
        for I in self.ins[e]:
            toks = {}
            for d in I.deps:
                s = d.dsem.h if d.is_dma else self.csem[d.eng]
                key = id(s)
                if key not in toks or toks[key][1] < d.val:
                    toks[key] = (s, d.val)
            for key, (s, v) in toks.items():
                if waited.get(key, 0) < v:
                    eng.wait_ge(s, v)
                    waited[key] = v
            r = I.fn(eng)
            if I.is_dma:
                r.then_inc(I.dsem.h, 16)
                dma_final[id(I.dsem.h)] = (I.dsem.h, I.val)
            elif I.signal:
                r.then_inc(self.csem[e], 1)
        for key, (s, v) in dma_final.items():
            if waited.get(key, 0) < v:
                eng.wait_ge(s, v)

    def run_phase(self):
        nc = self.nc
        for e in ENGS:
            for I in self.ins[e]:
                for d in I.deps:
                    if not d.is_dma:
                        d.signal = True
        for e in ENGS:
            if e == 'sp':
                continue
            c = self.ccount[e]
            for I in self.ins[e]:
                if I.is_dma:
                    continue
                if I.signal:
                    c += 1
                    I.val = c
            self.ccount[e] = c
        with nc.Block() as b:
            decs = {'sp': b.sync, 'pe': b.tensor, 'act': b.scalar, 'dve': b.vector, 'pool': b.gpsimd}
            for e in ENGS:
                if not self.ins[e]:
                    continue
                decs[e](lambda eng, e=e: self._emit_engine(e, eng))
        self.ins = {e: [] for e in ENGS}
        self.lastw = {}
        self.readers = {}

    @staticmethod
    def _rk(*vs):
        ks = []
        for v in vs:
            if isinstance(v, V):
                ks.extend(v.keys)
        return ks

    @staticmethod
    def _a(v):
        return v.ap if isinstance(v, V) else v

    def matmul(self, out, lhsT, rhs, start=True, stop=True):
        o, l, r = out.ap, lhsT.ap, rhs.ap
        return self.add('pe', lambda e: e.matmul(o, l, r, start=start, stop=stop),
                        reads=self._rk(lhsT, rhs), writes=self._rk(out))

    def transpose(self, out, in_, ident):
        o, i, d = out.ap, in_.ap, ident.ap
        return self.add('pe', lambda e: e.transpose(o, i, d), reads=self._rk(in_, ident), writes=self._rk(out))

    def act(self, out, in_, func, bias=None, scale=None, accum_out=None, eng='act'):
        kw = {}
        if bias is not None:
            kw['bias'] = self._a(bias)
        if scale is not None:
            kw['scale'] = self._a(scale)
        if accum_out is not None:
            kw['accum_out'] = accum_out.ap
        o, i = out.ap, in_.ap
        return self.add(eng, lambda e: e.activation(o, i, func, **kw),
                        reads=self._rk(in_, bias, scale), writes=self._rk(out, accum_out))

    def tt(self, eng, out, in0, in1, op):
        o, a, b = out.ap, in0.ap, in1.ap
        return self.add(eng, lambda e: e.tensor_tensor(o, a, b, op), reads=self._rk(in0, in1), writes=self._rk(out))

    def ts(self, eng, out, in0, s1, s2, op0, op1=None, accum_out=None):
        o, a = out.ap, in0.ap
        x1, x2 = self._a(s1), self._a(s2)
        kw = {}
        if op1 is not None:
            kw['op1'] = op1
        if accum_out is not None:
            kw['accum_out'] = accum_out.ap
        return self.add(eng, lambda e: e.tensor_scalar(o, a, x1, x2, op0, **kw),
                        reads=self._rk(in0, s1, s2), writes=self._rk(out, accum_out))

    def stt(self, out, in0, scalar, in1, op0, op1, eng='dve'):
        o, a, b = out.ap, in0.ap, in1.ap
        s = self._a(scalar)
        return self.add(eng, lambda e: e.scalar_tensor_tensor(o, a, s, b, op0, op1),
                        reads=self._rk(in0, scalar, in1), writes=self._rk(out))

    def copy(self, eng, out, in_):
        o, i = out.ap, in_.ap
        if eng == 'act':
            return self.add(eng, lambda e: e.copy(o, i), reads=self._rk(in_), writes=self._rk(out))
        return self.add(eng, lambda e: e.tensor_copy(o, i), reads=self._rk(in_), writes=self._rk(out))

    def memset(self, eng, out, c):
        o = out.ap
        return self.add(eng, lambda e: e.memset(o, c), writes=self._rk(out))

    def recip(self, out, in_, eng='dve'):
        o, i = out.ap, in_.ap
        return self.add(eng, lambda e: e.reciprocal(o, i), reads=self._rk(in_), writes=self._rk(out))

    def scan(self, out, d0, d1, initial, op0=None, op1=None):
        o, a, b = out.ap, d0.ap, d1.ap
        ini = self._a(initial)
        op0 = op0 or ALU.mult
        op1 = op1 or ALU.add
        return self.add('dve', lambda e: e.tensor_tensor_scan(o, a, b, ini, op0, op1),
                        reads=self._rk(d0, d1, initial), writes=self._rk(out))

    def reduce(self, out, in_, op, axis=None, eng='dve'):
        o, i = out.ap, in_.ap
        axis = axis or AX.X
        return self.add(eng, lambda e: e.tensor_reduce(o, i, axis, op), reads=self._rk(in_), writes=self._rk(out))

    def dma(self, q, out, in_, sem, **kw):
        o, i = out.ap, in_.ap
        return self.add(q, lambda e: e.dma_start(o, i, **kw), reads=self._rk(in_), writes=self._rk(out),
                        dsem=("sw_" if q == 'pool' else "hw_") + sem)


def dram_view(ap, key):
    return V(ap, (key,))


T = 2048
TC = 256
NT = 2304
D = 2048
KC = 16
HID = 5632
NHC = 44
EPS = 1e-6
TB_ALL = [(0, 256), (256, 512), (768, 512), (1280, 512), (1792, 512)]
TB_LAT = TB_ALL[1:]
MLA_SCALE = 192.0 ** -0.5
NA_SCALE = 128.0 ** -0.5
MASKVAL = -30000.0


def bcast_row(ap_row, n):
    return bass.AP(ap_row.tensor, ap_row.offset, [[0, 128], [1, n]])


def na_patterns():
    combos = {}
    pairs = {}
    for r0 in range(0, 32, 2):
        lst = []
        for R in range(0, 32, 2):
            pat = []
            for krl in range(2):
                for qrl in range(2):
                    r = r0 + qrl
                    rs = min(max(r - 4, 0), 24)
                    kr = R + krl
                    pat.append(1 if rs <= kr <= rs + 7 else 0)
            if not any(pat):
                continue
            key = (R - r0, tuple(pat))
            if key not in combos:
                combos[key] = len(combos)
            lst.append((R, combos[key]))
        pairs[r0] = lst
    return combos, pairs


def build(debug=False, stop_after=99):
    nc = bass.Bass("TRN2", target_bir_lowering=False)
    combos, na_pairs = na_patterns()
    NCMB = len(combos) + 1

    def din(name, shape):
        return nc.dram_tensor(name, list(shape), F32, kind="ExternalInput").ap()

    def dscr(name, shape, dt):
        return nc.dram_tensor(name, list(shape), dt, kind="ExternalOutput" if debug else "Internal").ap()

    x_in = din("x", [T, D])
    ctx_in = din("ctx", [TC, D])
    cc_in = din("cc", [128, 32])
    ident_in = din("ident", [128, 128])
    mod_w = din("mod_w", [2, D, 6 * D])
    mod_b = din("mod_b", [2, 6 * D])
    norm_mix = din("norm_mix", [2, D])
    norm_ffn = din("norm_ffn", [2, D])
    final_norm = din("final_norm", [1, D])
    w_in = din("w_in", [D, 3072])
    w_uq = din("w_uq", [512, 2048])
    qn_in = din("qn", [128, 4])
    kvn_in = din("kvn", [128, 2])
    w_ukv_k = din("w_ukv_k", [256, 1024])
    w_ukv_v = din("w_ukv_v", [256, 1024])
    cosT_in = din("cosT", [128, T])
    ssT_in = din("ssT", [128, T])
    lcw_in = din("lcw", [128, 8 * 4])
    lcb_in = din("lcb", [128, 8])
    ga_w = din("ga_w", [2, 8, 128, 128])
    gx_w = din("gx_w", [2, 8, 128, 128])
    gab_in = din("gab", [128, 16])
    gxb_in = din("gxb", [128, 16])
    lam_in = din("lam", [128, 16])
    w_out0 = din("w_out0", [D, D])
    ffn_w_up = din("ffn_w_up", [2, D, 2 * HID])
    fcw_in = din("fcw", [2, 128, NHC * 3])
    fcb_in = din("fcb", [2, 128, NHC])
    ffn_w_down = din("ffn_w_down", [2, HID, D])
    na_w_qkv = din("na_w_qkv", [D, 3 * D])
    na_bias = din("na_bias", [16, 128, NCMB * 128])
    na_w_out = din("na_w_out", [D, D])
    out = nc.dram_tensor("out", [T, D], F32, kind="ExternalOutput").ap()

    modv = dscr("modv", [2, 2, 6 * D], F32)
    xres = dscr("xres", [NT, D], F32)
    cT = dscr("cT", [8, 128, NT], F32)
    rxT = dscr("rxT", [8, 128, NT], F32)
    ryT = dscr("ryT", [8, 128, NT], F32)
    qT = dscr("qT", [16, 128, NT], BF16)
    kT = dscr("kT", [17, 128, NT], BF16)
    vtok = dscr("vtok", [NT, D], BF16)
    mixT = dscr("mixT", [16, 128, NT], BF16)
    mT = dscr("mT", [18, 128, NHC, 128], BF16)

    p = Prog(nc, n_dma_sems=90)
    gst = ExitStack()
    ident = p.sbuf(gst, "ident", [128, 128], BF16)
    ones16 = p.sbuf(gst, "ones16", [128, 128], BF16)
    ones32 = p.sbuf(gst, "ones32", [128, 128], F32)
    epsc = p.sbuf(gst, "epsc", [128, 1], F32)
    p.dma('pool', ident[:, :], V(ident_in, ('c',)), 's_c0')
    p.memset('dve', ones16[:, :], 1.0)
    p.memset('dve', ones32[:, :], 1.0)
    p.memset('dve', epsc[:, :], EPS)

    def dv(ap, *key):
        return V(ap, (key,))

    def xsrc(tt, stage):
        if stage == 0:
            if tt < 2:
                return dv(ctx_in[tt * 128:(tt + 1) * 128, :], 'xin')
            return dv(x_in[(tt - 2) * 128:(tt - 1) * 128, :], 'xin')
        return dv(xres[tt * 128:(tt + 1) * 128, :], 'xres', tt)

    def modrow(layer, stream, j):
        return modv[layer, stream:stream + 1, j * D:(j + 1) * D]

    sT = p.sbuf(gst, "sT", [128, 32], BF16)

    def gen_mod(st, layer, wcols, u0=0, u1=None):
        mw = [p.sbuf(st, "mw", [128, 16, wcols], BF16) for i in range(3)]
        mb = [p.sbuf(st, "mb_", [2, wcols], F32) for i in range(2)]
        mo = [p.sbuf(st, "mo", [2, wcols], F32) for i in range(2)]
        ps = [p.psum(st, "mps", [128, 512]) for i in range(2)]
        wsrc = mod_w[layer].rearrange("(kc p) n -> p kc n", p=128)
        nu = 6 * D // wcols if u1 is None else u1

        def wl(u):
            p.dma('pool', mw[u % 3][:, :, :], dv(wsrc[:, :, u * wcols:(u + 1) * wcols], 'cin'), 's_h%d' % (u % 3))

        wl(u0)
        wl(u0 + 1)
        for u in range(u0, nu):
            if u + 2 < nu:
                wl(u + 2)
            p.dma('sp', mb[u % 2][:, :], dv(bass.AP(mod_b.tensor, layer * 6 * D + u * wcols, [[0, 2], [1, wcols]]), 'cin'), 's_i%d' % (u % 2))
            pp = ps[u % 2]
            for kc in range(16):
                p.matmul(pp[0:2, 0:wcols], sT[:, 2 * kc:2 * kc + 2], mw[u % 3][:, kc, :], start=(kc == 0), stop=(kc == 15))
            p.tt('dve', mo[u % 2][:, :], pp[0:2, 0:wcols], mb[u % 2][:, :], ALU.add)
            p.dma('sp', dv(modv[layer][:, u * wcols:(u + 1) * wcols], 'modv', layer, u), mo[u % 2][:, :], 's_j%d' % (u % 2))
            yield

    def phase_mod():
        with ExitStack() as st:
            cct = p.sbuf(st, "cc", [128, 32], F32)
            p.dma('sp', cct[:, :], dv(cc_in, 'cin'), 's_a0')
            p.act(sT[:, :], cct[:, :], AF.Silu)
            for _ in gen_mod(st, 0, 512, 0, 8):
                pass
            p.run_phase()

    def phase_norm(hT, tiles, stage, layer, which, run=True, pre=None):
        norm_w = norm_mix if which == 0 else norm_ffn
        jsh, jsc = (0, 1) if which == 0 else (3, 4)
        with ExitStack() as st:
            if pre is not None:
                pre()
            NB = 4
            xt = [p.sbuf(st, "xt", [128, D], F32) for i in range(NB)]
            junk = p.sbuf(st, "junk", [128, D], BF16)
            t1 = [p.sbuf(st, "t1", [128, D], F32) for i in range(2)]
            hb = [p.sbuf(st, "hb", [128, D], BF16) for i in range(3)]
            stat = p.sbuf(st, "stat", [128, 4 * NB], F32)
            pst = [p.psum(st, "pst", [128, 8, 128], BF16) for i in range(4)]
            streams = sorted(set(1 if tt < 2 else 0 for tt in tiles))
            gsb = {}
            shb = {}
            gb = xt[0]
            p.dma('sp', gb[:, :], dv(bcast_row(norm_w[layer:layer + 1, :], D), 'cin'), 's_x0')
            for s in streams:
                gsb[s] = p.sbuf(st, "gsb", [128, D], F32)
                shb[s] = p.sbuf(st, "shb", [128, D], F32)
                p.dma('sp', gsb[s][:, :], dv(bcast_row(modrow(layer, s, jsc), D), 'modv', layer), 's_a%d' % s)
                p.dma('sp', shb[s][:, :], dv(bcast_row(modrow(layer, s, jsh), D), 'modv', layer), 's_b%d' % s)
                p.stt(gsb[s][:, :], gsb[s][:, :], 1.0, gb[:, :], ALU.add, ALU.mult)
            N_ = len(tiles)
            H2 = D // 2

            def stage0(n):
                tt = tiles[n]
                x = xt[n % NB]
                p.dma('sp', x[:, :], xsrc(tt, stage), 's_x%d' % (n % NB))
                ss = stat.sub(n % NB)
                c0 = (n % NB) * 4
                p.act(junk[:, :], x[:, :], AF.Square, accum_out=ss[:, c0:c0 + 1])
                p.ts('dve', ss[:, c0 + 1:c0 + 2], ss[:, c0:c0 + 1], 1.0 / D, EPS, ALU.mult, ALU.add)
                p.act(ss[:, c0 + 2:c0 + 3], ss[:, c0 + 1:c0 + 2], AF.Sqrt)
                p.recip(ss[:, c0 + 3:c0 + 4], ss[:, c0 + 2:c0 + 3])

            def stage1(n):
                tt = tiles[n]
                s = 1 if tt < 2 else 0
                x = xt[n % NB]
                ss = stat.sub(n % NB)
                c0 = (n % NB) * 4
                t = t1[n % 2]
                h = hb[n % 3]
                Q = D // 4
                for q in range(4):
                    cs = slice(q * Q, (q + 1) * Q)
                    p.stt(t.sub(q)[:, cs], x[:, cs], ss[:, c0 + 3:c0 + 4], gsb[s][:, cs], ALU.mult, ALU.mult)
                    p.tt('pool' if q < 3 else 'dve', h.sub(q)[:, cs], t.sub(q)[:, cs], shb[s][:, cs], ALU.add)

            def stage2(n):
                tt = tiles[n]
                h = hb[n % 3]
                for half in range(2):
                    pp = pst[(2 * n + half) % 4]
                    for j in range(8):
                        kc = half * 8 + j
                        p.transpose(pp[:, j, :], h.sub(kc // 4)[:, kc * 128:(kc + 1) * 128], ident[:, :])
                    p.copy('act', hT.sub(tt)[:, half * 8:(half + 1) * 8, tt * 128:(tt + 1) * 128], pp[:, :, :])

            for step in range(N_ + 2):
                if step < N_:
                    stage0(step)
                if 0 <= step - 1 < N_:
                    stage1(step - 1)
                if 0 <= step - 2 < N_:
                    stage2(step - 2)
            if run:
                p.run_phase()

    def hview(hT, kc, t0, n):
        keys = [(hT.name, tt) for tt in range(t0 // 128, (t0 + n + 127) // 128)]
        return V(hT.t[:, kc, t0:t0 + n], keys)

    def linear_fm(st, src, kcn, W, ncols, tokblocks, epilogue, skip=None, gw=256, nbuf=3, tag="w", ps=None, wb=None, preloaded=0):
        if wb is None:
            wb = [p.sbuf(st, "wl", [128, kcn, gw], BF16) for i in range(nbuf)]
        if ps is None:
            ps = [p.psum(st, "lps", [128, 512]) for i in range(4)]
        wsrc = W.rearrange("(kc p) n -> p kc n", p=128)
        cnt = 0
        ng = ncols // gw

        def wload(gi):
            p.dma('pool', wb[gi % nbuf][:, :, :], dv(wsrc[:, :, gi * gw:(gi + 1) * gw], 'cin'), 's_%s%d' % (tag, gi % nbuf))

        for gi in range(preloaded, min(nbuf - 1, ng)):
            wload(gi)
        for gi in range(ng):
            w = wb[gi % nbuf]
            if gi + nbuf - 1 < ng:
                wload(gi + nbuf - 1)
            for cj in range(gw // 128):
                chunk = gi * (gw // 128) + cj
                for bi, (t0, n) in enumerate(tokblocks):
                    if skip is not None and skip(chunk, bi):
                        continue
                    pp = ps[cnt % 4]
                    cnt += 1
                    for kc in range(kcn):
                        p.matmul(pp[:, 0:n], w[:, kc, cj * 128:(cj + 1) * 128], src(kc, t0, n),
                                 start=(kc == 0), stop=(kc == kcn - 1))
                    epilogue(chunk, bi, t0, n, pp)

    def phase_l0_proj():
        st_h = ExitStack()
        hT = p.sbuf(st_h, "hT", [128, KC, NT], BF16)

        st_w = ExitStack()
        wbp = [p.sbuf(st_w, "wl", [128, KC, 256], BF16) for i in range(3)]
        wsrc_p = w_in.rearrange("(kc p) n -> p kc n", p=128)

        def _pre():
            for gi in range(2):
                p.dma('pool', wbp[gi][:, :, :], dv(wsrc_p[:, :, gi * 256:(gi + 1) * 256], 'cin'), 's_w%d' % gi)
        phase_norm(hT, list(range(18)), 0, 0, 0, pre=_pre)
        with ExitStack() as st:
            stg = [p.sbuf(st, "stg", [128, 512], F32) for i in range(4)]
            cnt = [0]

            def epi(chunk, bi, t0, n, pp):
                k = cnt[0] % 4
                cnt[0] += 1
                s = stg[k]
                p.copy('act' if k % 2 == 0 else 'dve', s[:, 0:n], pp[:, 0:n])
                if chunk < 8:
                    dst = cT[chunk]
                else:
                    dst = (rxT if chunk < 16 else ryT)[(chunk - 8) % 8]
                p.dma('sp', dv(dst[:, t0:t0 + n], 'rT', chunk, bi), s[:, 0:n], 's_g%d' % k)

            linear_fm(st, lambda kc, t0, n: hview(hT, kc, t0, n), KC, w_in, 3072, TB_ALL, epi, gw=256, wb=wbp, preloaded=2)
            p.run_phase()
        st_w.close()
        st_h.close()
        if stop_after <= 1:
            return
        with ExitStack() as st:
            cqT = p.sbuf(st, "cqT", [128, 4, NT], F32)
            ckvT = p.sbuf(st, "ckvT", [128, 2, NT], F32)
            krT = p.sbuf(st, "krT", [128, NT], F32)
            krsT = p.sbuf(st, "krsT", [128, NT], F32)
            allb = list(range(5))
            for c in range(4):
                p.dma('sp', cqT.sub(*allb)[:, c, :], dv(cT[c], 'cT_in'), 's_q%d' % (c % 2))
            for c in range(2):
                p.dma('sp', ckvT.sub(*allb)[:, c, :], dv(cT[4 + c], 'cT_in'), 's_r%d' % c)
            p.dma('sp', krT.sub(*allb)[:, :], dv(cT[6], 'cT_in'), 's_k0')
            p.dma('sp', krsT.sub(*allb)[:, :], dv(cT[7], 'cT_in'), 's_k1')
            cqn = p.sbuf(st, "cqn", [128, 4, NT], BF16)
            cln = p.sbuf(st, "cln", [128, 2, NT], BF16)
            qn = p.sbuf(st, "qn", [128, 4], F32)
            kvn = p.sbuf(st, "kvn", [128, 2], F32)
            cosT = p.sbuf(st, "cosT", [128, T], F32)
            ssT = p.sbuf(st, "ssT", [128, T], F32)
            p.dma('sp', qn[:, :], dv(qn_in, 'cin'), 's_a0')
            p.dma('sp', kvn[:, :], dv(kvn_in, 'cin'), 's_a1')
            p.dma('sp', cosT[:, :], dv(cosT_in, 'cin'), 's_a2')
            p.dma('sp', ssT[:, :], dv(ssT_in, 'cin'), 's_b0')
            sq = [p.sbuf(st, "sq", [128, 512], F32) for i in range(2)]
            rsb = [p.sbuf(st, "rsb", [128, 512], F32) for i in range(2)]
            psn = [p.psum(st, "psn", [128, 512]) for i in range(2)]
            k = 0
            for (srcT, nch, dstT, gvec) in ((cqT, 4, cqn, qn), (ckvT, 2, cln, kvn)):
                for bi, (t0, n) in enumerate(TB_ALL):
                    pp = psn[k % 2]
                    r = rsb[k % 2]
                    for c in range(nch):
                        s = sq[(k * 4 + c) % 2]
                        p.act(s[:, 0:n], srcT.sub(bi)[:, c, t0:t0 + n], AF.Square)
                        p.matmul(pp[:, 0:n], ones32[:, :], s[:, 0:n], start=(c == 0), stop=(c == nch - 1))
                    p.ts('dve', r[:, 0:n], pp[:, 0:n], 1.0 / (128 * nch), EPS, ALU.mult, ALU.add)
                    p.act(r[:, 0:n], r[:, 0:n], AF.Sqrt)
                    p.recip(r[:, 0:n], r[:, 0:n])
                    for c in range(nch):
                        p.stt(dstT.sub(bi)[:, c, t0:t0 + n], srcT.sub(bi)[:, c, t0:t0 + n], gvec[:, c:c + 1], r[:, 0:n],
                              ALU.mult, ALU.mult)
                    k += 1
            KR = p.sbuf(st, "KR", [128, NT], BF16)
            ta = p.sbuf(st, "ta", [128, T], F32)
            tb = p.sbuf(st, "tb", [128, T], F32)
            allb = list(range(5))
            p.copy('act', KR[:, 0:TC], krT.sub(*allb)[:, 0:TC])
            p.tt('dve', ta[:, :], krT.sub(*allb)[:, TC:NT], cosT[:, :], ALU.mult)
            p.tt('pool', tb[:, :], krsT.sub(*allb)[:, TC:NT], ssT[:, :], ALU.mult)
            p.tt('dve', KR[:, TC:NT], ta[:, :], tb[:, :], ALU.add)
            p.dma('sp', dv(kT[8], 'kT', 8), KR[:, :], 's_b1')
            qstg = [p.sbuf(st, "qstg", [128, 512], BF16) for i in range(4)]
            fstg = [p.sbuf(st, "fstg", [128, 512], F32) for i in range(2)]
            tmpA = p.sbuf(st, "tmpA", [128, T], F32)
            qc = [0]

            def epi_q(chunk, bi, t0, n, pp):
                k = qc[0] % 4
                qc[0] += 1
                s = qstg[k]
                if chunk < 8:
                    p.copy('act', s[:, 0:n], pp[:, 0:n])
                    p.dma('sp', dv(qT[chunk][:, t0:t0 + n], 'qT', chunk, bi), s[:, 0:n], 's_g%d' % k)
                    return
                j = (chunk - 8) // 2
                if (chunk - 8) % 2 == 0:
                    if bi == 0:
                        p.copy('act', s[:, 0:n], pp[:, 0:n])
                        p.dma('sp', dv(qT[8 + j][:, t0:t0 + n], 'qT', 8 + j, bi), s[:, 0:n], 's_g%d' % k)
                    else:
                        p.tt('dve', tmpA.sub(bi)[:, t0 - TC:t0 - TC + n], pp[:, 0:n], cosT[:, t0 - TC:t0 - TC + n], ALU.mult)
                else:
                    f = fstg[k % 2]
                    p.tt('dve', f[:, 0:n], pp[:, 0:n], ssT[:, t0 - TC:t0 - TC + n], ALU.mult)
                    p.tt('pool', s[:, 0:n], f[:, 0:n], tmpA.sub(bi)[:, t0 - TC:t0 - TC + n], ALU.add)
                    p.dma('sp', dv(qT[8 + j][:, t0:t0 + n], 'qT', 8 + j, bi), s[:, 0:n], 's_g%d' % k)

            def skip_q(chunk, bi):
                return chunk >= 8 and (chunk - 8) % 2 == 1 and bi == 0

            def src_q(kc, t0, n):
                return V(cqn.t[:, kc, t0:t0 + n], [(cqn.name, bi) for bi, (a, b) in enumerate(TB_ALL) if a == t0])

            lps = [p.psum(st, "lps", [128, 512]) for i in range(4)]
            linear_fm(st, src_q, 4, w_uq, 2048, TB_ALL, epi_q, skip=skip_q, gw=512, tag="u", ps=lps)

            def epi_k(chunk, bi, t0, n, pp):
                k = qc[0] % 4
                qc[0] += 1
                s = qstg[k]
                p.copy('act' if k % 2 == 0 else 'dve', s[:, 0:n], pp[:, 0:n])
                p.dma('sp', dv(kT[chunk][:, t0:t0 + n], 'kT', chunk, bi), s[:, 0:n], 's_g%d' % k)

            def src_k(kc, t0, n):
                return V(cln.t[:, kc, t0:t0 + n], [(cln.name, bi) for bi, (a, b) in enumerate(TB_ALL) if a == t0])

            linear_fm(st, src_k, 2, w_ukv_k, 1024, TB_ALL, epi_k, gw=512, tag="v", ps=lps)
            wv = p.sbuf(st, "wv", [128, 2, 1024], BF16)
            p.dma('pool', wv[:, :, :], dv(w_ukv_v.rearrange("(kc p) n -> p kc n", p=128), 'cin'), 's_c1')
            vst = [p.sbuf(st, "vst", [128, 1024], BF16) for i in range(2)]
            psv = [p.psum(st, "psv", [128, 512]) for i in range(2)]
            for tt in range(18):
                bi = 0 if tt < 2 else 1 + (tt - 2) // 4
                vs = vst[tt % 2]
                for cb in range(2):
                    pp = psv[cb]
                    for kc in range(2):
                        p.matmul(pp[:, :], cln.sub(bi)[:, kc, tt * 128:(tt + 1) * 128], wv[:, kc, cb * 512:(cb + 1) * 512],
                                 start=(kc == 0), stop=(kc == 1))
                    p.copy('act' if cb == 0 else 'dve', vs[:, cb * 512:(cb + 1) * 512], pp[:, :])
                p.dma('sp', dv(vtok[tt * 128:(tt + 1) * 128, 0:1024], 'vtok', tt), vs[:, :], 's_x%d' % (tt % 2))
            p.run_phase()

    def gen_mla(st):
        if True:
            KR = p.sbuf(st, "KR", [128, NT], BF16)
            p.dma('sp', KR[:, :], dv(kT[8], 'kT8'), 's_a0')
            qh = [p.sbuf(st, "qh", [128, NT], BF16) for i in range(2)]
            qr = [p.sbuf(st, "qr", [128, NT], BF16) for i in range(2)]
            kh = [p.sbuf(st, "kh", [128, NT], BF16) for i in range(2)]
            vh = [p.sbuf(st, "vh", [128, 18, 128], BF16) for i in range(2)]
            oh = [p.sbuf(st, "oh", [128, NT], BF16) for i in range(2)]
            pT = [p.sbuf(st, "pT", [128, 512], BF16) for i in range(4)]
            rc = [p.sbuf(st, "rc", [128, 512], F32) for i in range(2)]
            psS = [p.psum(st, "psS", [128, 512]) for i in range(4)]
            psO = [p.psum(st, "psO", [128, 512]) for i in range(2)]
            psZ = [p.psum(st, "psZ", [128, 512]) for i in range(2)]
            def hload(h):
                b = h % 2
                p.dma('sp', qh[b][:, :], dv(qT[h], 'qTh'), 's_q%d' % b)
                p.dma('sp', qr[b][:, :], dv(qT[8 + h // 2], 'qTr'), 's_r%d' % b)
                p.dma('sp', kh[b][:, :], dv(kT[h], 'kTh'), 's_k%d' % b)
                p.dma('sp', vh[b][:, :, :], dv(vtok[:, h * 128:(h + 1) * 128].rearrange("(t p) c -> p t c", p=128), 'vh'),
                      's_v%d' % b)

            steps = []
            for h in range(8):
                for qi, (q0, qn_) in enumerate(TB_ALL):
                    kts = [0, 1] if qi == 0 else list(range(18))
                    for ki, kt in enumerate(kts):
                        steps.append((h, qi, q0, qn_, ki, kt, len(kts)))

            def qk(si):
                h, qi, q0, qn_, ki, kt, nk = steps[si]
                b = h % 2
                pb = (h % 2) * 64
                pS = psS[si % 4]
                p.matmul(pS[:, 0:qn_], kh[b][:, kt * 128:(kt + 1) * 128], qh[b][:, q0:q0 + qn_], start=True, stop=False)
                p.matmul(pS[:, 0:qn_], KR[pb:pb + 64, kt * 128:(kt + 1) * 128], qr[b][pb:pb + 64, q0:q0 + qn_],
                         start=False, stop=True)

            hload(0)
            nq = 0
            qk(0)
            qk(1)
            for si in range(len(steps)):
                h, qi, q0, qn_, ki, kt, nk = steps[si]
                b = h % 2
                if qi == 0 and ki == 0 and h + 1 < 8:
                    hload(h + 1)
                if si + 2 < len(steps):
                    qk(si + 2)
                pS = psS[si % 4]
                pt = pT[si % 4]
                po = psO[nq % 2]
                pz = psZ[nq % 2]
                p.act(pt[:, 0:qn_], pS[:, 0:qn_], AF.Exp, scale=MLA_SCALE)
                p.matmul(po[:, 0:qn_], vh[b][:, kt, :], pt[:, 0:qn_], start=(ki == 0), stop=(ki == nk - 1))
                p.matmul(pz[:, 0:qn_], ones16[:, :], pt[:, 0:qn_], start=(ki == 0), stop=(ki == nk - 1))
                if ki == nk - 1:
                    r = rc[nq % 2]
                    nq += 1
                    p.recip(r[:, 0:qn_], pz[:, 0:qn_])
                    p.tt('dve', oh[b].sub(qi)[:, q0:q0 + qn_], po[:, 0:qn_], r[:, 0:qn_], ALU.mult)
                    if qi == len(TB_ALL) - 1:
                        p.dma('sp', dv(mixT[h], 'mixT', h), oh[b].sub(0, 1, 2, 3, 4)[:, :], 's_o%d' % b)
                yield

    def gen_lru(st):
        W = NT + 6
        CT0 = 2
        LT0 = 261
        if True:
            gw = p.sbuf(st, "gw", [128, 32, 128], F32)
            for g, wsrc in enumerate((ga_w, gx_w)):
                for d in range(2):
                    p.dma('sp', gw[:, (g * 2 + d) * 8:(g * 2 + d) * 8 + 8, :], dv(wsrc[d].rearrange("n k e -> k n e"), 'cin'),
                          's_e%d' % (g * 2 + d))
            lcw = p.sbuf(st, "lcw", [128, 32], F32)
            lcb = p.sbuf(st, "lcb", [128, 8], F32)
            gab = p.sbuf(st, "gab", [128, 16], F32)
            gxb = p.sbuf(st, "gxb", [128, 16], F32)
            lam = p.sbuf(st, "lam", [128, 16], F32)
            p.dma('sp', lcw[:, :], dv(lcw_in, 'cin'), 's_b0')
            p.dma('sp', lcb[:, :], dv(lcb_in, 'cin'), 's_b1')
            p.dma('sp', gab[:, :], dv(gab_in, 'cin'), 's_b2')
            p.dma('sp', gxb[:, :], dv(gxb_in, 'cin'), 's_b3')
            p.dma('sp', lam[:, :], dv(lam_in, 'cin'), 's_b4')
            z = p.sbuf(st, "z", [128, 16], F32)
            y = p.sbuf(st, "y", [128, 16], F32)
            y2 = p.sbuf(st, "y2", [128, 16], F32)
            acc = p.sbuf(st, "acc", [128, 16], F32)
            c1 = p.sbuf(st, "c1", [128, 16], F32)
            c2 = p.sbuf(st, "c2", [128, 16], F32)
            p.ts('dve', z[:, :], lam[:, :], -1.0, None, ALU.mult)
            p.tt('dve', z[:, :], z[:, :], lam[:, :], ALU.max)
            p.act(z[:, :], z[:, :], AF.Exp, scale=-1.0)
            p.ts('dve', y[:, :], z[:, :], 2.0, None, ALU.add)
            p.recip(y[:, :], y[:, :])
            p.tt('dve', y[:, :], y[:, :], z[:, :], ALU.mult)
            p.tt('dve', y2[:, :], y[:, :], y[:, :], ALU.mult)
            p.ts('dve', acc[:, :], y2[:, :], 1.0 / 13, 1.0 / 11, ALU.mult, ALU.add)
            for cf in (1.0 / 9, 1.0 / 7, 1.0 / 5, 1.0 / 3, 1.0):
                p.tt('dve', acc[:, :], acc[:, :], y2[:, :], ALU.mult)
                p.ts('dve', acc[:, :], acc[:, :], cf, None, ALU.add)
            p.tt('dve', acc[:, :], acc[:, :], y[:, :], ALU.mult)
            p.ts('dve', z[:, :], lam[:, :], -1.0, 0.0, ALU.mult, ALU.max)
            p.stt(acc[:, :], acc[:, :], 2.0, z[:, :], ALU.mult, ALU.add)
            p.ts('dve', c1[:, :], acc[:, :], -8.0, None, ALU.mult)
            p.ts('dve', c2[:, :], acc[:, :], -16.0, None, ALU.mult)

            xp = [p.sbuf(st, "xp", [128, W], F32) for i in range(2)]
            ry = [p.sbuf(st, "ry", [128, NT], F32) for i in range(2)]
            xc = [p.sbuf(st, "xc", [128, W], F32) for i in range(2)]
            rr = [p.sbuf(st, "rr", [128, W], F32) for i in range(2)]
            ii = [p.sbuf(st, "ii", [128, W], F32) for i in range(2)]
            bb = [p.sbuf(st, "bb", [128, W], F32) for i in range(2)]
            hf = [p.sbuf(st, "hf", [128, W], F32) for i in range(2)]
            hbk = [p.sbuf(st, "hbk", [128, W], F32) for i in range(2)]
            lo = [p.sbuf(st, "lo", [128, NT], BF16) for i in range(2)]
            NPG = 6
            psg = [p.psum(st, "psg", [128, 512]) for i in range(NPG)]
            for i in range(2):
                p.memset('pool', xp[i][:, :], 0.0)
            blks = []
            c = CT0
            while c < W - 1:
                n = min(512, W - 1 - c)
                blks.append((c, n))
                c += n
            ng = [0]
            L = W - 3
            S = slice(2, W - 1)

            def S0(n):
                x = xp[n % 2]
                p.dma('sp', x[:, CT0:CT0 + TC], dv(rxT[n][:, 0:TC], 'rxl'), 's_x%d' % (n % 2))
                p.dma('sp', x[:, LT0:LT0 + T], dv(rxT[n][:, TC:NT], 'rxl'), 's_y%d' % (n % 2))
                p.dma('sp', ry[n % 2][:, :], dv(ryT[n], 'ryl'), 's_z%d' % (n % 2))
                xcn = xc[n % 2]
                p.act(xcn[:, 2:2 + L], x[:, 0:L], AF.Identity, bias=lcb[:, n:n + 1], scale=lcw[:, n * 4:n * 4 + 1])
                for tap in range(1, 4):
                    p.stt(xcn[:, 2:2 + L], x[:, tap:tap + L], lcw[:, n * 4 + tap:n * 4 + tap + 1], xcn[:, 2:2 + L], ALU.mult, ALU.add)

            def S1(n, d):
                u = 2 * n + d
                xcn = xc[n % 2]
                r_, i_, b_ = rr[u % 2], ii[u % 2], bb[u % 2]
                for (b0, bn) in blks:
                    pr = psg[ng[0] % NPG]
                    pi = psg[(ng[0] + 1) % NPG]
                    ng[0] += 2
                    p.matmul(pr[:, 0:bn], gw[:, (0 * 2 + d) * 8 + n, :], xcn[:, b0:b0 + bn])
                    p.matmul(pi[:, 0:bn], gw[:, (1 * 2 + d) * 8 + n, :], xcn[:, b0:b0 + bn])
                    p.act(r_[:, b0:b0 + bn], pr[:, 0:bn], AF.Sigmoid, bias=gab[:, d * 8 + n:d * 8 + n + 1])
                    p.act(i_[:, b0:b0 + bn], pi[:, 0:bn], AF.Sigmoid, bias=gxb[:, d * 8 + n:d * 8 + n + 1])
                p.act(b_[:, S], r_[:, S], AF.Exp, scale=c2[:, d * 8 + n:d * 8 + n + 1])
                p.act(r_[:, S], r_[:, S], AF.Exp, scale=c1[:, d * 8 + n:d * 8 + n + 1])
                p.act(b_[:, S], b_[:, S], AF.Sqrt, bias=1.0, scale=-1.0)
                p.tt('pool', i_[:, S], i_[:, S], xcn[:, S], ALU.mult)
                p.tt('dve', b_[:, S], b_[:, S], i_[:, S], ALU.mult)
                if d == 0:
                    h_ = hf[n % 2]
                    p.scan(h_[:, CT0:CT0 + TC], r_[:, CT0:CT0 + TC], b_[:, CT0:CT0 + TC], 0.0)
                    p.scan(h_[:, LT0:LT0 + T], r_[:, LT0:LT0 + T], b_[:, LT0:LT0 + T], h_[:, CT0 + TC - 1:CT0 + TC])
                else:
                    h_ = hbk[n % 2]
                    p.scan(h_[:, CT0 + TC - 1:CT0 - 1:-1], r_[:, CT0 + TC - 1:CT0 - 1:-1], b_[:, CT0 + TC - 1:CT0 - 1:-1], 0.0)
                    p.scan(h_[:, LT0 + T - 1:LT0 - 1:-1], r_[:, LT0 + T - 1:LT0 - 1:-1], b_[:, LT0 + T - 1:LT0 - 1:-1],
                           h_[:, CT0:CT0 + 1])

            def S2(n):
                h_, hb_ = hf[n % 2], hbk[n % 2]
                yv = ry[n % 2]
                p.tt('pool', h_[:, CT0:CT0 + TC], h_[:, CT0:CT0 + TC], hb_[:, CT0:CT0 + TC], ALU.add)
                p.tt('pool', h_[:, LT0:LT0 + T], h_[:, LT0:LT0 + T], hb_[:, LT0:LT0 + T], ALU.add)
                p.act(yv[:, :], yv[:, :], AF.Gelu_apprx_tanh)
                o = lo[n % 2]
                p.tt('dve', o[:, 0:TC], h_[:, CT0:CT0 + TC], yv[:, 0:TC], ALU.mult)
                p.tt('dve', o[:, TC:NT], h_[:, LT0:LT0 + T], yv[:, TC:NT], ALU.mult)
                p.dma('sp', dv(mixT[8 + n], 'mixT', 8 + n), o[:, :], 's_p%d' % (n % 2))

            gmod = gen_mod(st, 0, 256, 16, 48)

            def modstep(k):
                for _ in range(k):
                    try:
                        next(gmod)
                    except StopIteration:
                        pass

            S0(0)
            for n in range(8):
                S1(n, 0)
                if n + 1 < 8:
                    S0(n + 1)
                S1(n, 1)
                modstep(4)
                S2(n)
                yield
            modstep(64)

    def phase_mla():
        with ExitStack() as st:
            for _ in gen_mla(st):
                pass
            p.run_phase()

    def phase_lru():
        with ExitStack() as st:
            for _ in gen_lru(st):
                pass
            p.run_phase()

    def phase_mla_lru():
        with ExitStack() as st:
            ga = gen_mla(st)
            gb_ = gen_lru(st)
            alive_a = alive_b = True
            while alive_a or alive_b:
                for _ in range(5):
                    if alive_a:
                        try:
                            next(ga)
                        except StopIteration:
                            alive_a = False
                for _ in range(2):
                    if alive_b:
                        try:
                            next(gb_)
                        except StopIteration:
                            alive_b = False
            p.run_phase()

    def phase_outproj(Wout, layer, tiles, stage):
        with ExitStack() as st:
            wo = p.sbuf(st, "wo", [128, KC, D], BF16)
            wsrc = Wout.rearrange("(kc p) n -> p kc n", p=128)
            for q in range(4):
                p.dma('pool', wo.sub(q)[:, :, q * 512:(q + 1) * 512], dv(wsrc[:, :, q * 512:(q + 1) * 512], 'cin'), 's_w%d' % q)
            streams = sorted(set(1 if tt < 2 else 0 for tt in tiles))
            gb = {}
            for s in streams:
                gb[s] = p.sbuf(st, "gb", [128, D], F32)
                p.dma('sp', gb[s][:, :], dv(bcast_row(modrow(layer, s, 2), D), 'modv', layer), 's_a%d' % s)
            mt = [p.sbuf(st, "mt", [128, KC, 128], BF16) for i in range(3)]
            xt = [p.sbuf(st, "xt", [128, D], F32) for i in range(2)]
            tm = [p.sbuf(st, "tm", [128, 512], F32) for i in range(2)]
            ps = [p.psum(st, "ops", [128, 512]) for i in range(4)]
            k = 0
            for n, tt in enumerate(tiles):
                s = 1 if tt < 2 else 0
                m = mt[n % 3]
                p.dma('sp', m[:, :, :], dv(mixT[:, :, tt * 128:(tt + 1) * 128].rearrange("c p t -> p c t"), 'mixT_in'),
                      's_m%d' % (n % 3))
                x = xt[n % 2]
                p.dma('sp', x[:, :], xsrc(tt, stage), 's_x%d' % (n % 2))
                for cb in range(4):
                    pp = ps[k % 4]
                    t = tm[k % 2]
                    k += 1
                    for kc in range(KC):
                        p.matmul(pp[:, :], m[:, kc, :], wo.sub(cb)[:, kc, cb * 512:(cb + 1) * 512], start=(kc == 0), stop=(kc == KC - 1))
                    p.tt('dve', t[:, :], pp[:, :], gb[s][:, cb * 512:(cb + 1) * 512], ALU.mult)
                    p.tt('pool', x[:, cb * 512:(cb + 1) * 512], t[:, :], x[:, cb * 512:(cb + 1) * 512], ALU.add)
                p.dma('act', dv(xres[tt * 128:(tt + 1) * 128, :], 'xres', tt), x[:, :], 's_y%d' % (n % 2))
            p.run_phase()

    def phase_ffn(layer, tiles):
        lat_only = tiles[0] >= 2
        tokblocks = TB_LAT if lat_only else TB_ALL
        st_h = ExitStack()
        hT = p.sbuf(st_h, "hT", [128, KC, NT], BF16)
        st_w = ExitStack()
        wg = [p.sbuf(st_w, "wg", [128, KC, 256], BF16) for i in range(2)]
        wu = [p.sbuf(st_w, "wu", [128, KC, 256], BF16) for i in range(2)]
        wsrc = ffn_w_up[layer].rearrange("(kc p) n -> p kc n", p=128)

        def wload(g_):
            p.dma('pool', wg[g_ % 2][:, :, :], dv(wsrc[:, :, g_ * 256:(g_ + 1) * 256], 'cin'), 's_w%d' % (g_ % 2))
            p.dma('pool', wu[g_ % 2][:, :, :], dv(wsrc[:, :, HID + g_ * 256:HID + (g_ + 1) * 256], 'cin'), 's_u%d' % (g_ % 2))

        phase_norm(hT, tiles, 1, layer, 1, pre=lambda: wload(0))
        W = NT + 4
        CT0 = 1
        LT0 = 259

        def col(t0):
            return CT0 + t0 if t0 < TC else LT0 + (t0 - TC)

        with ExitStack() as st:
            fcw = p.sbuf(st, "fcw", [128, NHC * 3], F32)
            fcb = p.sbuf(st, "fcb", [128, NHC], F32)
            p.dma('sp', fcw[:, :], dv(fcw_in[layer], 'cin'), 's_c0')
            p.dma('sp', fcb[:, :], dv(fcb_in[layer], 'cin'), 's_c1')
            gbuf = [p.sbuf(st, "gbuf", [128, W], F32) for i in range(2)]
            ubuf = [p.sbuf(st, "ubuf", [128, W], F32) for i in range(2)]
            accb = [p.sbuf(st, "accb", [128, W], F32) for i in range(2)]
            mb = [p.sbuf(st, "mb", [128, W], BF16) for i in range(2)]
            ps = [p.psum(st, "fps", [128, 512]) for i in range(6)]
            NPS = 6
            for i in range(2):
                p.memset('pool', gbuf[i].sub(*range(len(tokblocks)))[:, :], 0.0)
                p.memset('pool', ubuf[i].sub(*range(len(tokblocks)))[:, :], 0.0)
            gmod = gen_mod(st, 1, 256) if layer == 0 else None
            k = 0
            c_lo = LT0 if lat_only else CT0
            Lc = (W - 1) - c_lo
            for c in range(NHC):
                gi = c // 2

                if c % 2 == 0 and gi + 1 < NHC // 2:
                    wload(gi + 1)
                cj = c % 2
                g = gbuf[c % 2]
                u = ubuf[c % 2]
                a = accb[c % 2]
                m = mb[c % 2]
                for bi, (t0, n) in enumerate(tokblocks):
                    pg = ps[k % 6]
                    pu = ps[(k + 1) % 6]
                    k += 2
                    for kc in range(KC):
                        p.matmul(pg[:, 0:n], wg[gi % 2][:, kc, cj * 128:(cj + 1) * 128], hview(hT, kc, t0, n), start=(kc == 0), stop=(kc == KC - 1))
                    p.copy('act', g.sub(bi)[:, col(t0):col(t0) + n], pg[:, 0:n])
                    for kc in range(KC):
                        p.matmul(pu[:, 0:n], wu[gi % 2][:, kc, cj * 128:(cj + 1) * 128], hview(hT, kc, t0, n), start=(kc == 0), stop=(kc == KC - 1))
                    p.copy('dve', u.sub(bi)[:, col(t0):col(t0) + n], pu[:, 0:n])
                allb = list(range(len(tokblocks)))
                gA = g.sub(*allb)
                uA = u.sub(*allb)
                p.act(a[:, c_lo:c_lo + Lc], gA[:, c_lo - 1:c_lo - 1 + Lc], AF.Identity, bias=fcb[:, c:c + 1], scale=fcw[:, 3 * c:3 * c + 1])
                p.stt(a[:, c_lo:c_lo + Lc], gA[:, c_lo:c_lo + Lc], fcw[:, 3 * c + 1:3 * c + 2], a[:, c_lo:c_lo + Lc], ALU.mult, ALU.add)
                p.stt(a[:, c_lo:c_lo + Lc], gA[:, c_lo + 1:c_lo + 1 + Lc], fcw[:, 3 * c + 2:3 * c + 3], a[:, c_lo:c_lo + Lc], ALU.mult, ALU.add)
                p.act(a[:, c_lo:c_lo + Lc], a[:, c_lo:c_lo + Lc], AF.Gelu_apprx_tanh)
                p.tt('dve', m[:, c_lo:c_lo + Lc], a[:, c_lo:c_lo + Lc], uA[:, c_lo:c_lo + Lc], ALU.mult)
                if not lat_only:
                    p.dma('sp', dv(mT[0:2, :, c, :].rearrange("a p t -> p a t"), 'mT', c, 0),
                          V(m.t[:, CT0:CT0 + TC].rearrange("p (a t) -> p a t", t=128), m.keys), 's_m%d' % (c % 2))
                p.dma('sp', dv(mT[2:18, :, c, :].rearrange("a p t -> p a t"), 'mT', c, 1),
                      V(m.t[:, LT0:LT0 + T].rearrange("p (a t) -> p a t", t=128), m.keys), 's_n%d' % (c % 2))
                if gmod is not None:
                    for _ in range(2 if c < 4 else 1):
                        try:
                            next(gmod)
                        except StopIteration:
                            gmod = None
                            break
            p.run_phase()
        st_w.close()
        st_h.close()
        with ExitStack() as st:
            wd = [p.sbuf(st, "wd", [128, NHC, 512], BF16) for i in range(2)]
            streams = sorted(set(1 if tt < 2 else 0 for tt in tiles))
            gb = {}
            for s in streams:
                gb[s] = p.sbuf(st, "gb", [128, D], F32)
                p.dma('sp', gb[s][:, :], dv(bcast_row(modrow(layer, s, 5), D), 'modv', layer), 's_a%d' % s)
            mt = [p.sbuf(st, "mt", [128, NHC, 128], BF16) for i in range(3)]
            xt = [p.sbuf(st, "xt", [128, 512], F32) for i in range(3)]
            tm = [p.sbuf(st, "tm", [128, 512], F32) for i in range(2)]
            ps = [p.psum(st, "dps", [128, 512]) for i in range(4)]
            wsrc = ffn_w_down[layer].rearrange("(kc p) n -> p kc n", p=128)
            k = 0
            def wdload(cb_):
                for hf_ in range(4):
                    p.dma('pool', wd[cb_ % 2].sub(hf_)[:, hf_ * 11:(hf_ + 1) * 11, :],
                          dv(wsrc[:, hf_ * 11:(hf_ + 1) * 11, cb_ * 512:(cb_ + 1) * 512], 'cin'), 's_w%d' % ((cb_ % 2) * 4 + hf_))

            wdload(0)
            for cb in range(4):
                w = wd[cb % 2]
                if cb + 1 < 4:
                    wdload(cb + 1)
                for tt in tiles:
                    s = 1 if tt < 2 else 0
                    m = mt[k % 3]
                    x = xt[k % 3]
                    t = tm[k % 2]
                    pp = ps[k % 4]
                    p.dma('sp', m[:, :, :], dv(mT[tt], 'mT_in'), 's_m%d' % (k % 3))
                    p.dma('sp', x[:, :], dv(xres[tt * 128:(tt + 1) * 128, cb * 512:(cb + 1) * 512], 'xres', tt, cb), 's_x%d' % (k % 3))
                    for c in range(NHC):
                        p.matmul(pp[:, :], m[:, c, :], w.sub(c // 11)[:, c, :], start=(c == 0), stop=(c == NHC - 1))
                    p.tt('dve', t[:, :], pp[:, :], gb[s][:, cb * 512:(cb + 1) * 512], ALU.mult)
                    p.tt('dve', x[:, :], t[:, :], x[:, :], ALU.add)
                    p.dma('act', dv(xres[tt * 128:(tt + 1) * 128, cb * 512:(cb + 1) * 512], 'xres', tt, cb), x[:, :], 's_y%d' % (k % 3))
                    k += 1
            p.run_phase()

    def phase_l1_proj():
        st_h = ExitStack()
        hT = p.sbuf(st_h, "hT", [128, KC, NT], BF16)

        st_w = ExitStack()
        wbp = [p.sbuf(st_w, "wl", [128, KC, 256], BF16) for i in range(3)]
        wsrc_p = na_w_qkv[:, 0:2 * D].rearrange("(kc p) n -> p kc n", p=128)

        def _pre():
            for gi in range(2):
                p.dma('pool', wbp[gi][:, :, :], dv(wsrc_p[:, :, gi * 256:(gi + 1) * 256], 'cin'), 's_w%d' % gi)
        phase_norm(hT, list(range(18)), 1, 1, 0, pre=_pre)
        with ExitStack() as st:
            qstg = [p.sbuf(st, "qstg", [128, 512], BF16) for i in range(4)]
            qc = [0]

            def epi(chunk, bi, t0, n, pp):
                k = qc[0] % 4
                qc[0] += 1
                s = qstg[k]
                p.copy('act' if k % 2 == 0 else 'dve', s[:, 0:n], pp[:, 0:n])
                if chunk < 16:
                    p.dma('sp', dv(qT[chunk][:, t0:t0 + n], 'qT', chunk, bi), s[:, 0:n], 's_g%d' % k)
                else:
                    p.dma('sp', dv(kT[chunk - 16][:, t0:t0 + n], 'kT', chunk, bi), s[:, 0:n], 's_g%d' % k)

            def skip(chunk, bi):
                return chunk < 16 and bi == 0

            linear_fm(st, lambda kc, t0, n: hview(hT, kc, t0, n), KC, na_w_qkv[:, 0:2 * D], 2 * D, TB_ALL, epi, skip=skip, gw=256, wb=wbp, preloaded=2)
            wv = p.sbuf(st, "wv", [128, KC, D], BF16)
            wsrc = na_w_qkv[:, 2 * D:3 * D].rearrange("(kc p) n -> p kc n", p=128)
            for q in range(4):
                p.dma('pool', wv.sub(q)[:, :, q * 512:(q + 1) * 512], dv(wsrc[:, :, q * 512:(q + 1) * 512], 'cin'), 's_v%d' % q)
            vst = [p.sbuf(st, "vst", [128, D], BF16) for i in range(2)]
            psv = [p.psum(st, "psv", [128, 512]) for i in range(4)]
            for tt in range(18):
                vs = vst[tt % 2]
                for cb in range(4):
                    pp = psv[cb]
                    for kc in range(KC):
                        p.matmul(pp[:, :], hT.sub(tt)[:, kc, tt * 128:(tt + 1) * 128], wv.sub(cb)[:, kc, cb * 512:(cb + 1) * 512],
                                 start=(kc == 0), stop=(kc == KC - 1))
                    p.copy('act' if cb % 2 == 0 else 'dve', vs[:, cb * 512:(cb + 1) * 512], pp[:, :])
                p.dma('sp', dv(vtok[tt * 128:(tt + 1) * 128, :], 'vtok', tt), vs[:, :], 's_x%d' % (tt % 2))
            p.run_phase()
        st_w.close()
        st_h.close()

    def phase_na_attn():
        CM_MASK = NCMB - 1
        with ExitStack() as st:
            qh = [p.sbuf(st, "qh", [128, NT], BF16) for i in range(2)]
            kh = [p.sbuf(st, "kh", [128, NT], BF16) for i in range(2)]
            vh = [p.sbuf(st, "vh", [128, 18, 128], BF16) for i in range(2)]
            bf = [p.sbuf(st, "bf", [128, NCMB * 128], F32) for i in range(2)]
            bq = [p.sbuf(st, "bq", [128, NCMB * 128], BF16) for i in range(2)]
            oh = [p.sbuf(st, "oh", [128, NT], BF16) for i in range(2)]
            pT = [p.sbuf(st, "pT", [128, 4, 256], BF16) for i in range(4)]
            rc = [p.sbuf(st, "rc", [128, 256], F32) for i in range(2)]
            psS = [p.psum(st, "psS", [128, 2, 256]) for i in range(4)]
            psO = [p.psum(st, "psO", [128, 512]) for i in range(2)]
            psZ = [p.psum(st, "psZ", [128, 512]) for i in range(2)]

            def hload(h):
                b = h % 2
                p.dma('sp', qh[b][:, TC:NT], dv(qT[h][:, TC:NT], 'qTh'), 's_q%d' % b)
                p.dma('sp', kh[b][:, :], dv(kT[h], 'kTh'), 's_k%d' % b)
                p.dma('sp', vh[b][:, :, :], dv(vtok[:, h * 128:(h + 1) * 128].rearrange("(t p) c -> p t c", p=128), 'vh'),
                      's_v%d' % b)
                p.dma('sp', bf[b][:, :], dv(na_bias[h], 'cin'), 's_b%d' % b)

            def bqprep(h):
                b = h % 2
                p.act(bq[b][:, :], bf[b][:, :], AF.Copy, scale=1.0 / NA_SCALE)

            steps = []
            for h in range(16):
                for qp in range(8):
                    dA = dict(na_pairs[4 * qp])
                    dB = dict(na_pairs[4 * qp + 2])
                    items = [(0, None, None), (1, None, None)]
                    for R in sorted(set(dA) | set(dB)):
                        items.append((2 + R // 2, dA.get(R, CM_MASK), dB.get(R, CM_MASK)))
                    assert len(items) <= 8
                    for half in range(2):
                        steps.append((h, qp, half, items[half * 4:half * 4 + 4], len(items)))
            NS = len(steps)

            def qk(si):
                h, qp, half, its, ntot = steps[si]
                b = h % 2
                q0 = TC + qp * 256
                for j, (kt, cmA, cmB) in enumerate(its):
                    pS = psS[(si % 2) * 2 + j // 2]
                    jj = j % 2
                    p.matmul(pS[:, jj, :], kh[b][:, kt * 128:(kt + 1) * 128], qh[b][:, q0:q0 + 256], start=True, stop=(cmA is None))
                    if cmA is not None:
                        p.matmul(pS[:, jj, 0:128], bq[b][:, cmA * 128:(cmA + 1) * 128], ident[:, :], start=False, stop=False)
                        p.matmul(pS[:, jj, 128:256], bq[b][:, cmB * 128:(cmB + 1) * 128], ident[:, :], start=False, stop=True)

            def front(si):
                h, qp, half, its, ntot = steps[si]
                b = h % 2
                pt = pT[si % 4]
                po = psO[(si // 2) % 2]
                n_ = len(its)
                for bk in range((n_ + 1) // 2):
                    m_ = min(2, n_ - bk * 2)
                    pS = psS[(si % 2) * 2 + bk]
                    p.act(pt.sub(bk)[:, bk * 2:bk * 2 + m_, :], pS[:, 0:m_, :], AF.Exp, scale=NA_SCALE)
                for j, (kt, cmA, cmB) in enumerate(its):
                    gi = half * 4 + j
                    p.matmul(po[:, 0:256], vh[b][:, kt, :], pt.sub(j // 2)[:, j, :], start=(gi == 0), stop=(gi == ntot - 1))

            def back(si):
                h, qp, half, its, ntot = steps[si]
                b = h % 2
                q0 = TC + qp * 256
                pt = pT[si % 4]
                po = psO[(si // 2) % 2]
                pz = psZ[(si // 2) % 2]
                for j in range(len(its)):
                    gi = half * 4 + j
                    p.matmul(pz[:, 0:256], ones16[:, :], pt.sub(j // 2)[:, j, :], start=(gi == 0), stop=(gi == ntot - 1))
                if half == 1:
                    r = rc[(si // 2) % 2]
                    p.recip(r[:, :], pz[:, 0:256])
                    p.tt('dve', oh[b].sub(qp)[:, q0:q0 + 256], po[:, 0:256], r[:, :], ALU.mult)
                    if qp == 7:
                        p.dma('sp', dv(mixT[h][:, TC:NT], 'mixT', h), oh[b].sub(*range(8))[:, TC:NT], 's_o%d' % b)

            hload(0)
            bqprep(0)
            qk(0)
            for si in range(NS + 1):
                if si < NS:
                    h, qp, half, its, ntot = steps[si]
                    if qp == 0 and half == 0 and h + 1 < 16:
                        hload(h + 1)
                    if qp == 4 and half == 0 and h + 1 < 16:
                        bqprep(h + 1)
                    if si + 1 < NS:
                        qk(si + 1)
                if si >= 1:
                    back(si - 1)
                if si < NS:
                    front(si)
            p.run_phase()

    def phase_final():
        with ExitStack() as st:
            gb = p.sbuf(st, "gb", [128, D], F32)
            p.dma('sp', gb[:, :], dv(bcast_row(final_norm[0:1, :], D), 'cin'), 's_a0')
            NB = 4
            xt = [p.sbuf(st, "xt", [128, D], F32) for i in range(NB)]
            ot = [p.sbuf(st, "ot", [128, D], F32) for i in range(3)]
            junk = p.sbuf(st, "junk", [128, D], BF16)
            stat = p.sbuf(st, "stat", [128, 4 * NB], F32)
            H2 = D // 2

            def stage0(n):
                x = xt[n % NB]
                p.dma('sp', x[:, :], xsrc(n + 2, 1), 's_x%d' % (n % NB))
                ss = stat.sub(n % NB)
                c0 = (n % NB) * 4
                p.act(junk[:, :], x[:, :], AF.Square, accum_out=ss[:, c0:c0 + 1])
                p.ts('dve', ss[:, c0 + 1:c0 + 2], ss[:, c0:c0 + 1], 1.0 / D, EPS, ALU.mult, ALU.add)
                p.act(ss[:, c0 + 2:c0 + 3], ss[:, c0 + 1:c0 + 2], AF.Sqrt)
                p.recip(ss[:, c0 + 3:c0 + 4], ss[:, c0 + 2:c0 + 3])

            def stage1(n):
                x = xt[n % NB]
                o = ot[n % 3]
                ss = stat.sub(n % NB)
                c0 = (n % NB) * 4
                p.stt(o[:, :], x[:, :], ss[:, c0 + 3:c0 + 4], gb[:, :], ALU.mult, ALU.mult)
                p.dma('sp', dv(out[n * 128:(n + 1) * 128, :], 'out', n), o[:, :], 's_y%d' % (n % 3))

            for step in range(17):
                if step < 16:
                    stage0(step)
                if 0 <= step - 1 < 16:
                    stage1(step - 1)
            p.run_phase()

    phases = [
        phase_mod,
        phase_l0_proj,
        None,
        phase_mla,
        phase_lru,
        lambda: phase_outproj(w_out0, 0, list(range(18)), 0),
        lambda: phase_ffn(0, list(range(18))),
        phase_l1_proj,
        phase_na_attn,
        lambda: phase_outproj(na_w_out, 1, list(range(2, 18)), 1),
        lambda: phase_ffn(1, list(range(2, 18))),
        phase_final,
    ]
    for i, ph in enumerate(phases):
        if i > stop_after:
            break
        if ph is not None:
            ph()
    gst.close()
    p.close()
    return nc, p


def _fo_part(v, nchunks):
    return np.ascontiguousarray(np.asarray(v, np.float32).reshape(nchunks, 128).T)


def prep_shared(inp):
    f32 = np.float32
    sh = {}
    sh["ident"] = np.eye(128, dtype=f32)
    sh["mod_w"] = np.ascontiguousarray(inp["mod_w"], f32)
    sh["mod_b"] = np.ascontiguousarray(inp["mod_b"], f32)
    sh["norm_mix"] = np.ascontiguousarray(inp["norm_mix"], f32)
    sh["norm_ffn"] = np.ascontiguousarray(inp["norm_ffn"], f32)
    sh["final_norm"] = np.ascontiguousarray(inp["final_norm"], f32).reshape(1, D)
    f = np.arange(64)
    a_, j_, p_ = f // 32, (f // 16) % 2, f % 16
    partner = a_ * 32 + (1 - j_) * 16 + p_
    w_in = np.asarray(inp["mla_w_in"][0], f32)
    kr = w_in[:, 768:832]
    krs = kr[:, partner]
    sh["w_in"] = np.ascontiguousarray(np.concatenate(
        [w_in[:, 0:768], kr, kr, krs, krs, w_in[:, 832:1856], w_in[:, 1856:2880]], axis=1))
    w_uq = np.asarray(inp["mla_w_uq"][0], f32)
    cols = []
    for h in range(8):
        cols.append(w_uq[:, h * 192:h * 192 + 128])
    for j in range(4):
        r0 = w_uq[:, (2 * j) * 192 + 128:(2 * j) * 192 + 192]
        r1 = w_uq[:, (2 * j + 1) * 192 + 128:(2 * j + 1) * 192 + 192]
        cols += [r0, r1, r0[:, partner], r1[:, partner]]
    sh["w_uq"] = np.ascontiguousarray(np.concatenate(cols, axis=1))
    sh["qn"] = _fo_part(inp["mla_q_norm"][0], 4)
    sh["kvn"] = _fo_part(inp["mla_kv_norm"][0], 2)
    w_ukv = np.asarray(inp["mla_w_ukv"][0], f32).reshape(256, 8, 256)
    sh["w_ukv_k"] = np.ascontiguousarray(w_ukv[:, :, 0:128].reshape(256, 1024))
    sh["w_ukv_v"] = np.ascontiguousarray(w_ukv[:, :, 128:256].reshape(256, 1024))
    pos = np.arange(T)
    inv = (10000.0 ** (-np.arange(16, dtype=np.float32) / 16)).astype(f32)
    ar = (pos // 64).astype(f32)[:, None] * inv
    ac = (pos % 64).astype(f32)[:, None] * inv
    ang = np.concatenate([ar, ar, ac, ac], axis=-1)
    cos = np.cos(ang).astype(f32)
    sin = np.sin(ang).astype(f32)
    sgn = np.where(j_ == 0, -1.0, 1.0).astype(f32)
    ss = sin * sgn[None, :]
    sh["cosT"] = np.ascontiguousarray(np.concatenate([cos.T, cos.T], axis=0))
    sh["ssT"] = np.ascontiguousarray(np.concatenate([ss.T, ss.T], axis=0))
    lcw = np.asarray(inp["lru_conv_w"][0], f32)
    sh["lcw"] = np.ascontiguousarray(lcw.reshape(4, 8, 128).transpose(2, 1, 0).reshape(128, 32))
    sh["lcb"] = _fo_part(inp["lru_conv_b"][0], 8)
    sh["ga_w"] = np.ascontiguousarray(inp["lru_gate_a_w"][0], f32)
    sh["gx_w"] = np.ascontiguousarray(inp["lru_gate_x_w"][0], f32)
    sh["gab"] = _fo_part(np.asarray(inp["lru_gate_a_b"][0]).reshape(-1), 16)
    sh["gxb"] = _fo_part(np.asarray(inp["lru_gate_x_b"][0]).reshape(-1), 16)
    sh["lam"] = _fo_part(np.asarray(inp["lru_lambda"][0]).reshape(-1), 16)
    sh["w_out0"] = np.ascontiguousarray(inp["mix_w_out"][0], f32)
    sh["ffn_w_up"] = np.ascontiguousarray(inp["ffn_w_up"], f32)
    fcw = np.asarray(inp["ffn_conv_w"], f32)
    sh["fcw"] = np.ascontiguousarray(fcw.reshape(2, 3, NHC, 128).transpose(0, 3, 2, 1).reshape(2, 128, NHC * 3))
    fcb = np.asarray(inp["ffn_conv_b"], f32)
    sh["fcb"] = np.ascontiguousarray(fcb.reshape(2, NHC, 128).transpose(0, 2, 1))
    sh["ffn_w_down"] = np.ascontiguousarray(inp["ffn_w_down"], f32)
    sh["na_w_qkv"] = np.ascontiguousarray(inp["na_w_qkv"][0], f32)
    sh["na_w_out"] = np.ascontiguousarray(inp["na_w_out"][0], f32)
    combos, _ = na_patterns()
    rb = np.asarray(inp["na_rel_bias"][0], f32)
    colq = np.arange(64)
    cs = np.clip(colq - 8, 0, 48)
    col_ok = (colq[None, :] >= cs[:, None]) & (colq[None, :] < cs[:, None] + 16)
    dc_idx = np.clip(colq[None, :] - colq[:, None] + 15, 0, 30)
    tab = np.full((16, 128, len(combos) + 1, 128), MASKVAL, f32)
    for (delta, pat), cm in combos.items():
        for krl in range(2):
            for qrl in range(2):
                if not pat[krl * 2 + qrl]:
                    continue
                dr = delta + krl - qrl + 7
                g = rb[:, dr, :][:, dc_idx]
                blk = np.where(col_ok[None], g, f32(MASKVAL))
                tab[:, qrl * 64:(qrl + 1) * 64, cm, krl * 64:(krl + 1) * 64] = blk
    sh["na_bias"] = np.ascontiguousarray(tab.reshape(16, 128, (len(combos) + 1) * 128))
    return sh


def prep_core(inp, b):
    m = {}
    m["x"] = np.ascontiguousarray(inp["x"][b], np.float32)
    m["ctx"] = np.ascontiguousarray(inp["ctx"][b], np.float32)
    cc = np.stack([np.asarray(inp["c"][b], np.float32).reshape(16, 128).T,
                   np.asarray(inp["c_ctx"], np.float32).reshape(16, 128).T], axis=-1)
    m["cc"] = np.ascontiguousarray(cc.reshape(128, 32))
    return m


_NC_CACHE = {}


def kernel(**inputs):
    if "nc" not in _NC_CACHE:
        _NC_CACHE["nc"] = build()[0]
    nc = _NC_CACHE["nc"]
    sh = prep_shared(inputs)
    in_maps = []
    for b in range(8):
        m = dict(sh)
        m.update(prep_core(inputs, b))
        in_maps.append(m)
    res = run_bass_kernel_spmd(nc, in_maps, core_ids=list(range(8)))
    return np.stack([np.asarray(r["out"], np.float32) for r in res.results], axis=0)
```
